# Optimizing a Trainium2 kernel written in Bass

```python
import math
import jax
import jax.numpy as jnp
from jax import lax
import numpy as np

D_MODEL = 1024
BATCH = 16
SEQ = 4096
DEPTH = 4

MEM_LEN = 256
MIX_WIDTH = D_MODEL // 2
N_BRANCH = 4
GDN_HEADS = 4
GDN_HEAD_DIM = MIX_WIDTH // GDN_HEADS
GDN_CHUNK = 64
GDN_CONV = 4
S5_GROUP = 16
S5_GROUPS = MIX_WIDTH // S5_GROUP
S5_STATE = 64
LRU_BLOCKS = 8
LRU_BLOCK = MIX_WIDTH // LRU_BLOCKS
LRU_CONV = 4
LRU_C = 8.0
SC_CONV = 3
XA_HEADS = 4
XA_HEAD_DIM = D_MODEL // XA_HEADS
D_FF = 128 * ((8 * D_MODEL // 3 + 127) // 128)
FFN_CONV = 3
EPS = 1e-6
IN_SIZES = (3 * MIX_WIDTH, MIX_WIDTH, 2 * GDN_HEADS, 2 * GDN_HEADS, MIX_WIDTH, MIX_WIDTH, MIX_WIDTH,
            MIX_WIDTH, MIX_WIDTH, MIX_WIDTH, N_BRANCH * D_MODEL)
IN_WIDTH = sum(IN_SIZES)

kernel_name = "hybrid_parallel_gated_bidir_encoder"


def _centred(width):
    return (width // 2, width - 1 - width // 2)


def rms_norm(x, gain):
    xf = x.astype(jnp.float32)
    y = xf * lax.rsqrt(jnp.mean(xf * xf, axis=-1, keepdims=True) + EPS)
    return (y * gain.astype(jnp.float32)).astype(x.dtype)


def dwconv(x, w):
    return lax.conv_general_dilated(
        x, w[:, None, :].astype(x.dtype), window_strides=(1,), padding=[_centred(w.shape[0])],
        dimension_numbers=("NWC", "WIO", "NWC"), feature_group_count=x.shape[-1])


def _l2norm(t):
    t = t.astype(jnp.float32)
    return t * lax.rsqrt(jnp.sum(t * t, axis=-1, keepdims=True) + EPS)


def _linear_scan(a, b, reverse):
    def combine(e1, e2):
        a1, b1 = e1
        a2, b2 = e2
        return a2 * a1, a2 * b1 + b2
    return lax.associative_scan(combine, (a, b), reverse=reverse, axis=1)[1]


def _complex_linear_scan(a_re, a_im, b_re, b_im, reverse):
    def combine(e1, e2):
        a1r, a1i, b1r, b1i = e1
        a2r, a2i, b2r, b2i = e2
        return (a2r * a1r - a2i * a1i, a2r * a1i + a2i * a1r,
                a2r * b1r - a2i * b1i + b2r, a2r * b1i + a2i * b1r + b2i)
    _, _, h_re, h_im = lax.associative_scan(combine, (a_re, a_im, b_re, b_im), reverse=reverse, axis=1)
    return h_re, h_im


def _gdn_chunked(q, k, v, log_a, beta):
    f32 = jnp.float32
    Bn, S, H, Dk = q.shape
    Dv = v.shape[-1]
    C = GDN_CHUNK
    N = S // C

    def blocks(t):
        t = t.astype(f32).reshape((Bn, N, C, H) + t.shape[3:])
        return jnp.moveaxis(t, 3, 1)

    q = blocks(q) * (Dk ** -0.5)
    k, v, g, b = blocks(k), blocks(v), blocks(log_a), blocks(beta)
    gam = jnp.cumsum(g, axis=-1)
    incl = jnp.tril(jnp.ones((C, C), dtype=bool))
    strict = jnp.tril(jnp.ones((C, C), dtype=bool), -1)
    diff = gam[..., :, None] - gam[..., None, :]
    decay = jnp.where(incl, jnp.exp(jnp.where(incl, diff, 0.0)), 0.0)
    kb = k * b[..., None]
    a_mat = jnp.eye(C, dtype=f32) + jnp.where(strict, jnp.einsum("bhncd,bhnmd->bhncm", kb, k) * decay, 0.0)
    rhs = jnp.concatenate([v * b[..., None], kb * jnp.exp(gam)[..., None]], axis=-1)
    wy = lax.linalg.triangular_solve(a_mat, rhs, left_side=True, lower=True, unit_diagonal=True)
    u, w = wy[..., :Dv], wy[..., Dv:]
    qk = jnp.einsum("bhncd,bhnmd->bhncm", q, k) * decay
    q_dec = q * jnp.exp(gam)[..., None]
    k_dec = k * jnp.exp(gam[..., -1:] - gam)[..., None]
    g_last = jnp.exp(gam[..., -1])

    def step(state, xs):
        q_c, k_c, u_c, w_c, qk_c, gl = xs
        v_new = u_c - jnp.einsum("bhcd,bhde->bhce", w_c, state)
        o = jnp.einsum("bhcd,bhde->bhce", q_c, state) + jnp.einsum("bhcm,bhme->bhce", qk_c, v_new)
        state = state * gl[..., None, None] + jnp.einsum("bhcd,bhce->bhde", k_c, v_new)
        return state, o

    xs = tuple(jnp.moveaxis(t, 2, 0) for t in (q_dec, k_dec, u, w, qk, g_last))
    _, o = lax.scan(step, jnp.zeros((Bn, H, Dk, Dv), f32), xs)
    o = jnp.moveaxis(o, 0, 2)
    return jnp.moveaxis(o, 1, 3).reshape(Bn, S, H, Dv)


def gdn_mixer(qkv, z, beta_logit, alpha, conv_w, a_log, dt_bias, out_gain):
    f32 = jnp.float32
    Bn, S, _ = qkv.shape
    qkv = jax.nn.silu(dwconv(qkv, conv_w))
    q, k, v = jnp.split(qkv, 3, axis=-1)
    heads = lambda t: t.reshape(Bn, S, GDN_HEADS, GDN_HEAD_DIM)
    q, k, v = _l2norm(heads(q)), _l2norm(heads(k)), heads(v).astype(f32)
    beta = jax.nn.sigmoid(beta_logit.astype(f32)).reshape(Bn, S, 2, GDN_HEADS)
    log_a = -jnp.exp(a_log.astype(f32)) * jax.nn.softplus(
        alpha.astype(f32).reshape(Bn, S, 2, GDN_HEADS) + dt_bias.astype(f32))
    flip = lambda t: jnp.flip(t, axis=1)
    o_fwd = _gdn_chunked(q, k, v, log_a[:, :, 0], beta[:, :, 0])
    o_bwd = flip(_gdn_chunked(flip(q), flip(k), flip(v), flip(log_a[:, :, 1]), flip(beta[:, :, 1])))
    o = o_fwd + o_bwd
    o = o * lax.rsqrt(jnp.mean(o * o, axis=-1, keepdims=True) + EPS) * out_gain.astype(f32)
    o = o.reshape(Bn, S, MIX_WIDTH) * jax.nn.silu(z.astype(f32))
    return o.astype(qkv.dtype)


def s5_mixer(u, lam_re, lam_im, log_step, b_re, b_im, c_re, c_im, d, glu_w, glu_b):
    f32 = jnp.float32
    Bn, S, W = u.shape
    uf = u.astype(f32)
    ug = uf.reshape(Bn, S, S5_GROUPS, S5_GROUP)
    lr = jnp.minimum(lam_re.astype(f32), -1e-4)
    li = lam_im.astype(f32)
    step = jnp.exp(log_step.astype(f32))[..., None]
    mag = jnp.exp(lr * step)
    abar_re, abar_im = mag * jnp.cos(li * step), mag * jnp.sin(li * step)
    den = lr * lr + li * li
    f_re = ((abar_re - 1.0) * lr + abar_im * li) / den
    f_im = (abar_im * lr - (abar_re - 1.0) * li) / den
    br, bi = b_re.astype(f32), b_im.astype(f32)
    bbar_re = f_re[..., None] * br - f_im[..., None] * bi
    bbar_im = f_re[..., None] * bi + f_im[..., None] * br
    cr, ci = c_re.astype(f32), c_im.astype(f32)
    y = uf * d.astype(f32)
    for di in range(2):
        bu_re = jnp.einsum("bsgj,gpj->bsgp", ug, bbar_re[di])
        bu_im = jnp.einsum("bsgj,gpj->bsgp", ug, bbar_im[di])
        a_re = jnp.broadcast_to(abar_re[di], (1, S) + abar_re.shape[1:])
        a_im = jnp.broadcast_to(abar_im[di], (1, S) + abar_im.shape[1:])
        h_re, h_im = _complex_linear_scan(a_re, a_im, bu_re, bu_im, reverse=di == 1)
        y_dir = jnp.einsum("bsgp,gjp->bsgj", h_re, cr[di]) - jnp.einsum("bsgp,gjp->bsgj", h_im, ci[di])
        y = y + y_dir.reshape(Bn, S, W)
    zg = jax.nn.gelu(y)
    out = zg * jax.nn.sigmoid(zg @ glu_w.astype(f32) + glu_b.astype(f32))
    return out.astype(u.dtype)


def rglru_mixer(xr, gate_in, conv_w, conv_b, wa, ba, wx, bx, lam):
    f32 = jnp.float32
    Bn, S, W = xr.shape
    xc = (dwconv(xr, conv_w) + conv_b.astype(xr.dtype)).astype(f32)
    xb = xc.reshape(Bn, S, LRU_BLOCKS, LRU_BLOCK)
    outs = []
    for di in range(2):
        r = jax.nn.sigmoid(jnp.einsum("bsnk,nkm->bsnm", xb, wa[di].astype(f32)).reshape(Bn, S, W) + ba[di].astype(f32))
        i = jax.nn.sigmoid(jnp.einsum("bsnk,nkm->bsnm", xb, wx[di].astype(f32)).reshape(Bn, S, W) + bx[di].astype(f32))
        log_a = -LRU_C * r * jax.nn.softplus(-lam[di].astype(f32))
        b = jnp.sqrt(-jnp.expm1(2.0 * log_a)) * (i * xc)
        outs.append(_linear_scan(jnp.exp(log_a), b, reverse=di == 1))
    h = outs[0] + outs[1]
    return (h * jax.nn.gelu(gate_in.astype(f32))).astype(xr.dtype)


def short_conv_mixer(b_gate, c_gate, xin, conv_w):
    return b_gate * dwconv(c_gate * xin, conv_w)


def cross_attention(h, mem_h, w_q, w_kv, w_o):
    Bn, S, D = h.shape
    M = mem_h.shape[1]
    q = (h @ w_q).reshape(Bn, S, XA_HEADS, XA_HEAD_DIM)
    k, v = jnp.split(mem_h @ w_kv, 2, axis=-1)
    k = k.reshape(Bn, M, XA_HEADS, XA_HEAD_DIM)
    v = v.reshape(Bn, M, XA_HEADS, XA_HEAD_DIM)
    s = jnp.einsum("bshd,bmhd->bhsm", q, k).astype(jnp.float32) * (XA_HEAD_DIM ** -0.5)
    p = jax.nn.softmax(s, axis=-1).astype(v.dtype)
    o = jnp.einsum("bhsm,bmhd->bshd", p, v).reshape(Bn, S, D)
    return o @ w_o


def conv_glu_ffn(h, w_up, conv_w, conv_b, w_down):
    u = dwconv(h @ w_up, conv_w) + conv_b.astype(h.dtype)
    gate, up = jnp.split(u, 2, axis=-1)
    return (jax.nn.silu(gate) * up) @ w_down


def setup_inputs(seed: int = 0) -> dict:
    key = jax.random.key(seed)
    keys = jax.random.split(key, 64)
    counter = [0]
    f32 = jnp.float32
    L, W, D, F = DEPTH, MIX_WIDTH, D_MODEL, D_FF
    G, P, J = S5_GROUPS, S5_STATE, S5_GROUP

    def nxt():
        counter[0] += 1
        return keys[counter[0] - 1]

    def nrm(shape, scale):
        return scale * jax.random.normal(nxt(), shape, f32)

    def gain(shape):
        return 1.0 + nrm(shape, 0.02)

    def unif(shape, lo, hi):
        return jax.random.uniform(nxt(), shape, f32, lo, hi)

    dt = jnp.exp(unif((L, 2, GDN_HEADS), math.log(1e-3), math.log(1e-1)))
    gdn_dt_bias = dt + jnp.log(-jnp.expm1(-dt))
    n_idx = jnp.arange(P, dtype=f32)
    a_c = unif((L, 2, W), 0.9, 0.999)
    a_base = a_c ** (1.0 / LRU_C)
    lru_lambda = jnp.log(a_base) - jnp.log1p(-a_base)
    return {
        "x": nrm((BATCH, SEQ, D), 1.0),
        "mem": nrm((BATCH, MEM_LEN, D), 1.0),
        "mix_norm": gain((L, D)),
        "w_in": nrm((L, D, IN_WIDTH), D ** -0.5),
        "gdn_conv": nrm((L, GDN_CONV, 3 * W), GDN_CONV ** -0.5),
        "gdn_a_log": jnp.log(unif((L, 2, GDN_HEADS), 1.0, 16.0)),
        "gdn_dt_bias": gdn_dt_bias,
        "gdn_out_norm": gain((L, GDN_HEAD_DIM)),
        "s5_lambda_re": -0.5 + nrm((L, 2, G, P), 0.01),
        "s5_lambda_im": math.pi * n_idx + nrm((L, 2, G, P), 0.01),
        "s5_log_step": unif((L, 2, G), math.log(1e-3), math.log(1e-1)),
        "s5_b_re": nrm((L, 2, G, P, J), (2 * J) ** -0.5),
        "s5_b_im": nrm((L, 2, G, P, J), (2 * J) ** -0.5),
        "s5_c_re": nrm((L, 2, G, J, P), P ** -0.5),
        "s5_c_im": nrm((L, 2, G, J, P), P ** -0.5),
        "s5_d": nrm((L, W), 1.0),
        "s5_glu_w": nrm((L, W, W), W ** -0.5),
        "s5_glu_b": nrm((L, W), 0.01),
        "lru_conv_w": nrm((L, LRU_CONV, W), LRU_CONV ** -0.5),
        "lru_conv_b": nrm((L, W), 0.01),
        "lru_gate_a_w": nrm((L, 2, LRU_BLOCKS, LRU_BLOCK, LRU_BLOCK), LRU_BLOCK ** -0.5),
        "lru_gate_a_b": nrm((L, 2, W), 0.01),
        "lru_gate_x_w": nrm((L, 2, LRU_BLOCKS, LRU_BLOCK, LRU_BLOCK), LRU_BLOCK ** -0.5),
        "lru_gate_x_b": nrm((L, 2, W), 0.01),
        "lru_lambda": lru_lambda,
        "sc_conv": nrm((L, SC_CONV, W), SC_CONV ** -0.5),
        "w_branch": nrm((L, N_BRANCH, W, D), W ** -0.5),
        "w_mix_out": nrm((L, D, D), D ** -0.5),
        "xa_norm": gain((L, D)),
        "xa_mem_norm": gain((L, D)),
        "xa_w_q": nrm((L, D, D), D ** -0.5),
        "xa_w_kv": nrm((L, D, 2 * D), D ** -0.5),
        "xa_w_o": nrm((L, D, D), D ** -0.5),
        "ffn_norm": gain((L, D)),
        "ffn_w_up": nrm((L, D, 2 * F), D ** -0.5),
        "ffn_conv_w": nrm((L, FFN_CONV, 2 * F), FFN_CONV ** -0.5),
        "ffn_conv_b": nrm((L, 2 * F), 0.01),
        "ffn_w_down": nrm((L, F, D), F ** -0.5),
        "final_norm": gain((D,)),
    }


def reference(x, mem, mix_norm, w_in, gdn_conv, gdn_a_log, gdn_dt_bias, gdn_out_norm,
              s5_lambda_re, s5_lambda_im, s5_log_step, s5_b_re, s5_b_im, s5_c_re, s5_c_im, s5_d,
              s5_glu_w, s5_glu_b, lru_conv_w, lru_conv_b, lru_gate_a_w, lru_gate_a_b, lru_gate_x_w,
              lru_gate_x_b, lru_lambda, sc_conv, w_branch, w_mix_out, xa_norm, xa_mem_norm, xa_w_q,
              xa_w_kv, xa_w_o, ffn_norm, ffn_w_up, ffn_conv_w, ffn_conv_b, ffn_w_down, final_norm):
    Bn, S, D = x.shape
    splits = [int(i) for i in np.cumsum(IN_SIZES)[:-1]]
    for l in range(DEPTH):
        h = rms_norm(x, mix_norm[l])
        (qkv, z, beta_logit, alpha, s5_u, lru_x, lru_g, sc_b, sc_c, sc_x, gate_logits) = jnp.split(
            h @ w_in[l], splits, axis=-1)
        ys = (
            gdn_mixer(qkv, z, beta_logit, alpha, gdn_conv[l], gdn_a_log[l], gdn_dt_bias[l], gdn_out_norm[l]),
            s5_mixer(s5_u, s5_lambda_re[l], s5_lambda_im[l], s5_log_step[l], s5_b_re[l], s5_b_im[l],
                     s5_c_re[l], s5_c_im[l], s5_d[l], s5_glu_w[l], s5_glu_b[l]),
            rglru_mixer(lru_x, lru_g, lru_conv_w[l], lru_conv_b[l], lru_gate_a_w[l], lru_gate_a_b[l],
                        lru_gate_x_w[l], lru_gate_x_b[l], lru_lambda[l]),
            short_conv_mixer(sc_b, sc_c, sc_x, sc_conv[l]),
        )
        gates = jax.nn.sigmoid(gate_logits.reshape(Bn, S, N_BRANCH, D))
        merged = gates[:, :, 0] * (ys[0] @ w_branch[l, 0])
        for m in range(1, N_BRANCH):
            merged = merged + gates[:, :, m] * (ys[m] @ w_branch[l, m])
        x = x + merged @ w_mix_out[l]
        x = x + cross_attention(rms_norm(x, xa_norm[l]), rms_norm(mem, xa_mem_norm[l]),
                                xa_w_q[l], xa_w_kv[l], xa_w_o[l])
        x = x + conv_glu_ffn(rms_norm(x, ffn_norm[l]), ffn_w_up[l], ffn_conv_w[l], ffn_conv_b[l], ffn_w_down[l])
    return rms_norm(x, final_norm)
```

```python
import contextlib
import math
import numpy as np
import concourse.bass as bass
import concourse.mybir as mybir
from concourse.bass_utils import run_bass_kernel_spmd

F32 = mybir.dt.float32
BF16 = mybir.dt.bfloat16
AF = mybir.ActivationFunctionType
ALU = mybir.AluOpType

D = 1024
W = 512
MEM = 256
DFF = 2816
INW = 9232
EPS = 1e-6
NCORES = 8


class Trk:
    __slots__ = ("w", "r")

    def __init__(self):
        self.w = None
        self.r = []


def _trks(v):
    t = v.t
    return t if isinstance(t, (list, tuple)) else (t,)


class V:
    __slots__ = ("ap", "t")

    def __init__(self, ap, t):
        self.ap = ap
        self.t = t

    def __getitem__(self, k):
        return V(self.ap[k], self.t)

    def rr(self, pat, **kw):
        return V(self.ap.rearrange(pat, **kw), self.t)

    def bc(self, shape):
        return V(self.ap.broadcast_to(list(shape)), self.t)


class Prog:
    CE = ("pe", "act", "dve", "pool")
    NSLOT = 8

    def __init__(self, nc):
        self.nc = nc
        self.es = contextlib.ExitStack()
        self.q = {e: [] for e in ("pe", "act", "dve", "pool", "sp")}
        self.cnt = {e: 0 for e in self.CE}
        self.sem = {e: self.es.enter_context(nc.semaphore("s_" + e)) for e in self.CE}
        self.known = {e: {} for e in self.q}
        self.dsem = {}
        self.dcnt = {}
        for qn in ("sp", "pool"):
            self.dsem[qn] = [self.es.enter_context(nc.semaphore("d_%s%d" % (qn, i))) for i in range(self.NSLOT)]
            self.dcnt[qn] = 0
        self.nuid = 0
        self.nbank = 0
        self.banks = []

    def sb(self, shape, dt, name=None):
        self.nuid += 1
        h = self.es.enter_context(self.nc.sbuf_tensor(name or ("sb%d" % self.nuid), list(shape), dt))
        return V(h[:], Trk())

    def dram(self, name, shape, dt, kind="Internal"):
        h = self.nc.dram_tensor(name, list(shape), dt, kind=kind)
        return V(h.ap(), Trk())

    def init_banks(self):
        for i in range(8):
            h = self.es.enter_context(self.nc.psum_tensor("bank%d" % i, [128, 512], F32))
            self.banks.append(V(h[:], Trk()))

    def bank(self):
        b = self.banks[self.nbank % 8]
        self.nbank += 1
        return b

    def bankbf(self):
        b = self.bank()
        return V(b.ap.bitcast(BF16), b.t)

    def _deps(self, reads, writes):
        deps = []
        for v in reads:
            for t in _trks(v):
                if t.w is not None:
                    deps.append(t.w)
        for v in writes:
            for t in _trks(v):
                if t.w is not None:
                    deps.append(t.w)
                deps.extend(t.r)
        return deps

    def _waits(self, eng, deps, raw_same, always=False):
        kn = self.known[eng]
        need = {}
        for tok in deps:
            key, val, src = tok
            if key == eng and not always:
                if eng == "pe":
                    continue
                if tok not in raw_same or val < self.cnt[eng] - 1:
                    continue
            if kn.get(key, 0) >= val:
                continue
            if need.get(key, 0) < val:
                need[key] = val
        out = []
        for key, val in need.items():
            kn[key] = val
            out.append((self._semof(key), val))
        return out

    def _semof(self, key):
        if isinstance(key, str):
            return self.sem[key]
        return self.dsem[key[0]][key[1]]

    def op(self, eng, name, args, kw, reads, writes):
        deps = self._deps(reads, writes)
        raw = set()
        for v in reads:
            for t in _trks(v):
                if t.w is not None:
                    raw.add(t.w)
        waits = self._waits(eng, deps, raw)
        self.cnt[eng] += 1
        tok = (eng, self.cnt[eng], eng)
        self.q[eng].append((waits, name, args, kw, (self.sem[eng], 1)))
        for v in reads:
            for t in _trks(v):
                t.r.append(tok)
        for v in writes:
            for t in _trks(v):
                t.w = tok
                t.r = []
        return tok

    def dma(self, out, in_, qn="sp", **kw):
        deps = self._deps([in_], [out])
        n = self.dcnt[qn]
        slot = n % self.NSLOT
        gen = n // self.NSLOT
        self.dcnt[qn] += 1
        key = (qn, slot)
        waits = self._waits(qn, deps, set(), always=True)
        if gen > 0 and self.known[qn].get(key, 0) < 16 * gen:
            self.known[qn][key] = 16 * gen
            waits.append((self.dsem[qn][slot], 16 * gen))
        tok = (key, 16 * (gen + 1), qn)
        self.q[qn].append((waits, "dma_start", (), dict(out=out.ap, in_=in_.ap, **kw), (self.dsem[qn][slot], 16)))
        for t in _trks(in_):
            t.r.append(tok)
        for t in _trks(out):
            t.w = tok
            t.r = []
        return tok

    def barrier(self):
        toks = [(e, self.cnt[e], e) for e in self.CE if self.cnt[e] > 0]
        for qn in self.dsem:
            n = self.dcnt[qn]
            for s in range(self.NSLOT):
                k = (n - s + self.NSLOT - 1) // self.NSLOT
                if k > 0:
                    toks.append(((qn, s), 16 * k, qn))
        for eng in self.q:
            waits = self._waits(eng, toks, set(), always=True)
            if waits:
                self.q[eng].append((waits, None, (), {}, None))

    @staticmethod
    def _a(x):
        return x.ap if isinstance(x, V) else x

    def mm(self, out, lhsT, rhs, start=True, stop=True):
        rd = [lhsT, rhs] + ([] if start else [out])
        return self.op("pe", "matmul", (out.ap, lhsT.ap, rhs.ap), dict(start=start, stop=stop), rd, [out])

    def tr(self, out, in_, ident):
        return self.op("pe", "transpose", (out.ap, in_.ap, ident.ap), {}, [in_, ident], [out])

    def act(self, out, in_, func, scale=1.0, bias=0.0, accum_out=None):
        rd = [in_] + [x for x in (scale, bias) if isinstance(x, V)]
        wr = [out] + ([accum_out] if accum_out is not None else [])
        kw = dict(scale=self._a(scale), bias=self._a(bias))
        if accum_out is not None:
            kw["accum_out"] = accum_out.ap
        return self.op("act", "activation", (out.ap, in_.ap, func), kw, rd, wr)

    def ts(self, out, in0, s1, s2, op0, op1=None, eng="dve"):
        rd = [in0] + [x for x in (s1, s2) if isinstance(x, V)]
        kw = {}
        if op1 is not None:
            kw["op1"] = op1
        return self.op(eng, "tensor_scalar", (out.ap, in0.ap, self._a(s1), self._a(s2), op0), kw, rd, [out])

    def tt(self, out, in0, in1, op, eng="dve"):
        return self.op(eng, "tensor_tensor", (out.ap, in0.ap, in1.ap, op), {}, [in0, in1], [out])

    def stt(self, out, in0, s, in1, op0, op1):
        rd = [in0, in1] + ([s] if isinstance(s, V) else [])
        return self.op("dve", "scalar_tensor_tensor", (out.ap, in0.ap, self._a(s), in1.ap, op0, op1), {}, rd, [out])

    def cp(self, out, in_, eng="dve"):
        if eng == "act":
            return self.op("act", "activation", (out.ap, in_.ap, AF.Identity), {}, [in_], [out])
        return self.op(eng, "tensor_copy", (out.ap, in_.ap), {}, [in_], [out])

    def scan(self, out, d0, d1, init):
        rd = [d0, d1] + ([init] if isinstance(init, V) else [])
        return self.op("dve", "tensor_tensor_scan", (out.ap, d0.ap, d1.ap, self._a(init), ALU.mult, ALU.add), {}, rd, [out])

    def memset(self, out, val, eng="dve"):
        return self.op(eng, "memset", (out.ap, val), {}, [], [out])

    def recip(self, out, in_):
        return self.op("dve", "reciprocal", (out.ap, in_.ap), {}, [in_], [out])

    def wait_all(self, eng, toks):
        waits = self._waits(eng, toks, set(), always=True)
        self.q[eng].append((waits, None, (), {}, None))

    def build(self):
        nc = self.nc
        q = self.q

        def replay(e, ops):
            for waits, name, args, kw, inc in ops:
                for sem, val in waits:
                    e.wait_ge(sem, val)
                if name is None:
                    continue
                ins = getattr(e, name)(*args, **kw)
                if inc is not None:
                    ins.then_inc(inc[0], inc[1])

        with nc.Block() as block:
            @block.tensor
            def _(e):
                replay(e, q["pe"])

            @block.scalar
            def _(e):
                replay(e, q["act"])

            @block.vector
            def _(e):
                replay(e, q["dve"])

            @block.gpsimd
            def _(e):
                replay(e, q["pool"])

            @block.sync
            def _(e):
                replay(e, q["sp"])
        self.es.close()


class Arena:
    def __init__(self, P, nelem):
        self.P = P
        self.n = nelem
        self.h = P.es.enter_context(P.nc.sbuf_tensor("arena", [128, nelem], BF16))
        self.off = 0

    def reset(self, to=0):
        self.P.barrier()
        self.off = to

    def alloc(self, shape, dt):
        if isinstance(shape, int):
            shape = (shape,)
        n = int(np.prod(shape))
        four = dt in (F32, mybir.dt.int32)
        ne = n * 2 if four else n
        ne = (ne + 1) // 2 * 2
        assert self.off + ne <= self.n, "arena overflow %d + %d > %d" % (self.off, ne, self.n)
        ap = self.h[:, self.off:self.off + ne]
        self.off += ne
        if four:
            ap = ap.bitcast(dt)
        if n != (ne // 2 if four else ne):
            ap = ap[:, 0:n]
        if len(shape) == 2:
            ap = ap.rearrange("p (a b) -> p a b", a=shape[0])
        elif len(shape) == 3:
            ap = ap.rearrange("p (a b c) -> p a b c", a=shape[0], b=shape[1])
        return V(ap, Trk())


MCH = [(128 * i, 128) for i in range(16)] + [(2048, 16)] + [(2064 + 128 * i, 128) for i in range(56)]
MC_S5U, MC_LRUX, MC_LRUG, MC_SCB, MC_SCC, MC_SCX, MC_GATE = 17, 21, 25, 29, 33, 37, 41

WSPEC = [
    ("mix_norm", (128, 8)), ("w_in", (D, INW)), ("gdn_conv", (128, 12, 4)), ("gdn_alog", (1, 8)),
    ("gdn_dtb", (1, 8)), ("gdn_gain", (1, 128)),
    ("s5_lre", (128, 32)), ("s5_lim", (128, 32)), ("s5_lstep", (128, 32)),
    ("s5_btre", (32, 128, 128)), ("s5_btim", (32, 128, 128)), ("s5_ctre", (32, 128, 128)), ("s5_ctim", (32, 128, 128)),
    ("s5_d", (128, 4)), ("s5_glu_w", (W, W)), ("s5_glu_b", (128, 4)),
    ("lru_cw", (128, 4, 4)), ("lru_cb", (128, 4)), ("lru_wa", (8, 128, 128)), ("lru_wx", (8, 128, 128)),
    ("lru_ba", (128, 8)), ("lru_bx", (128, 8)), ("lru_lam", (128, 8)),
    ("sc_cw", (128, 4, 3)), ("w_branch", (4 * W, D)), ("w_mix_out", (D, D)),
    ("xa_norm", (128, 8)), ("xa_mnorm", (128, 8)), ("xa_wq", (D, D)), ("xa_wkv", (D, 2 * D)), ("xa_wo", (D, D)),
    ("ffn_norm", (128, 8)), ("ffn_wup", (D, 2 * DFF)), ("ffn_cw", (128, 44, 3)), ("ffn_cb", (128, 44)),
    ("ffn_wdown", (DFF, D)),
]


def build_program(S, NL, dbg=()):
    T = 2 * S
    NT5 = S // 512
    NT1 = S // 128
    LOGS = int(round(math.log2(S)))
    nc = bass.Bass("TRN2", target_bir_lowering=False)
    P = Prog(nc)
    P.init_banks()
    xT = P.dram("xT", [D, T], F32, kind="ExternalInput")
    memT = P.dram("memT", [D, 2 * MEM], F32, kind="ExternalInput")
    wts = {}
    for name, shp in WSPEC:
        wts[name] = P.dram(name, [NL] + list(shp), F32, kind="ExternalInput")
    fin_norm = P.dram("fin_norm", [128, 8], F32, kind="ExternalInput")
    c_ident = P.dram("c_ident", [128, 128], F32, kind="ExternalInput")
    c_ucum = P.dram("c_ucum", [2, 128, 128], F32, kind="ExternalInput")
    c_negm = P.dram("c_negm", [2, 128, 128], F32, kind="ExternalInput")
    c_msk = P.dram("c_msk", [7, 128, 128], F32, kind="ExternalInput")
    xoT = P.dram("xoT", [D, T], F32, kind="ExternalOutput")
    ynT = P.dram("ynT", [D, T], F32, kind="ExternalOutput")
    xres = nc.dram_tensor("xres", [D, T], F32, kind="Internal").ap()
    projT = nc.dram_tensor("projT", [73 * 128, T], BF16, kind="Internal").ap()
    baTok = nc.dram_tensor("baTok", [T, 16], F32, kind="Internal").ap()
    ofwd = nc.dram_tensor("ofwd", [T, 512], F32, kind="Internal").ap()
    ysT = nc.dram_tensor("ysT", [4 * W, T], BF16, kind="ExternalOutput" if dbg else "Internal").ap()
    hffT = nc.dram_tensor("hffT", [D, T], BF16, kind="Internal").ap()
    dtrk = {}

    def DV(ap, *keys):
        ts_ = []
        for k in keys:
            if k not in dtrk:
                dtrk[k] = Trk()
            ts_.append(dtrk[k])
        return V(ap, ts_)

    def xres_v(b, cols=None):
        ap = xres[:, b * S:(b + 1) * S] if cols is None else xres[:, b * S + cols[0]: b * S + cols[1]]
        return DV(ap.rearrange("(k p) t -> p k t", p=128), ("xres", b))

    ident_f = P.sb([128, 128], F32)
    ident_b = P.sb([128, 128], BF16)
    ones_f = P.sb([128, 128], F32)
    ones_b = P.sb([128, 128], BF16)
    identf4 = P.sb([128, 4, 128], F32)
    ucum = P.sb([128, 2, 128], F32)
    negm = P.sb([128, 2, 128], F32)
    eps_c = P.sb([128, 1], F32)
    one_c = P.sb([128, 1], F32)
    P.dma(ident_f, c_ident)
    P.dma(ucum, c_ucum.rr("d p i -> p d i"))
    P.dma(negm, c_negm.rr("d p i -> p d i"))
    mskf = P.sb([128, 7, 128], F32)
    msk = P.sb([128, 7, 128], BF16)
    P.dma(mskf, c_msk.rr("d p i -> p d i"))
    P.cp(msk, mskf)
    P.cp(ident_b, ident_f)
    P.memset(ones_f, 1.0)
    P.memset(ones_b, 1.0)
    P.memset(eps_c, EPS)
    P.memset(one_c, 1.0)
    for h in range(4):
        P.cp(identf4[:, h, :], ident_f)
    rem = nc.sbuf_bytes_remaining
    rem = rem() if callable(rem) else rem
    ar = Arena(P, (int(rem) - 4096) // 2 // 2 * 2)

    for b in range(2):
        P.dma(DV(xres[:, b * S:(b + 1) * S], ("xres", b)), V(xT.ap[:, b * S:(b + 1) * S], xT.t))

    rr_ = [0]

    def evac(out, in_):
        rr_[0] += 1
        P.cp(out, in_, eng="act" if rr_[0] % 2 else "dve")

    def load_w(dst, src_ap, trk):
        P.dma(dst, V(src_ap, trk), qn="pool")

    def rmsnorm_tile(xt, gain, dst, n, tmp_sq, tmp_r):
        P.act(tmp_sq[:, :, 0:n], xt, AF.Square)
        ps = P.bank()
        for kc in range(8):
            P.mm(ps[:, 0:n], ones_b, tmp_sq[:, kc, 0:n], start=(kc == 0), stop=(kc == 7))
        P.act(tmp_r[:, 0:n], ps[:, 0:n], AF.Sqrt, scale=1.0 / D, bias=eps_c)
        P.recip(tmp_r[:, 0:n], tmp_r[:, 0:n])
        for kc in range(8):
            P.stt(dst[:, kc, :], xt[:, kc, :], gain[:, kc:kc + 1], tmp_r[:, 0:n], ALU.mult, ALU.mult)

    for l in range(NL):
        wl = {k: V(v.ap[l], v.t) for k, v in wts.items()}

        def phase_A(b):
            ar.reset()
            hT = ar.alloc((8, S), BF16)
            gain = ar.alloc(8, F32)
            P.dma(gain, wl["mix_norm"])
            xb = [ar.alloc((8, 512), F32) for _ in range(2)]
            sq = ar.alloc((8, 512), BF16)
            rt = ar.alloc(512, F32)
            for tt in range(NT5):
                xt = xb[tt % 2]
                P.dma(xt, xres_v(b, (tt * 512, tt * 512 + 512)))
                rmsnorm_tile(xt, gain, hT[:, :, tt * 512:(tt + 1) * 512], 512, sq, rt)
            wb = [ar.alloc((8, 512), BF16) for _ in range(2)]
            st = [ar.alloc(512, BF16) for _ in range(4)]
            wba = ar.alloc((8, 16), BF16)
            batok = ar.alloc((NT1, 16), F32)
            for kc in range(8):
                load_w(wba[:, kc, :], wl["w_in"].ap[kc * 128:(kc + 1) * 128, 2048:2064], wl["w_in"].t)
            for n in range(NT1):
                ps = P.bank()
                for kc in range(8):
                    P.mm(ps[:, 0:16], hT[:, kc, n * 128:(n + 1) * 128], wba[:, kc, :], start=(kc == 0), stop=(kc == 7))
                evac(batok[:, n, :], ps[:, 0:16])
            for n0 in range(0, NT1, 8):
                n1 = min(n0 + 8, NT1)
                P.dma(DV(baTok[b * S + n0 * 128: b * S + n1 * 128, :].rearrange("(n p) c -> p n c", p=128), ("ba", b, n0)), batok[:, n0:n1, :])
            groups = [list(range(g * 4, g * 4 + 4)) for g in range(4)] + [list(range(17 + g * 4, 17 + g * 4 + 4)) for g in range(14)]
            si = 0
            for gi, grp in enumerate(groups):
                wt = wb[gi % 2]
                c0 = MCH[grp[0]][0]
                for kc in range(8):
                    load_w(wt[:, kc, :], wl["w_in"].ap[kc * 128:(kc + 1) * 128, c0:c0 + 512], wl["w_in"].t)
                for tt in range(NT5):
                    for j, mc in enumerate(grp):
                        ps = P.bank()
                        for kc in range(8):
                            P.mm(ps, wt[:, kc, j * 128:(j + 1) * 128], hT[:, kc, tt * 512:(tt + 1) * 512],
                                 start=(kc == 0), stop=(kc == 7))
                        s_ = st[si % 4]
                        si += 1
                        if mc >= MC_GATE:
                            P.act(s_, ps, AF.Sigmoid)
                        else:
                            evac(s_, ps)
                        P.dma(DV(projT[mc * 128:(mc + 1) * 128, b * S + tt * 512: b * S + (tt + 1) * 512], ("proj", mc, b, tt)), s_)

        def proj_v(mc, b, n=1):
            keys = [("proj", mc + i, b, tt) for i in range(n) for tt in range(NT5)]
            return DV(projT[mc * 128:(mc + n) * 128, b * S:(b + 1) * S], *keys)

        def phase_gdn(b):
            ar.reset()
            qn = ar.alloc((4, S), BF16)
            kn = ar.alloc((4, S), BF16)
            vv = ar.alloc((4, S), BF16)
            cw = ar.alloc((12, 4), F32)
            P.dma(cw, wl["gdn_conv"])
            mark = ar.off
            xp = ar.alloc(S + 4, BF16)
            acc = ar.alloc(S, F32)
            sqb = ar.alloc(S, BF16)
            rtf = ar.alloc(S, F32)
            P.memset(xp[:, 0:2], 0.0)
            P.memset(xp[:, S + 2:S + 4], 0.0)
            for c in range(12):
                which, hd = c // 4, c % 4
                P.dma(xp[:, 2:S + 2], proj_v(c, b))
                P.ts(acc, xp[:, 0:S], cw[:, c, 0:1], None, ALU.mult)
                for k in range(1, 4):
                    P.stt(acc, xp[:, k:k + S], cw[:, c, k:k + 1], acc, ALU.mult, ALU.add)
                if which == 2:
                    P.act(vv[:, hd, :], acc, AF.Silu)
                    continue
                P.act(acc, acc, AF.Silu)
                P.act(sqb, acc, AF.Square)
                for tt in range(NT5):
                    ps = P.bank()
                    P.mm(ps, ones_b, sqb[:, tt * 512:(tt + 1) * 512])
                    P.act(rtf[:, tt * 512:(tt + 1) * 512], ps, AF.Sqrt, bias=eps_c)
                P.recip(rtf, rtf)
                P.tt((qn if which == 0 else kn)[:, hd, :], acc, rtf, ALU.mult)
            ar.reset(mark)
            bt = ar.alloc((NT1, 16), F32)
            for n0 in range(0, NT1, 8):
                n1 = min(n0 + 8, NT1)
                P.dma(bt[:, n0:n1, :], DV(baTok[b * S + n0 * 128: b * S + n1 * 128, :].rearrange("(n p) c -> p n c", p=128), ("ba", b, n0)))
            beta = ar.alloc((NT1, 8), F32)
            gtk = ar.alloc((NT1, 8), F32)
            dtb = ar.alloc(8, F32)
            nega = ar.alloc(8, F32)
            gainrow = ar.alloc(128, F32)
            P.dma(dtb, V(wl["gdn_dtb"].ap.partition_broadcast(128), wl["gdn_dtb"].t))
            P.dma(nega, V(wl["gdn_alog"].ap.partition_broadcast(128), wl["gdn_alog"].t))
            P.dma(gainrow, V(wl["gdn_gain"].ap.partition_broadcast(128), wl["gdn_gain"].t))
            P.act(nega, nega, AF.Exp)
            P.ts(nega, nega, -1.0, None, ALU.mult)
            P.act(beta, bt[:, :, 0:8], AF.Sigmoid)
            P.tt(gtk, bt[:, :, 8:16], dtb.rr("p (o c) -> p o c", o=1).bc([128, NT1, 8]), ALU.add)
            P.act(gtk, gtk, AF.Exp)
            P.act(gtk, gtk, AF.Ln, bias=one_c)
            P.tt(gtk, gtk, nega.rr("p (o c) -> p o c", o=1).bc([128, NT1, 8]), ALU.mult)
            A = lambda shp, dt: ar.alloc(shp, dt)
            ktok, vtok = A((4, 128), BF16), A((4, 128), BF16)
            gam, ngam, eg, negeg, egs, dl, egl, gl = [A(4, F32) for _ in range(8)]
            gbs = A((4, 128), F32)
            Ds = A((4, 128), F32)
            Di = A((4, 128), F32)
            Y = A((4, 128), BF16)
            qkT = A((4, 128), BF16)
            Zs = A((7, 4 * 128), BF16)
            Zb = [A((4, 128), BF16) for _ in range(1)]
            Xb = [A((4, 128), BF16) for _ in range(2)]
            Wt, T1 = A((4, 128), BF16), A((4, 128), BF16)
            xx, vn, kdec = A((4, 128), BF16), A((4, 128), BF16), A((4, 128), BF16)
            Bs, o_ = A((4, 128), F32), A((4, 128), F32)
            Sm, Sb = A((4, 128), F32), A((4, 128), BF16)
            of_, os_, junk = A((4, 128), F32), A((4, 128), F32), A(128, F32)
            ss, rs = A(4, F32), A(4, F32)
            on = A((4, 128), BF16)
            zt = A((4, 128), BF16)
            sz = A((4, 128), F32)
            y0 = A((4, 128), BF16)
            sdk = 128.0 ** -0.5

            def h4(v):
                return v.rr("p (h d) -> p h d", h=4)

            for d in range(2):
                P.memset(Sm, 0.0)
                P.memset(Sb, 0.0)
                order = range(NT1) if d == 0 else range(NT1 - 1, -1, -1)
                for n in order:
                    sl = slice(n * 128, (n + 1) * 128)
                    pk = h4(P.bankbf()[:, 0:512])
                    for h in range(4):
                        P.tr(pk[:, h, :], kn[:, h, sl], ident_b)
                    P.cp(ktok, pk, eng="act")
                    pv = h4(P.bankbf()[:, 0:512])
                    for h in range(4):
                        P.tr(pv[:, h, :], vv[:, h, sl], ident_b)
                    P.cp(vtok, pv)
                    gcol = gtk[:, n, d * 4:(d + 1) * 4]
                    bcol = beta[:, n, d * 4:(d + 1) * 4]
                    pg = P.bank()
                    P.mm(pg[:, 0:4], ucum[:, d, :], gcol)
                    P.mm(pg[:, 4:8], ones_f, gcol)
                    P.cp(gam, pg[:, 0:4])
                    P.ts(ngam, pg[:, 0:4], -1.0, None, ALU.mult)
                    P.act(eg, pg[:, 0:4], AF.Exp)
                    P.ts(negeg, eg, -1.0, None, ALU.mult)
                    P.ts(egs, eg, sdk, None, ALU.mult)
                    P.tt(dl, pg[:, 4:8], gam, ALU.subtract)
                    P.act(egl, dl, AF.Exp)
                    P.act(gl, pg[:, 4:8], AF.Exp)
                    pd = h4(P.bank())
                    for h in range(4):
                        P.ts(gbs[:, h, :], ones_f, gcol[:, h:h + 1], None, ALU.mult)
                        P.mm(pd[:, h, :], gbs[:, h, :], ucum[:, d, :], start=True, stop=False)
                        P.mm(pd[:, h, :], ident_f, negm[:, d, :], start=False, stop=True)
                    for h in range(4):
                        P.act(Ds[:, h, :], pd[:, h, :], AF.Exp, bias=ngam[:, h:h + 1])
                    pkk = h4(P.bank())
                    pqk = h4(P.bank())
                    for h in range(4):
                        P.mm(pkk[:, h, :], kn[:, h, sl], kn[:, h, sl])
                    for h in range(4):
                        P.mm(pqk[:, h, :], kn[:, h, sl], qn[:, h, sl])
                    for h in range(4):
                        P.stt(Y[:, h, :], pkk[:, h, :], bcol[:, h:h + 1], Ds[:, h, :], ALU.mult, ALU.mult)
                    P.tt(Di, Ds, identf4, ALU.add)
                    P.stt(qkT, pqk, sdk, Di, ALU.mult, ALU.mult)
                    pz = h4(P.bankbf()[:, 0:512])
                    for h in range(4):
                        P.tr(pz[:, h, :], Y[:, h, :], ident_b)
                    Z = Zb[0]
                    P.cp(Z, pz, eng="act")
                    for lv in range(1, 7):
                        mb = V(msk.ap[:, lv:lv + 1, :].broadcast_to([128, 4, 128]), msk.t)
                        P.tt(Zs[:, lv, :].rr("p (h d) -> p h d", h=4), Z, mb, ALU.mult, eng="pool")
                    m0 = V(msk.ap[:, 0:1, :].broadcast_to([128, 4, 128]), msk.t)
                    P.tt(T1, Y, m0, ALU.mult)
                    X = Xb[0]
                    P.tt(X, identf4, T1, ALU.subtract)
                    for lv in range(1, 7):
                        Xn = Xb[lv % 2]
                        pW = h4(P.bankbf()[:, 0:512])
                        for h in range(4):
                            P.tr(pW[:, h, :], X[:, h, :], ident_b)
                        P.cp(Wt, pW, eng="act")
                        p1 = h4(P.bank())
                        for h in range(4):
                            P.mm(p1[:, h, :], Zs[:, lv, h * 128:(h + 1) * 128], X[:, h, :])
                        P.cp(T1, p1)
                        p2 = h4(P.bank())
                        for h in range(4):
                            P.mm(p2[:, h, :], Wt[:, h, :], T1[:, h, :])
                        P.tt(Xn, X, p2, ALU.subtract)
                        X = Xn
                    pks = h4(P.bank())
                    for h in range(4):
                        P.mm(pks[:, h, :], kn[:, h, sl], Sb[:, h, :])
                    for h in range(4):
                        P.stt(xx[:, h, :], pks[:, h, :], negeg[:, h:h + 1], vtok[:, h, :], ALU.mult, ALU.add)
                    pvn = h4(P.bank())
                    for h in range(4):
                        P.mm(pvn[:, h, :], X[:, h, :], xx[:, h, :])
                    for h in range(4):
                        P.act(vn[:, h, :], pvn[:, h, :], AF.Identity, scale=bcol[:, h:h + 1])
                    pA = h4(P.bank())
                    for h in range(4):
                        P.mm(pA[:, h, :], qn[:, h, sl], Sb[:, h, :])
                    pB = h4(P.bank())
                    for h in range(4):
                        P.mm(pB[:, h, :], qkT[:, h, :], vn[:, h, :])
                    P.cp(Bs, pB, eng="act")
                    for h in range(4):
                        P.stt(o_[:, h, :], pA[:, h, :], egs[:, h:h + 1], Bs[:, h, :], ALU.mult, ALU.add)
                    for h in range(4):
                        P.act(kdec[:, h, :], ktok[:, h, :], AF.Identity, scale=egl[:, h:h + 1])
                    pS = h4(P.bank())
                    for h in range(4):
                        P.mm(pS[:, h, :], kdec[:, h, :], vn[:, h, :])
                    for h in range(4):
                        P.stt(Sm[:, h, :], Sm[:, h, :], gl[:, h:h + 1], pS[:, h, :], ALU.mult, ALU.add)
                    P.cp(Sb, Sm, eng="act")
                    ofv = DV(ofwd[b * S + n * 128: b * S + (n + 1) * 128, :].rearrange("p (h d) -> p h d", h=4), ("ofwd", b, n))
                    if d == 0:
                        P.dma(ofv, o_)
                        continue
                    P.dma(of_, ofv)
                    P.tt(os_, o_, of_, ALU.add)
                    for h in range(4):
                        P.act(junk, os_[:, h, :], AF.Square, accum_out=ss[:, h:h + 1])
                    P.act(rs, ss, AF.Sqrt, scale=1.0 / 128, bias=eps_c)
                    P.recip(rs, rs)
                    for h in range(4):
                        P.stt(on[:, h, :], os_[:, h, :], rs[:, h:h + 1], gainrow, ALU.mult, ALU.mult)
                    pt = h4(P.bankbf()[:, 0:512])
                    for h in range(4):
                        P.tr(pt[:, h, :], on[:, h, :], ident_b)
                    zv = proj_v(12, b, 4)
                    P.dma(zt, V(zv.ap[:, n * 128:(n + 1) * 128].rearrange("(h p) t -> p h t", p=128), zv.t))
                    P.act(sz, zt, AF.Silu)
                    P.tt(y0, pt, sz, ALU.mult)
                    P.dma(DV(ysT[0:512, b * S + n * 128: b * S + (n + 1) * 128].rearrange("(h p) t -> p h t", p=128), ("ys", 0, b, n)), y0)

        def s5_tables():
            lre, lim, lst = [ar.alloc(32, F32) for _ in range(3)]
            P.dma(lre, wl["s5_lre"])
            P.dma(lim, wl["s5_lim"])
            P.dma(lst, wl["s5_lstep"])
            t = [ar.alloc(32, F32) for _ in range(8)]
            ti = ar.alloc(32, mybir.dt.int32)
            fre, fim, nfim = [ar.alloc(32, F32) for _ in range(3)]
            pw = ar.alloc((LOGS, 3, 32), F32)
            P.act(lst, lst, AF.Exp)
            P.ts(lre, lre, -1e-4, None, ALU.min)
            mag, ph = t[0], t[1]
            P.tt(mag, lre, lst, ALU.mult)
            P.act(mag, mag, AF.Exp)
            P.tt(ph, lim, lst, ALU.mult)

            def sin_of(dst, src, shift):
                u, r, m = t[5], t[6], t[7]
                P.ts(u, src, shift, 1.0 / (2 * math.pi), ALU.add, ALU.mult)
                P.cp(ti, u)
                P.cp(r, ti)
                P.tt(r, u, r, ALU.subtract)
                P.ts(m, r, 0.5, None, ALU.is_gt)
                P.tt(r, r, m, ALU.subtract)
                P.ts(m, r, -0.5, None, ALU.is_lt)
                P.tt(r, r, m, ALU.add)
                P.act(dst, r, AF.Sin, scale=2 * math.pi)

            cs, sn = t[2], t[3]
            sin_of(sn, ph, 0.0)
            sin_of(cs, ph, math.pi / 2)
            are, aim = pw[:, 0, 0, :], pw[:, 0, 1, :]
            P.tt(are, mag, cs, ALU.mult)
            P.tt(aim, mag, sn, ALU.mult)
            den, am1, x1, x2 = t[0], t[1], t[2], t[3]
            P.tt(den, lre, lre, ALU.mult)
            P.tt(x1, lim, lim, ALU.mult)
            P.tt(den, den, x1, ALU.add)
            P.recip(den, den)
            P.ts(am1, are, -1.0, None, ALU.add)
            P.tt(x1, am1, lre, ALU.mult)
            P.tt(x2, aim, lim, ALU.mult)
            P.tt(x1, x1, x2, ALU.add)
            P.tt(fre, x1, den, ALU.mult)
            P.tt(x1, aim, lre, ALU.mult)
            P.tt(x2, am1, lim, ALU.mult)
            P.tt(x1, x1, x2, ALU.subtract)
            P.tt(fim, x1, den, ALU.mult)
            P.ts(nfim, fim, -1.0, None, ALU.mult)
            for k in range(LOGS):
                re_, im_ = pw[:, k, 0, :], pw[:, k, 1, :]
                P.ts(pw[:, k, 2, :], im_, -1.0, None, ALU.mult)
                if k + 1 < LOGS:
                    P.tt(x1, re_, re_, ALU.mult)
                    P.tt(x2, im_, im_, ALU.mult)
                    P.tt(pw[:, k + 1, 0, :], x1, x2, ALU.subtract)
                    P.tt(x1, re_, im_, ALU.mult)
                    P.ts(pw[:, k + 1, 1, :], x1, 2.0, None, ALU.mult)
            return fre, fim, nfim, pw

        def phase_s5():
            ar.reset()
            fre, fim, nfim, pw = s5_tables()
            dsk = ar.alloc(4, F32)
            glb = ar.alloc(4, F32)
            P.dma(dsk, wl["s5_d"])
            P.dma(glb, wl["s5_glu_b"])
            gw = ar.alloc((4, W), BF16)
            for kc in range(4):
                load_w(gw[:, kc, :], wl["s5_glu_w"].ap[kc * 128:(kc + 1) * 128, :], wl["s5_glu_w"].t)
            wB = ar.alloc((8, 2, 128), BF16)
            wC = ar.alloc((8, 2, 128), BF16)
            uT = ar.alloc(S, BF16)
            yacc = ar.alloc(S, F32)
            zg = ar.alloc((4, S), BF16)
            RA = [ar.alloc(S, F32) for _ in range(2)]
            RB = [ar.alloc(S, F32) for _ in range(2)]
            tm = [ar.alloc(512, F32) for _ in range(2)]
            stg = [ar.alloc(512, BF16) for _ in range(2)]
            sg = ar.alloc(512, F32)
            for b in range(2):
                for cc in range(4):
                    for pr in range(4):
                        for d in range(2):
                            ii = pr * 2 + d
                            gi = d * 16 + cc * 4 + pr
                            load_w(wB[:, ii, 0, :], wl["s5_btre"].ap[gi], wl["s5_btre"].t)
                            load_w(wB[:, ii, 1, :], wl["s5_btim"].ap[gi], wl["s5_btim"].t)
                            load_w(wC[:, ii, 0, :], wl["s5_ctre"].ap[gi], wl["s5_ctre"].t)
                            load_w(wC[:, ii, 1, :], wl["s5_ctim"].ap[gi], wl["s5_ctim"].t)
                    P.ts(wC[:, :, 1, :], wC[:, :, 1, :], -1.0, None, ALU.mult)
                    P.dma(uT, proj_v(MC_S5U + cc, b))
                    P.ts(yacc, uT, dsk[:, cc:cc + 1], None, ALU.mult)
                    for pr in range(4):
                        for d in range(2):
                            ii = pr * 2 + d
                            col = d * 16 + cc * 4 + pr
                            cur, oth = RA, RB
                            for tt in range(NT5):
                                ts_ = slice(tt * 512, (tt + 1) * 512)
                                p1, p2 = P.bank(), P.bank()
                                P.mm(p1, wB[:, ii, 0, :], uT[:, ts_])
                                P.mm(p2, wB[:, ii, 1, :], uT[:, ts_])
                                P.ts(tm[0], p1, fre[:, col:col + 1], None, ALU.mult)
                                P.stt(cur[0][:, ts_], p2, nfim[:, col:col + 1], tm[0], ALU.mult, ALU.add)
                                P.ts(tm[1], p2, fre[:, col:col + 1], None, ALU.mult)
                                P.stt(cur[1][:, ts_], p1, fim[:, col:col + 1], tm[1], ALU.mult, ALU.add)
                            for k in range(LOGS):
                                sh = 1 << k
                                cr, ci, nci = pw[:, k, 0, col:col + 1], pw[:, k, 1, col:col + 1], pw[:, k, 2, col:col + 1]
                                if d == 0:
                                    dst, src, keep = slice(sh, S), slice(0, S - sh), slice(0, sh)
                                else:
                                    dst, src, keep = slice(0, S - sh), slice(sh, S), slice(S - sh, S)
                                P.stt(oth[0][:, dst], cur[0][:, src], cr, cur[0][:, dst], ALU.mult, ALU.add)
                                P.stt(oth[0][:, dst], cur[1][:, src], nci, oth[0][:, dst], ALU.mult, ALU.add)
                                P.stt(oth[1][:, dst], cur[1][:, src], cr, cur[1][:, dst], ALU.mult, ALU.add)
                                P.stt(oth[1][:, dst], cur[0][:, src], ci, oth[1][:, dst], ALU.mult, ALU.add)
                                P.cp(oth[0][:, keep], cur[0][:, keep], eng="act")
                                P.cp(oth[1][:, keep], cur[1][:, keep], eng="act")
                                cur, oth = oth, cur
                            hb = [V(oth[i].ap.bitcast(BF16)[:, 0:S], oth[i].t) for i in range(2)]
                            P.cp(hb[0], cur[0], eng="act")
                            P.cp(hb[1], cur[1], eng="act")
                            for tt in range(NT5):
                                ts_ = slice(tt * 512, (tt + 1) * 512)
                                pc = P.bank()
                                P.mm(pc, wC[:, ii, 0, :], hb[0][:, ts_], start=True, stop=False)
                                P.mm(pc, wC[:, ii, 1, :], hb[1][:, ts_], start=False, stop=True)
                                P.tt(yacc[:, ts_], yacc[:, ts_], pc, ALU.add)
                    P.act(zg[:, cc, :], yacc, AF.Gelu)
                for co in range(4):
                    for tt in range(NT5):
                        ts_ = slice(tt * 512, (tt + 1) * 512)
                        ps = P.bank()
                        for kc in range(4):
                            P.mm(ps, gw[:, kc, co * 128:(co + 1) * 128], zg[:, kc, ts_], start=(kc == 0), stop=(kc == 3))
                        P.act(sg, ps, AF.Sigmoid, bias=glb[:, co:co + 1])
                        s_ = stg[(co * NT5 + tt) % 2]
                        P.tt(s_, zg[:, co, ts_], sg, ALU.mult)
                        P.dma(DV(ysT[512 + co * 128: 512 + (co + 1) * 128, b * S + tt * 512: b * S + (tt + 1) * 512], ("ys", 1, b, co, tt)), s_)

        def phase_lru_sc():
            ar.reset()
            cwl = ar.alloc((4, 4), F32)
            cbl = ar.alloc(4, F32)
            ba_, bx_, lam_ = ar.alloc(8, F32), ar.alloc(8, F32), ar.alloc(8, F32)
            scw = ar.alloc((4, 3), F32)
            P.dma(cwl, wl["lru_cw"])
            P.dma(cbl, wl["lru_cb"])
            P.dma(ba_, wl["lru_ba"])
            P.dma(bx_, wl["lru_bx"])
            P.dma(lam_, wl["lru_lam"])
            P.dma(scw, wl["sc_cw"])
            WA, WX = ar.alloc((8, 128), BF16), ar.alloc((8, 128), BF16)
            for i in range(8):
                load_w(WA[:, i, :], wl["lru_wa"].ap[i], wl["lru_wa"].t)
                load_w(WX[:, i, :], wl["lru_wx"].ap[i], wl["lru_wx"].t)
            P.act(lam_, lam_, AF.Exp, scale=-1.0)
            P.act(lam_, lam_, AF.Ln, bias=one_c)
            P.ts(lam_, lam_, -8.0, None, ALU.mult)
            xp = ar.alloc(S + 4, BF16)
            xc = ar.alloc(S, F32)
            xcb = ar.alloc(S, BF16)
            aa, bb = ar.alloc(S, F32), ar.alloc(S, F32)
            hh = [ar.alloc(S, F32) for _ in range(2)]
            gch = ar.alloc(S, BF16)
            gg = ar.alloc(S, F32)
            yo = ar.alloc(S, BF16)
            r_, i_, t_ = ar.alloc(512, F32), ar.alloc(512, F32), ar.alloc(512, F32)
            s1, s2, s3 = ar.alloc(S, BF16), ar.alloc(S, BF16), ar.alloc(S + 2, F32)
            P.memset(xp[:, 0:2], 0.0)
            P.memset(xp[:, S + 2:S + 4], 0.0)
            P.memset(s3[:, 0:1], 0.0)
            P.memset(s3[:, S + 1:S + 2], 0.0)
            for b in range(2):
                for c in range(4):
                    P.dma(xp[:, 2:S + 2], proj_v(MC_LRUX + c, b))
                    P.dma(gch, proj_v(MC_LRUG + c, b))
                    P.ts(xc, xp[:, 0:S], cwl[:, c, 0:1], cbl[:, c:c + 1], ALU.mult, ALU.add)
                    for k in range(1, 4):
                        P.stt(xc, xp[:, k:k + S], cwl[:, c, k:k + 1], xc, ALU.mult, ALU.add)
                    P.cp(xcb, xc, eng="act")
                    for d in range(2):
                        j = d * 4 + c
                        for tt in range(NT5):
                            ts_ = slice(tt * 512, (tt + 1) * 512)
                            pr_, pi_ = P.bank(), P.bank()
                            P.mm(pr_, WA[:, j, :], xcb[:, ts_])
                            P.mm(pi_, WX[:, j, :], xcb[:, ts_])
                            P.act(r_, pr_, AF.Sigmoid, bias=ba_[:, j:j + 1])
                            P.act(i_, pi_, AF.Sigmoid, bias=bx_[:, j:j + 1])
                            P.act(aa[:, ts_], r_, AF.Exp, scale=lam_[:, j:j + 1])
                            P.act(t_, aa[:, ts_], AF.Square)
                            P.act(t_, t_, AF.Sqrt, scale=-1.0, bias=one_c)
                            P.tt(i_, i_, t_, ALU.mult)
                            P.tt(bb[:, ts_], i_, xc[:, ts_], ALU.mult)
                        if d == 0:
                            P.scan(hh[0], aa, bb, 0.0)
                        else:
                            P.scan(hh[1][:, ::-1], aa[:, ::-1], bb[:, ::-1], 0.0)
                    P.tt(hh[0], hh[0], hh[1], ALU.add)
                    P.act(gg, gch, AF.Gelu)
                    P.tt(yo, hh[0], gg, ALU.mult)
                    P.dma(DV(ysT[1024 + c * 128: 1024 + (c + 1) * 128, b * S:(b + 1) * S], ("ys", 2, b, c)), yo)
                    P.dma(s1, proj_v(MC_SCC + c, b))
                    P.dma(s2, proj_v(MC_SCX + c, b))
                    P.tt(s3[:, 1:S + 1], s1, s2, ALU.mult)
                    P.dma(s1, proj_v(MC_SCB + c, b))
                    P.ts(gg, s3[:, 0:S], scw[:, c, 0:1], None, ALU.mult)
                    for k in range(1, 3):
                        P.stt(gg, s3[:, k:k + S], scw[:, c, k:k + 1], gg, ALU.mult, ALU.add)
                    P.tt(s2, gg, s1, ALU.mult)
                    P.dma(DV(ysT[1536 + c * 128: 1536 + (c + 1) * 128, b * S:(b + 1) * S], ("ys", 3, b, c)), s2)

        def ys_v(m, b, cols):
            if m == 0:
                keys = [("ys", 0, b, n) for n in range(NT1)]
            elif m == 1:
                keys = [("ys", 1, b, co, tt) for co in range(4) for tt in range(NT5)]
            else:
                keys = [("ys", m, b, c) for c in range(4)]
            return DV(ysT[m * 512:(m + 1) * 512, b * S + cols[0]: b * S + cols[1]].rearrange("(k p) t -> p k t", p=128), *keys)

        def phase_merge_xa():
            ar.reset()
            wbr = ar.alloc((16, D), BF16)
            wmo = ar.alloc((8, D), BF16)
            wq = ar.alloc((8, D), BF16)
            wo = ar.alloc((8, D), BF16)
            for kc in range(16):
                for hf in range(2):
                    load_w(wbr[:, kc, hf * 512:(hf + 1) * 512], wl["w_branch"].ap[kc * 128:(kc + 1) * 128, hf * 512:(hf + 1) * 512], wl["w_branch"].t)
            for kc in range(8):
                for hf in range(2):
                    cs = slice(hf * 512, (hf + 1) * 512)
                    load_w(wmo[:, kc, cs], wl["w_mix_out"].ap[kc * 128:(kc + 1) * 128, cs], wl["w_mix_out"].t)
                    load_w(wq[:, kc, cs], wl["xa_wq"].ap[kc * 128:(kc + 1) * 128, cs], wl["xa_wq"].t)
                    load_w(wo[:, kc, cs], wl["xa_wo"].ap[kc * 128:(kc + 1) * 128, cs], wl["xa_wo"].t)
            gx = ar.alloc(8, F32)
            gm = ar.alloc(8, F32)
            P.dma(gx, wl["xa_norm"])
            P.dma(gm, wl["xa_mnorm"])
            KT = ar.alloc((8, MEM), BF16)
            Vm = ar.alloc((2, D), BF16)
            mark = ar.off
            for b in range(2):
                ar.reset(mark)
                wkv = ar.alloc((8, 512), BF16)
                mt = ar.alloc((8, MEM), F32)
                mh = ar.alloc((8, MEM), BF16)
                sq = ar.alloc((8, 512), BF16)
                rt = ar.alloc(512, F32)
                P.dma(mt, V(memT.ap[:, b * MEM:(b + 1) * MEM].rearrange("(k p) t -> p k t", p=128), memT.t))
                rmsnorm_tile(mt, gm, mh, MEM, sq, rt)
                for cb in range(4):
                    for kc in range(8):
                        load_w(wkv[:, kc, :], wl["xa_wkv"].ap[kc * 128:(kc + 1) * 128, cb * 512:(cb + 1) * 512], wl["xa_wkv"].t)
                    if cb < 2:
                        for j in range(4):
                            ps = P.bank()
                            for kc in range(8):
                                P.mm(ps[:, 0:MEM], wkv[:, kc, j * 128:(j + 1) * 128], mh[:, kc, :], start=(kc == 0), stop=(kc == 7))
                            evac(KT[:, cb * 4 + j, :], ps[:, 0:MEM])
                    else:
                        for mz in range(2):
                            ps = P.bank()
                            for kc in range(8):
                                P.mm(ps, mh[:, kc, mz * 128:(mz + 1) * 128], wkv[:, kc, :], start=(kc == 0), stop=(kc == 7))
                            evac(Vm[:, mz, (cb - 2) * 512:(cb - 1) * 512], ps)
                xt = ar.alloc((8, 512), F32)
                ysb = ar.alloc((16, 512), BF16)
                gtj = ar.alloc((2, 4, 512), BF16)
                acc = ar.alloc(512, F32)
                tmp = ar.alloc(512, F32)
                mg = ar.alloc((8, 512), BF16)
                xn = ar.alloc((8, 512), BF16)
                qT = mg
                oT = xn
                E = ar.alloc((2, 512), BF16)
                rsum = ar.alloc(512, F32)
                for tt in range(NT5):
                    cols = (tt * 512, (tt + 1) * 512)
                    P.dma(xt, xres_v(b, cols))
                    for m in range(4):
                        P.dma(ysb[:, m * 4:(m + 1) * 4, :], ys_v(m, b, cols))
                    for j in range(8):
                        gts = gtj[:, j % 2, :, :]
                        for m in range(4):
                            gv = proj_v(MC_GATE + m * 8 + j, b)
                            P.dma(gts[:, m, :], V(gv.ap[:, cols[0]:cols[1]], gv.t))
                        for m in range(4):
                            ps = P.bank()
                            for kc in range(4):
                                P.mm(ps, wbr[:, m * 4 + kc, j * 128:(j + 1) * 128], ysb[:, m * 4 + kc, :], start=(kc == 0), stop=(kc == 3))
                            if m == 0:
                                P.tt(acc, ps, gts[:, 0, :], ALU.mult)
                            else:
                                P.tt(tmp, ps, gts[:, m, :], ALU.mult)
                                P.tt(mg[:, j, :] if m == 3 else acc, acc, tmp, ALU.add, eng="pool")
                    for jo in range(8):
                        ps = P.bank()
                        for kc in range(8):
                            P.mm(ps, wmo[:, kc, jo * 128:(jo + 1) * 128], mg[:, kc, :], start=(kc == 0), stop=(kc == 7))
                        P.tt(xt[:, jo, :], xt[:, jo, :], ps, ALU.add)
                    rmsnorm_tile(xt, gx, xn, 512, sq, rt)
                    for jo in range(8):
                        ps = P.bank()
                        for kc in range(8):
                            P.mm(ps, wq[:, kc, jo * 128:(jo + 1) * 128], xn[:, kc, :], start=(kc == 0), stop=(kc == 7))
                        evac(qT[:, jo, :], ps)
                    for h in range(4):
                        for mz in range(2):
                            ps = P.bank()
                            for hc in range(2):
                                P.mm(ps, KT[:, 2 * h + hc, mz * 128:(mz + 1) * 128], qT[:, 2 * h + hc, :], start=(hc == 0), stop=(hc == 1))
                            P.act(E[:, mz, :], ps, AF.Exp, scale=1.0 / 16.0)
                        ps = P.bank()
                        for mz in range(2):
                            P.mm(ps, ones_b, E[:, mz, :], start=(mz == 0), stop=(mz == 1))
                        P.recip(rsum, ps)
                        for hc in range(2):
                            ps = P.bank()
                            for mz in range(2):
                                P.mm(ps, Vm[:, mz, (2 * h + hc) * 128:(2 * h + hc + 1) * 128], E[:, mz, :], start=(mz == 0), stop=(mz == 1))
                            P.tt(oT[:, 2 * h + hc, :], ps, rsum, ALU.mult)
                    for jo in range(8):
                        ps = P.bank()
                        for kc in range(8):
                            P.mm(ps, wo[:, kc, jo * 128:(jo + 1) * 128], oT[:, kc, :], start=(kc == 0), stop=(kc == 7))
                        P.tt(xt[:, jo, :], xt[:, jo, :], ps, ALU.add)
                    P.dma(xres_v(b, cols), xt)

        def phase_ffn():
            ar.reset()
            gf = ar.alloc(8, F32)
            P.dma(gf, wl["ffn_norm"])
            fcw = ar.alloc((44, 3), F32)
            fcb = ar.alloc(44, F32)
            P.dma(fcw, wl["ffn_cw"])
            P.dma(fcb, wl["ffn_cb"])
            xt = ar.alloc((8, 512), F32)
            hb = ar.alloc((8, 512), BF16)
            sq = ar.alloc((8, 512), BF16)
            rt = ar.alloc(512, F32)
            for b in range(2):
                for tt in range(NT5):
                    cols = (tt * 512, (tt + 1) * 512)
                    P.dma(xt, xres_v(b, cols))
                    rmsnorm_tile(xt, gf, hb, 512, sq, rt)
                    P.dma(DV(hffT[:, b * S + cols[0]: b * S + cols[1]].rearrange("(k p) t -> p k t", p=128), ("hff", b, tt)), hb)
            wup = ar.alloc((8, 22 * 128), BF16)
            wdn = ar.alloc((11, D), BF16)
            hp = ar.alloc((8, 512), BF16)
            hid = ar.alloc((11, 512), BF16)
            cg, cu = ar.alloc(512, F32), ar.alloc(512, F32)
            tiles = []
            t0 = 0
            while t0 < S:
                n = min(510, S - t0)
                tiles.append((t0, n))
                t0 += n
            for hf in range(2):
                for kc in range(8):
                    for part in range(2):
                        c0 = part * DFF + hf * 1408
                        for q3 in range(3):
                            w_ = 512 if q3 < 2 else 384
                            load_w(wup[:, kc, part * 1408 + q3 * 512: part * 1408 + q3 * 512 + w_],
                                   wl["ffn_wup"].ap[kc * 128:(kc + 1) * 128, c0 + q3 * 512: c0 + q3 * 512 + w_], wl["ffn_wup"].t)
                for kc in range(11):
                    for hh_ in range(2):
                        cs = slice(hh_ * 512, (hh_ + 1) * 512)
                        load_w(wdn[:, kc, cs], wl["ffn_wdown"].ap[hf * 1408 + kc * 128: hf * 1408 + (kc + 1) * 128, cs], wl["ffn_wdown"].t)
                for b in range(2):
                    hkeys = [("hff", b, tt) for tt in range(NT5)]
                    for (t0, n) in tiles:
                        lo, hi = max(t0 - 1, 0), min(t0 + n + 1, S)
                        off = lo - (t0 - 1)
                        nin = hi - lo
                        if off:
                            P.memset(hp[:, :, 0:1], 0.0)
                        if hi < t0 + n + 1:
                            P.memset(hp[:, :, n + 1:n + 2], 0.0)
                        P.dma(hp[:, :, off:off + nin], DV(hffT[:, b * S + lo: b * S + hi].rearrange("(k p) t -> p k t", p=128), *hkeys))
                        for fc in range(11):
                            pg_, pu_ = P.bank(), P.bank()
                            for kc in range(8):
                                P.mm(pg_[:, 0:n + 2], wup[:, kc, fc * 128:(fc + 1) * 128], hp[:, kc, 0:n + 2], start=(kc == 0), stop=(kc == 7))
                            for kc in range(8):
                                P.mm(pu_[:, 0:n + 2], wup[:, kc, 1408 + fc * 128: 1408 + (fc + 1) * 128], hp[:, kc, 0:n + 2], start=(kc == 0), stop=(kc == 7))
                            ig, iu = hf * 11 + fc, 22 + hf * 11 + fc
                            P.ts(cg[:, 0:n], pg_[:, 0:n], fcw[:, ig, 0:1], fcb[:, ig:ig + 1], ALU.mult, ALU.add)
                            P.stt(cg[:, 0:n], pg_[:, 1:n + 1], fcw[:, ig, 1:2], cg[:, 0:n], ALU.mult, ALU.add)
                            P.stt(cg[:, 0:n], pg_[:, 2:n + 2], fcw[:, ig, 2:3], cg[:, 0:n], ALU.mult, ALU.add)
                            P.ts(cu[:, 0:n], pu_[:, 0:n], fcw[:, iu, 0:1], fcb[:, iu:iu + 1], ALU.mult, ALU.add, eng="pool") if False else \
                                P.ts(cu[:, 0:n], pu_[:, 0:n], fcw[:, iu, 0:1], fcb[:, iu:iu + 1], ALU.mult, ALU.add)
                            P.stt(cu[:, 0:n], pu_[:, 1:n + 1], fcw[:, iu, 1:2], cu[:, 0:n], ALU.mult, ALU.add)
                            P.stt(cu[:, 0:n], pu_[:, 2:n + 2], fcw[:, iu, 2:3], cu[:, 0:n], ALU.mult, ALU.add)
                            P.act(cg[:, 0:n], cg[:, 0:n], AF.Silu)
                            P.tt(hid[:, fc, 0:n], cg[:, 0:n], cu[:, 0:n], ALU.mult)
                        P.dma(xt[:, :, 0:n], xres_v(b, (t0, t0 + n)))
                        for jo in range(8):
                            ps = P.bank()
                            for kc in range(11):
                                P.mm(ps[:, 0:n], wdn[:, kc, jo * 128:(jo + 1) * 128], hid[:, kc, 0:n], start=(kc == 0), stop=(kc == 10))
                            P.tt(xt[:, jo, 0:n], xt[:, jo, 0:n], ps[:, 0:n], ALU.add)
                        P.dma(xres_v(b, (t0, t0 + n)), xt[:, :, 0:n])

        for b in range(2):
            phase_A(b)
        for b in range(2):
            phase_gdn(b)
        phase_s5()
        phase_lru_sc()
        phase_merge_xa()
        phase_ffn()

    ar.reset()
    gfin = ar.alloc(8, F32)
    P.dma(gfin, fin_norm)
    xt = ar.alloc((8, 512), F32)
    yt = ar.alloc((8, 512), F32)
    sq = ar.alloc((8, 512), BF16)
    rt = ar.alloc(512, F32)
    toks = []
    for b in range(2):
        for tt in range(NT5):
            cols = (tt * 512, (tt + 1) * 512)
            P.dma(xt, xres_v(b, cols))
            P.act(sq, xt, AF.Square)
            ps = P.bank()
            for kc in range(8):
                P.mm(ps, ones_b, sq[:, kc, :], start=(kc == 0), stop=(kc == 7))
            P.act(rt, ps, AF.Sqrt, scale=1.0 / D, bias=eps_c)
            P.recip(rt, rt)
            for kc in range(8):
                P.stt(yt[:, kc, :], xt[:, kc, :], gfin[:, kc:kc + 1], rt, ALU.mult, ALU.mult)
            osl = slice(b * S + cols[0], b * S + cols[1])
            toks.append(P.dma(V(xoT.ap[:, osl].rearrange("(k p) t -> p k t", p=128), Trk()), xt))
            toks.append(P.dma(V(ynT.ap[:, osl].rearrange("(k p) t -> p k t", p=128), Trk()), yt))
    P.wait_all("sp", toks)
    P.build()
    return nc


def _pc(a, n):
    return np.ascontiguousarray(a.reshape(n, 128).T)


def prep_layer(inp, l):
    f = lambda k: np.asarray(inp[k][l], dtype=np.float32)
    o = {}
    o["mix_norm"] = _pc(f("mix_norm"), 8)
    o["w_in"] = f("w_in")
    o["gdn_conv"] = np.ascontiguousarray(f("gdn_conv").reshape(4, 12, 128).transpose(2, 1, 0))
    o["gdn_alog"] = f("gdn_a_log").reshape(1, 8)
    o["gdn_dtb"] = f("gdn_dt_bias").reshape(1, 8)
    o["gdn_gain"] = f("gdn_out_norm").reshape(1, 128)

    def st(a):
        return np.ascontiguousarray(a.reshape(2, 16, 2, 64).transpose(2, 3, 0, 1).reshape(128, 32))
    o["s5_lre"] = st(f("s5_lambda_re"))
    o["s5_lim"] = st(f("s5_lambda_im"))
    o["s5_lstep"] = st(np.repeat(f("s5_log_step")[:, :, None], 64, axis=2))

    def bt(a):
        out = np.zeros((2, 16, 128, 128), np.float32)
        for g in range(32):
            pair, g2 = g // 2, g % 2
            r0 = 16 * (g % 8)
            out[:, pair, r0:r0 + 16, g2 * 64:(g2 + 1) * 64] = a[:, g].transpose(0, 2, 1)
        return out.reshape(32, 128, 128)

    def ct(a):
        out = np.zeros((2, 16, 128, 128), np.float32)
        for g in range(32):
            pair, g2 = g // 2, g % 2
            c0 = 16 * (g % 8)
            out[:, pair, g2 * 64:(g2 + 1) * 64, c0:c0 + 16] = a[:, g].transpose(0, 2, 1)
        return out.reshape(32, 128, 128)
    o["s5_btre"], o["s5_btim"] = bt(f("s5_b_re")), bt(f("s5_b_im"))
    o["s5_ctre"], o["s5_ctim"] = ct(f("s5_c_re")), ct(f("s5_c_im"))
    o["s5_d"] = _pc(f("s5_d"), 4)
    o["s5_glu_w"] = f("s5_glu_w")
    o["s5_glu_b"] = _pc(f("s5_glu_b"), 4)
    o["lru_cw"] = np.ascontiguousarray(f("lru_conv_w").reshape(4, 4, 128).transpose(2, 1, 0))
    o["lru_cb"] = _pc(f("lru_conv_b"), 4)

    def bd(a):
        out = np.zeros((2, 4, 128, 128), np.float32)
        for n in range(8):
            c, bq = n // 2, n % 2
            out[:, c, bq * 64:(bq + 1) * 64, bq * 64:(bq + 1) * 64] = a[:, n]
        return out.reshape(8, 128, 128)
    o["lru_wa"], o["lru_wx"] = bd(f("lru_gate_a_w")), bd(f("lru_gate_x_w"))
    p2 = lambda a: np.ascontiguousarray(a.reshape(2, 4, 128).transpose(2, 0, 1).reshape(128, 8))
    o["lru_ba"], o["lru_bx"], o["lru_lam"] = p2(f("lru_gate_a_b")), p2(f("lru_gate_x_b")), p2(f("lru_lambda"))
    o["sc_cw"] = np.ascontiguousarray(f("sc_conv").reshape(3, 4, 128).transpose(2, 1, 0))
    o["w_branch"] = f("w_branch").reshape(4 * W, D)
    o["w_mix_out"] = f("w_mix_out")
    o["xa_norm"], o["xa_mnorm"] = _pc(f("xa_norm"), 8), _pc(f("xa_mem_norm"), 8)
    o["xa_wq"], o["xa_wkv"], o["xa_wo"] = f("xa_w_q"), f("xa_w_kv"), f("xa_w_o")
    o["ffn_norm"] = _pc(f("ffn_norm"), 8)
    o["ffn_wup"] = f("ffn_w_up")
    o["ffn_cw"] = np.ascontiguousarray(f("ffn_conv_w").reshape(3, 44, 128).transpose(2, 1, 0))
    o["ffn_cb"] = _pc(f("ffn_conv_b"), 44)
    o["ffn_wdown"] = f("ffn_w_down")
    return o


def consts():
    i = np.arange(128)
    c = {"c_ident": np.eye(128, dtype=np.float32)}
    uc = np.zeros((2, 128, 128), np.float32)
    uc[0] = (i[:, None] <= i[None, :])
    uc[1] = (i[:, None] >= i[None, :])
    ng = np.zeros((2, 128, 128), np.float32)
    ng[0] = np.where(i[None, :] > i[:, None], 0.0, -30000.0)
    ng[1] = np.where(i[None, :] < i[:, None], 0.0, -30000.0)
    c["c_ucum"], c["c_negm"] = uc, ng
    mk = np.zeros((7, 128, 128), np.float32)
    for lv in range(7):
        sz = 1 << lv
        mk[lv] = ((i[:, None] // (2 * sz)) == (i[None, :] // (2 * sz))) & ((i[:, None] // sz) != (i[None, :] // sz))
    c["c_msk"] = mk
    return c


_PROG = {}


DBG = []


def run_layers(xT_list, memT_list, inp, layers, S):
    NL = len(layers)
    key = (S, NL)
    if key not in _PROG:
        _PROG[key] = build_program(S, NL, dbg=bool(DBG))
    nc = _PROG[key]
    per = [prep_layer(inp, l) for l in layers]
    shared = {k: np.ascontiguousarray(np.stack([p[k] for p in per])) for k in per[0]}
    shared["fin_norm"] = _pc(np.asarray(inp["final_norm"], np.float32), 8)
    shared.update(consts())
    in_maps = []
    for c in range(len(xT_list)):
        m = dict(shared)
        m["xT"] = xT_list[c]
        m["memT"] = memT_list[c]
        in_maps.append(m)
    res = run_bass_kernel_spmd(nc, in_maps, core_ids=list(range(len(xT_list))))
    if DBG:
        DBG.append(res.results)
    return [r["xoT"] for r in res.results], [r["ynT"] for r in res.results]


def kernel(**inp):
    x = np.asarray(inp["x"], np.float32)
    mem = np.asarray(inp["mem"], np.float32)
    B, S, _ = x.shape
    depth = inp["w_in"].shape[0]
    nco = B // 2
    xT = [np.ascontiguousarray(x[2 * c:2 * c + 2].reshape(2 * S, D).T) for c in range(nco)]
    mT = [np.ascontiguousarray(mem[2 * c:2 * c + 2].reshape(2 * MEM, D).T) for c in range(nco)]
    yn = None
    for l in range(depth):
        xT, yn = run_layers(xT, mT, inp, [l], S)
    out = np.empty((B, S, D), np.float32)
    for c in range(nco):
        out[2 * c:2 * c + 2] = yn[c].T.reshape(2, S, D)
    return out
```

```python
import contextlib
import math
import numpy as np
import concourse.bass as bass
import concourse.mybir as mybir
from concourse.bass_utils import run_bass_kernel_spmd

F32 = mybir.dt.float32
BF16 = mybir.dt.bfloat16
AF = mybir.ActivationFunctionType
ALU = mybir.AluOpType

D = 1024
W = 512
MEM = 256
DFF = 2816
INW = 9232
EPS = 1e-6
NCORES = 8


class Trk:
    __slots__ = ("w", "r")

    def __init__(self):
        self.w = None
        self.r = []


def _trks(v):
    t = v.t
    return t if isinstance(t, (list, tuple)) else (t,)


class V:
    __slots__ = ("ap", "t")

    def __init__(self, ap, t):
        self.ap = ap
        self.t = t

    def __getitem__(self, k):
        return V(self.ap[k], self.t)

    def rr(self, pat, **kw):
        return V(self.ap.rearrange(pat, **kw), self.t)

    def bc(self, shape):
        return V(self.ap.broadcast_to(list(shape)), self.t)


class Prog:
    CE = ("pe", "act", "dve", "pool")
    NSLOT = 8

    def __init__(self, nc):
        self.nc = nc
        self.es = contextlib.ExitStack()
        self.q = {e: [] for e in ("pe", "act", "dve", "pool", "sp")}
        self.cnt = {e: 0 for e in self.CE}
        self.sem = {e: self.es.enter_context(nc.semaphore("s_" + e)) for e in self.CE}
        self.known = {e: {} for e in self.q}
        self.dsem = {}
        self.dcnt = {}
        for qn in ("sp", "pool"):
            self.dsem[qn] = [self.es.enter_context(nc.semaphore("d_%s%d" % (qn, i))) for i in range(self.NSLOT)]
            self.dcnt[qn] = 0
        self.nuid = 0
        self.nbank = 0
        self.banks = []

    def sb(self, shape, dt, name=None):
        self.nuid += 1
        h = self.es.enter_context(self.nc.sbuf_tensor(name or ("sb%d" % self.nuid), list(shape), dt))
        return V(h[:], Trk())

    def dram(self, name, shape, dt, kind="Internal"):
        h = self.nc.dram_tensor(name, list(shape), dt, kind=kind)
        return V(h.ap(), Trk())

    def init_banks(self):
        for i in range(8):
            h = self.es.enter_context(self.nc.psum_tensor("bank%d" % i, [128, 512], F32))
            self.banks.append(V(h[:], Trk()))

    def bank(self):
        b = self.banks[self.nbank % 8]
        self.nbank += 1
        return b

    def bankbf(self):
        b = self.bank()
        return V(b.ap.bitcast(BF16), b.t)

    def _deps(self, reads, writes):
        deps = []
        for v in reads:
            for t in _trks(v):
                if t.w is not None:
                    deps.append(t.w)
        for v in writes:
            for t in _trks(v):
                if t.w is not None:
                    deps.append(t.w)
                deps.extend(t.r)
        return deps

    def _waits(self, eng, deps, raw_same, always=False):
        kn = self.known[eng]
        need = {}
        for tok in deps:
            key, val, src = tok
            if key == eng and not always:
                if eng == "pe":
                    continue
                if tok not in raw_same or val < self.cnt[eng] - 1:
                    continue
            if kn.get(key, 0) >= val:
                continue
            if need.get(key, 0) < val:
                need[key] = val
        out = []
        for key, val in need.items():
            kn[key] = val
            out.append((self._semof(key), val))
        return out

    def _semof(self, key):
        if isinstance(key, str):
            return self.sem[key]
        return self.dsem[key[0]][key[1]]

    def op(self, eng, name, args, kw, reads, writes):
        deps = self._deps(reads, writes)
        raw = set()
        for v in reads:
            for t in _trks(v):
                if t.w is not None:
                    raw.add(t.w)
        waits = self._waits(eng, deps, raw)
        self.cnt[eng] += 1
        tok = (eng, self.cnt[eng], eng)
        self.q[eng].append((waits, name, args, kw, (self.sem[eng], 1)))
        for v in reads:
            for t in _trks(v):
                t.r.append(tok)
        for v in writes:
            for t in _trks(v):
                t.w = tok
                t.r = []
        return tok

    def dma(self, out, in_, qn="sp", **kw):
        deps = self._deps([in_], [out])
        n = self.dcnt[qn]
        slot = n % self.NSLOT
        gen = n // self.NSLOT
        self.dcnt[qn] += 1
        key = (qn, slot)
        waits = self._waits(qn, deps, set(), always=True)
        if gen > 0 and self.known[qn].get(key, 0) < 16 * gen:
            self.known[qn][key] = 16 * gen
            waits.append((self.dsem[qn][slot], 16 * gen))
        tok = (key, 16 * (gen + 1), qn)
        self.q[qn].append((waits, "dma_start", (), dict(out=out.ap, in_=in_.ap, **kw), (self.dsem[qn][slot], 16)))
        for t in _trks(in_):
            t.r.append(tok)
        for t in _trks(out):
            t.w = tok
            t.r = []
        return tok

    def barrier(self):
        toks = [(e, self.cnt[e], e) for e in self.CE if self.cnt[e] > 0]
        for qn in self.dsem:
            n = self.dcnt[qn]
            for s in range(self.NSLOT):
                k = (n - s + self.NSLOT - 1) // self.NSLOT
                if k > 0:
                    toks.append(((qn, s), 16 * k, qn))
        for eng in self.q:
            waits = self._waits(eng, toks, set(), always=True)
            if waits:
                self.q[eng].append((waits, None, (), {}, None))

    @staticmethod
    def _a(x):
        return x.ap if isinstance(x, V) else x

    def mm(self, out, lhsT, rhs, start=True, stop=True):
        rd = [lhsT, rhs] + ([] if start else [out])
        return self.op("pe", "matmul", (out.ap, lhsT.ap, rhs.ap), dict(start=start, stop=stop), rd, [out])

    def tr(self, out, in_, ident):
        return self.op("pe", "transpose", (out.ap, in_.ap, ident.ap), {}, [in_, ident], [out])

    def act(self, out, in_, func, scale=1.0, bias=0.0, accum_out=None):
        rd = [in_] + [x for x in (scale, bias) if isinstance(x, V)]
        wr = [out] + ([accum_out] if accum_out is not None else [])
        kw = dict(scale=self._a(scale), bias=self._a(bias))
        if accum_out is not None:
            kw["accum_out"] = accum_out.ap
        return self.op("act", "activation", (out.ap, in_.ap, func), kw, rd, wr)

    def ts(self, out, in0, s1, s2, op0, op1=None, eng="dve"):
        rd = [in0] + [x for x in (s1, s2) if isinstance(x, V)]
        kw = {}
        if op1 is not None:
            kw["op1"] = op1
        return self.op(eng, "tensor_scalar", (out.ap, in0.ap, self._a(s1), self._a(s2), op0), kw, rd, [out])

    def tt(self, out, in0, in1, op, eng="dve"):
        return self.op(eng, "tensor_tensor", (out.ap, in0.ap, in1.ap, op), {}, [in0, in1], [out])

    def stt(self, out, in0, s, in1, op0, op1):
        rd = [in0, in1] + ([s] if isinstance(s, V) else [])
        return self.op("dve", "scalar_tensor_tensor", (out.ap, in0.ap, self._a(s), in1.ap, op0, op1), {}, rd, [out])

    def cp(self, out, in_, eng="dve"):
        if eng == "act":
            return self.op("act", "activation", (out.ap, in_.ap, AF.Identity), {}, [in_], [out])
        return self.op(eng, "tensor_copy", (out.ap, in_.ap), {}, [in_], [out])

    def scan(self, out, d0, d1, init):
        rd = [d0, d1] + ([init] if isinstance(init, V) else [])
        return self.op("dve", "tensor_tensor_scan", (out.ap, d0.ap, d1.ap, self._a(init), ALU.mult, ALU.add), {}, rd, [out])

    def memset(self, out, val, eng="dve"):
        return self.op(eng, "memset", (out.ap, val), {}, [], [out])

    def recip(self, out, in_):
        return self.op("dve", "reciprocal", (out.ap, in_.ap), {}, [in_], [out])

    def wait_all(self, eng, toks):
        waits = self._waits(eng, toks, set(), always=True)
        self.q[eng].append((waits, None, (), {}, None))

    def build(self):
        nc = self.nc
        q = self.q

        def replay(e, ops):
            for waits, name, args, kw, inc in ops:
                for sem, val in waits:
                    e.wait_ge(sem, val)
                if name is None:
                    continue
                ins = getattr(e, name)(*args, **kw)
                if inc is not None:
                    ins.then_inc(inc[0], inc[1])

        with nc.Block() as block:
            @block.tensor
            def _(e):
                replay(e, q["pe"])

            @block.scalar
            def _(e):
                replay(e, q["act"])

            @block.vector
            def _(e):
                replay(e, q["dve"])

            @block.gpsimd
            def _(e):
                replay(e, q["pool"])

            @block.sync
            def _(e):
                replay(e, q["sp"])
        self.es.close()


class Arena:
    def __init__(self, P, nelem):
        self.P = P
        self.n = nelem
        self.h = P.es.enter_context(P.nc.sbuf_tensor("arena", [128, nelem], BF16))
        self.off = 0

    def reset(self, to=0):
        self.P.barrier()
        self.off = to

    def alloc(self, shape, dt):
        if isinstance(shape, int):
            shape = (shape,)
        n = int(np.prod(shape))
        four = dt in (F32, mybir.dt.int32)
        ne = n * 2 if four else n
        ne = (ne + 1) // 2 * 2
        assert self.off + ne <= self.n, "arena overflow %d + %d > %d" % (self.off, ne, self.n)
        ap = self.h[:, self.off:self.off + ne]
        self.off += ne
        if four:
            ap = ap.bitcast(dt)
        if n != (ne // 2 if four else ne):
            ap = ap[:, 0:n]
        if len(shape) == 2:
            ap = ap.rearrange("p (a b) -> p a b", a=shape[0])
        elif len(shape) == 3:
            ap = ap.rearrange("p (a b c) -> p a b c", a=shape[0], b=shape[1])
        return V(ap, Trk())


MCH = [(128 * i, 128) for i in range(16)] + [(2048, 16)] + [(2064 + 128 * i, 128) for i in range(56)]
MC_S5U, MC_LRUX, MC_LRUG, MC_SCB, MC_SCC, MC_SCX, MC_GATE = 17, 21, 25, 29, 33, 37, 41

WSPEC = [
    ("mix_norm", (128, 8)), ("w_in", (D, INW)), ("gdn_conv", (128, 12, 4)), ("gdn_alog", (1, 8)),
    ("gdn_dtb", (1, 8)), ("gdn_gain", (1, 128)),
    ("s5_lre", (128, 32)), ("s5_lim", (128, 32)), ("s5_lstep", (128, 32)),
    ("s5_btre", (32, 128, 128)), ("s5_btim", (32, 128, 128)), ("s5_ctre", (32, 128, 128)), ("s5_ctim", (32, 128, 128)),
    ("s5_d", (128, 4)), ("s5_glu_w", (W, W)), ("s5_glu_b", (128, 4)),
    ("lru_cw", (128, 4, 4)), ("lru_cb", (128, 4)), ("lru_wa", (8, 128, 128)), ("lru_wx", (8, 128, 128)),
    ("lru_ba", (128, 8)), ("lru_bx", (128, 8)), ("lru_lam", (128, 8)),
    ("sc_cw", (128, 4, 3)), ("w_branch", (4 * W, D)), ("w_mix_out", (D, D)),
    ("xa_norm", (128, 8)), ("xa_mnorm", (128, 8)), ("xa_wq", (D, D)), ("xa_wkv", (D, 2 * D)), ("xa_wo", (D, D)),
    ("ffn_norm", (128, 8)), ("ffn_wup", (D, 2 * DFF)), ("ffn_cw", (128, 44, 3)), ("ffn_cb", (128, 44)),
    ("ffn_wdown", (DFF, D)),
]


def build_program(S, NL, dbg=()):
    T = 2 * S
    NT5 = S // 512
    NT1 = S // 128
    LOGS = int(round(math.log2(S)))
    nc = bass.Bass("TRN2", target_bir_lowering=False)
    P = Prog(nc)
    P.init_banks()
    xT = P.dram("xT", [D, T], F32, kind="ExternalInput")
    memT = P.dram("memT", [D, 2 * MEM], F32, kind="ExternalInput")
    wts = {}
    for name, shp in WSPEC:
        wts[name] = P.dram(name, [NL] + list(shp), F32, kind="ExternalInput")
    fin_norm = P.dram("fin_norm", [128, 8], F32, kind="ExternalInput")
    c_ident = P.dram("c_ident", [128, 128], F32, kind="ExternalInput")
    c_ucum = P.dram("c_ucum", [2, 128, 128], F32, kind="ExternalInput")
    c_negm = P.dram("c_negm", [2, 128, 128], F32, kind="ExternalInput")
    c_msk = P.dram("c_msk", [7, 128, 128], F32, kind="ExternalInput")
    xoT = P.dram("xoT", [D, T], F32, kind="ExternalOutput")
    ynT = P.dram("ynT", [D, T], F32, kind="ExternalOutput")
    xres = nc.dram_tensor("xres", [D, T], F32, kind="Internal").ap()
    projT = nc.dram_tensor("projT", [73 * 128, T], BF16, kind="Internal").ap()
    baTok = nc.dram_tensor("baTok", [T, 16], F32, kind="Internal").ap()
    ofwd = nc.dram_tensor("ofwd", [T, 512], F32, kind="Internal").ap()
    ysT = nc.dram_tensor("ysT", [4 * W, T], BF16, kind="ExternalOutput" if dbg else "Internal").ap()
    hffT = nc.dram_tensor("hffT", [D, T], BF16, kind="Internal").ap()
    dtrk = {}

    def DV(ap, *keys):
        ts_ = []
        for k in keys:
            if k not in dtrk:
                dtrk[k] = Trk()
            ts_.append(dtrk[k])
        return V(ap, ts_)

    def xres_v(b, cols=None):
        ap = xres[:, b * S:(b + 1) * S] if cols is None else xres[:, b * S + cols[0]: b * S + cols[1]]
        return DV(ap.rearrange("(k p) t -> p k t", p=128), ("xres", b))

    ident_f = P.sb([128, 128], F32)
    ident_b = P.sb([128, 128], BF16)
    ones_f = P.sb([128, 128], F32)
    ones_b = P.sb([128, 128], BF16)
    identf4 = P.sb([128, 4, 128], F32)
    ucum = P.sb([128, 2, 128], F32)
    negm = P.sb([128, 2, 128], F32)
    eps_c = P.sb([128, 1], F32)
    one_c = P.sb([128, 1], F32)
    P.dma(ident_f, c_ident)
    P.dma(ucum, c_ucum.rr("d p i -> p d i"))
    P.dma(negm, c_negm.rr("d p i -> p d i"))
    mskf = P.sb([128, 7, 128], F32)
    msk = P.sb([128, 7, 128], BF16)
    P.dma(mskf, c_msk.rr("d p i -> p d i"))
    P.cp(msk, mskf)
    P.cp(ident_b, ident_f)
    P.memset(ones_f, 1.0)
    P.memset(ones_b, 1.0)
    P.memset(eps_c, EPS)
    P.memset(one_c, 1.0)
    for h in range(4):
        P.cp(identf4[:, h, :], ident_f)
    rem = nc.sbuf_bytes_remaining
    rem = rem() if callable(rem) else rem
    ar = Arena(P, (int(rem) - 4096) // 2 // 2 * 2)

    for b in range(2):
        P.dma(DV(xres[:, b * S:(b + 1) * S], ("xres", b)), V(xT.ap[:, b * S:(b + 1) * S], xT.t))

    rr_ = [0]

    def evac(out, in_):
        rr_[0] += 1
        P.cp(out, in_, eng="act" if rr_[0] % 2 else "dve")

    def load_w(dst, src_ap, trk):
        P.dma(dst, V(src_ap, trk), qn="pool")

    def rmsnorm_tile(xt, gain, dst, n, tmp_sq, tmp_r):
        P.act(tmp_sq[:, :, 0:n], xt, AF.Square)
        ps = P.bank()
        for kc in range(8):
            P.mm(ps[:, 0:n], ones_b, tmp_sq[:, kc, 0:n], start=(kc == 0), stop=(kc == 7))
        P.act(tmp_r[:, 0:n], ps[:, 0:n], AF.Sqrt, scale=1.0 / D, bias=eps_c)
        P.recip(tmp_r[:, 0:n], tmp_r[:, 0:n])
        for kc in range(8):
            P.stt(dst[:, kc, :], xt[:, kc, :], gain[:, kc:kc + 1], tmp_r[:, 0:n], ALU.mult, ALU.mult)

    for l in range(NL):
        wl = {k: V(v.ap[l], v.t) for k, v in wts.items()}

        def phase_A(b):
            ar.reset()
            hT = ar.alloc((8, S), BF16)
            gain = ar.alloc(8, F32)
            P.dma(gain, wl["mix_norm"])
            xb = [ar.alloc((8, 512), F32) for _ in range(2)]
            sq = ar.alloc((8, 512), BF16)
            rt = ar.alloc(512, F32)
            for tt in range(NT5):
                xt = xb[tt % 2]
                P.dma(xt, xres_v(b, (tt * 512, tt * 512 + 512)))
                rmsnorm_tile(xt, gain, hT[:, :, tt * 512:(tt + 1) * 512], 512, sq, rt)
            wb = [ar.alloc((8, 512), BF16) for _ in range(2)]
            st = [ar.alloc(512, BF16) for _ in range(4)]
            wba = ar.alloc((8, 16), BF16)
            batok = ar.alloc((NT1, 16), F32)
            for kc in range(8):
                load_w(wba[:, kc, :], wl["w_in"].ap[kc * 128:(kc + 1) * 128, 2048:2064], wl["w_in"].t)
            for n in range(NT1):
                ps = P.bank()
                for kc in range(8):
                    P.mm(ps[:, 0:16], hT[:, kc, n * 128:(n + 1) * 128], wba[:, kc, :], start=(kc == 0), stop=(kc == 7))
                evac(batok[:, n, :], ps[:, 0:16])
            for n0 in range(0, NT1, 8):
                n1 = min(n0 + 8, NT1)
                P.dma(DV(baTok[b * S + n0 * 128: b * S + n1 * 128, :].rearrange("(n p) c -> p n c", p=128), ("ba", b, n0)), batok[:, n0:n1, :])
            groups = [list(range(g * 4, g * 4 + 4)) for g in range(4)] + [list(range(17 + g * 4, 17 + g * 4 + 4)) for g in range(14)]
            si = 0
            for gi, grp in enumerate(groups):
                wt = wb[gi % 2]
                c0 = MCH[grp[0]][0]
                for kc in range(8):
                    load_w(wt[:, kc, :], wl["w_in"].ap[kc * 128:(kc + 1) * 128, c0:c0 + 512], wl["w_in"].t)
                for tt in range(NT5):
                    for j, mc in enumerate(grp):
                        ps = P.bank()
                        for kc in range(8):
                            P.mm(ps, wt[:, kc, j * 128:(j + 1) * 128], hT[:, kc, tt * 512:(tt + 1) * 512],
                                 start=(kc == 0), stop=(kc == 7))
                        s_ = st[si % 4]
                        si += 1
                        if mc >= MC_GATE:
                            P.act(s_, ps, AF.Sigmoid)
                        else:
                            evac(s_, ps)
                        P.dma(DV(projT[mc * 128:(mc + 1) * 128, b * S + tt * 512: b * S + (tt + 1) * 512], ("proj", mc, b, tt)), s_)

        def proj_v(mc, b, n=1):
            keys = [("proj", mc + i, b, tt) for i in range(n) for tt in range(NT5)]
            return DV(projT[mc * 128:(mc + n) * 128, b * S:(b + 1) * S], *keys)

        def phase_gdn(b):
            ar.reset()
            qn = ar.alloc((4, S), BF16)
            kn = ar.alloc((4, S), BF16)
            vv = ar.alloc((4, S), BF16)
            cw = ar.alloc((12, 4), F32)
            P.dma(cw, wl["gdn_conv"])
            mark = ar.off
            xp = ar.alloc(S + 4, BF16)
            acc = ar.alloc(S, F32)
            sqb = ar.alloc(S, BF16)
            rtf = ar.alloc(S, F32)
            P.memset(xp[:, 0:2], 0.0)
            P.memset(xp[:, S + 2:S + 4], 0.0)
            for c in range(12):
                which, hd = c // 4, c % 4
                P.dma(xp[:, 2:S + 2], proj_v(c, b))
                P.ts(acc, xp[:, 0:S], cw[:, c, 0:1], None, ALU.mult)
                for k in range(1, 4):
                    P.stt(acc, xp[:, k:k + S], cw[:, c, k:k + 1], acc, ALU.mult, ALU.add)
                if which == 2:
                    P.act(vv[:, hd, :], acc, AF.Silu)
                    continue
                P.act(acc, acc, AF.Silu)
                P.act(sqb, acc, AF.Square)
                for tt in range(NT5):
                    ps = P.bank()
                    P.mm(ps, ones_b, sqb[:, tt * 512:(tt + 1) * 512])
                    P.act(rtf[:, tt * 512:(tt + 1) * 512], ps, AF.Sqrt, bias=eps_c)
                P.recip(rtf, rtf)
                P.tt((qn if which == 0 else kn)[:, hd, :], acc, rtf, ALU.mult)
            ar.reset(mark)
            bt = ar.alloc((NT1, 16), F32)
            for n0 in range(0, NT1, 8):
                n1 = min(n0 + 8, NT1)
                P.dma(bt[:, n0:n1, :], DV(baTok[b * S + n0 * 128: b * S + n1 * 128, :].rearrange("(n p) c -> p n c", p=128), ("ba", b, n0)))
            beta = ar.alloc((NT1, 8), F32)
            gtk = ar.alloc((NT1, 8), F32)
            dtb = ar.alloc(8, F32)
            nega = ar.alloc(8, F32)
            gainrow = ar.alloc(128, F32)
            P.dma(dtb, V(wl["gdn_dtb"].ap.partition_broadcast(128), wl["gdn_dtb"].t))
            P.dma(nega, V(wl["gdn_alog"].ap.partition_broadcast(128), wl["gdn_alog"].t))
            P.dma(gainrow, V(wl["gdn_gain"].ap.partition_broadcast(128), wl["gdn_gain"].t))
            P.act(nega, nega, AF.Exp)
            P.ts(nega, nega, -1.0, None, ALU.mult)
            P.act(beta, bt[:, :, 0:8], AF.Sigmoid)
            P.tt(gtk, bt[:, :, 8:16], dtb.rr("p (o c) -> p o c", o=1).bc([128, NT1, 8]), ALU.add)
            P.act(gtk, gtk, AF.Exp)
            P.act(gtk, gtk, AF.Ln, bias=one_c)
            P.tt(gtk, gtk, nega.rr("p (o c) -> p o c", o=1).bc([128, NT1, 8]), ALU.mult)
            A = lambda shp, dt: ar.alloc(shp, dt)
            ktok, vtok = A((4, 128), BF16), A((4, 128), BF16)
            gam, ngam, eg, negeg, egs, dl, egl, gl = [A(4, F32) for _ in range(8)]
            gbs = A((4, 128), F32)
            Ds = A((4, 128), F32)
            Di = A((4, 128), F32)
            Y = A((4, 128), BF16)
            qkT = A((4, 128), BF16)
            Zs = A((7, 4 * 128), BF16)
            Zb = [A((4, 128), BF16) for _ in range(1)]
            Xb = [A((4, 128), BF16) for _ in range(2)]
            Wt, T1 = A((4, 128), BF16), A((4, 128), BF16)
            xx, vn, kdec = A((4, 128), BF16), A((4, 128), BF16), A((4, 128), BF16)
            Bs, o_ = A((4, 128), F32), A((4, 128), F32)
            Sm, Sb = A((4, 128), F32), A((4, 128), BF16)
            of_, os_, junk = A((4, 128), F32), A((4, 128), F32), A(128, F32)
            ss, rs = A(4, F32), A(4, F32)
            on = A((4, 128), BF16)
            zt = A((4, 128), BF16)
            sz = A((4, 128), F32)
            y0 = A((4, 128), BF16)
            sdk = 128.0 ** -0.5

            def h4(v):
                return v.rr("p (h d) -> p h d", h=4)

            for d in range(2):
                P.memset(Sm, 0.0)
                P.memset(Sb, 0.0)
                order = range(NT1) if d == 0 else range(NT1 - 1, -1, -1)
                for n in order:
                    sl = slice(n * 128, (n + 1) * 128)
                    pk = h4(P.bankbf()[:, 0:512])
                    for h in range(4):
                        P.tr(pk[:, h, :], kn[:, h, sl], ident_b)
                    P.cp(ktok, pk, eng="act")
                    pv = h4(P.bankbf()[:, 0:512])
                    for h in range(4):
                        P.tr(pv[:, h, :], vv[:, h, sl], ident_b)
                    P.cp(vtok, pv)
                    gcol = gtk[:, n, d * 4:(d + 1) * 4]
                    bcol = beta[:, n, d * 4:(d + 1) * 4]
                    pg = P.bank()
                    P.mm(pg[:, 0:4], ucum[:, d, :], gcol)
                    P.mm(pg[:, 4:8], ones_f, gcol)
                    P.cp(gam, pg[:, 0:4])
                    P.ts(ngam, pg[:, 0:4], -1.0, None, ALU.mult)
                    P.act(eg, pg[:, 0:4], AF.Exp)
                    P.ts(negeg, eg, -1.0, None, ALU.mult)
                    P.ts(egs, eg, sdk, None, ALU.mult)
                    P.tt(dl, pg[:, 4:8], gam, ALU.subtract)
                    P.act(egl, dl, AF.Exp)
                    P.act(gl, pg[:, 4:8], AF.Exp)
                    pd = h4(P.bank())
                    for h in range(4):
                        P.ts(gbs[:, h, :], ones_f, gcol[:, h:h + 1], None, ALU.mult)
                        P.mm(pd[:, h, :], gbs[:, h, :], ucum[:, d, :], start=True, stop=False)
                        P.mm(pd[:, h, :], ident_f, negm[:, d, :], start=False, stop=True)
                    for h in range(4):
                        P.act(Ds[:, h, :], pd[:, h, :], AF.Exp, bias=ngam[:, h:h + 1])
                    pkk = h4(P.bank())
                    pqk = h4(P.bank())
                    for h in range(4):
                        P.mm(pkk[:, h, :], kn[:, h, sl], kn[:, h, sl])
                    for h in range(4):
                        P.mm(pqk[:, h, :], kn[:, h, sl], qn[:, h, sl])
                    for h in range(4):
                        P.stt(Y[:, h, :], pkk[:, h, :], bcol[:, h:h + 1], Ds[:, h, :], ALU.mult, ALU.mult)
                    P.tt(Di, Ds, identf4, ALU.add)
                    P.stt(qkT, pqk, sdk, Di, ALU.mult, ALU.mult)
                    pz = h4(P.bankbf()[:, 0:512])
                    for h in range(4):
                        P.tr(pz[:, h, :], Y[:, h, :], ident_b)
                    Z = Zb[0]
                    P.cp(Z, pz, eng="act")
                    for lv in range(1, 7):
                        mb = V(msk.ap[:, lv:lv + 1, :].broadcast_to([128, 4, 128]), msk.t)
                        P.tt(Zs[:, lv, :].rr("p (h d) -> p h d", h=4), Z, mb, ALU.mult, eng="pool")
                    m0 = V(msk.ap[:, 0:1, :].broadcast_to([128, 4, 128]), msk.t)
                    P.tt(T1, Y, m0, ALU.mult)
                    X = Xb[0]
                    P.tt(X, identf4, T1, ALU.subtract)
                    for lv in range(1, 7):
                        Xn = Xb[lv % 2]
                        pW = h4(P.bankbf()[:, 0:512])
                        for h in range(4):
                            P.tr(pW[:, h, :], X[:, h, :], ident_b)
                        P.cp(Wt, pW, eng="act")
                        p1 = h4(P.bank())
                        for h in range(4):
                            P.mm(p1[:, h, :], Zs[:, lv, h * 128:(h + 1) * 128], X[:, h, :])
                        P.cp(T1, p1)
                        p2 = h4(P.bank())
                        for h in range(4):
                            P.mm(p2[:, h, :], Wt[:, h, :], T1[:, h, :])
                        P.tt(Xn, X, p2, ALU.subtract)
                        X = Xn
                    pks = h4(P.bank())
                    for h in range(4):
                        P.mm(pks[:, h, :], kn[:, h, sl], Sb[:, h, :])
                    for h in range(4):
                        P.stt(xx[:, h, :], pks[:, h, :], negeg[:, h:h + 1], vtok[:, h, :], ALU.mult, ALU.add)
                    pvn = h4(P.bank())
                    for h in range(4):
                        P.mm(pvn[:, h, :], X[:, h, :], xx[:, h, :])
                    for h in range(4):
                        P.act(vn[:, h, :], pvn[:, h, :], AF.Identity, scale=bcol[:, h:h + 1])
                    pA = h4(P.bank())
                    for h in range(4):
                        P.mm(pA[:, h, :], qn[:, h, sl], Sb[:, h, :])
                    pB = h4(P.bank())
                    for h in range(4):
                        P.mm(pB[:, h, :], qkT[:, h, :], vn[:, h, :])
                    P.cp(Bs, pB, eng="act")
                    for h in range(4):
                        P.stt(o_[:, h, :], pA[:, h, :], egs[:, h:h + 1], Bs[:, h, :], ALU.mult, ALU.add)
                    for h in range(4):
                        P.act(kdec[:, h, :], ktok[:, h, :], AF.Identity, scale=egl[:, h:h + 1])
                    pS = h4(P.bank())
                    for h in range(4):
                        P.mm(pS[:, h, :], kdec[:, h, :], vn[:, h, :])
                    for h in range(4):
                        P.stt(Sm[:, h, :], Sm[:, h, :], gl[:, h:h + 1], pS[:, h, :], ALU.mult, ALU.add)
                    P.cp(Sb, Sm, eng="act")
                    ofv = DV(ofwd[b * S + n * 128: b * S + (n + 1) * 128, :].rearrange("p (h d) -> p h d", h=4), ("ofwd", b, n))
                    if d == 0:
                        P.dma(ofv, o_)
                        continue
                    P.dma(of_, ofv)
                    P.tt(os_, o_, of_, ALU.add)
                    for h in range(4):
                        P.act(junk, os_[:, h, :], AF.Square, accum_out=ss[:, h:h + 1])
                    P.act(rs, ss, AF.Sqrt, scale=1.0 / 128, bias=eps_c)
                    P.recip(rs, rs)
                    for h in range(4):
                        P.stt(on[:, h, :], os_[:, h, :], rs[:, h:h + 1], gainrow, ALU.mult, ALU.mult)
                    pt = h4(P.bankbf()[:, 0:512])
                    for h in range(4):
                        P.tr(pt[:, h, :], on[:, h, :], ident_b)
                    zv = proj_v(12, b, 4)
                    P.dma(zt, V(zv.ap[:, n * 128:(n + 1) * 128].rearrange("(h p) t -> p h t", p=128), zv.t))
                    P.act(sz, zt, AF.Silu)
                    P.tt(y0, pt, sz, ALU.mult)
                    P.dma(DV(ysT[0:512, b * S + n * 128: b * S + (n + 1) * 128].rearrange("(h p) t -> p h t", p=128), ("ys", 0, b, n)), y0)

        def s5_tables():
            lre, lim, lst = [ar.alloc(32, F32) for _ in range(3)]
            P.dma(lre, wl["s5_lre"])
            P.dma(lim, wl["s5_lim"])
            P.dma(lst, wl["s5_lstep"])
            t = [ar.alloc(32, F32) for _ in range(8)]
            ti = ar.alloc(32, mybir.dt.int32)
            fre, fim, nfim = [ar.alloc(32, F32) for _ in range(3)]
            pw = ar.alloc((LOGS, 3, 32), F32)
            P.act(lst, lst, AF.Exp)
            P.ts(lre, lre, -1e-4, None, ALU.min)
            mag, ph = t[0], t[1]
            P.tt(mag, lre, lst, ALU.mult)
            P.act(mag, mag, AF.Exp)
            P.tt(ph, lim, lst, ALU.mult)

            def sin_of(dst, src, shift):
                u, r, m = t[5], t[6], t[7]
                P.ts(u, src, shift, 1.0 / (2 * math.pi), ALU.add, ALU.mult)
                P.cp(ti, u)
                P.cp(r, ti)
                P.tt(r, u, r, ALU.subtract)
                P.ts(m, r, 0.5, None, ALU.is_gt)
                P.tt(r, r, m, ALU.subtract)
                P.ts(m, r, -0.5, None, ALU.is_lt)
                P.tt(r, r, m, ALU.add)
                P.act(dst, r, AF.Sin, scale=2 * math.pi)

            cs, sn = t[2], t[3]
            sin_of(sn, ph, 0.0)
            sin_of(cs, ph, math.pi / 2)
            are, aim = pw[:, 0, 0, :], pw[:, 0, 1, :]
            P.tt(are, mag, cs, ALU.mult)
            P.tt(aim, mag, sn, ALU.mult)
            den, am1, x1, x2 = t[0], t[1], t[2], t[3]
            P.tt(den, lre, lre, ALU.mult)
            P.tt(x1, lim, lim, ALU.mult)
            P.tt(den, den, x1, ALU.add)
            P.recip(den, den)
            P.ts(am1, are, -1.0, None, ALU.add)
            P.tt(x1, am1, lre, ALU.mult)
            P.tt(x2, aim, lim, ALU.mult)
            P.tt(x1, x1, x2, ALU.add)
            P.tt(fre, x1, den, ALU.mult)
            P.tt(x1, aim, lre, ALU.mult)
            P.tt(x2, am1, lim, ALU.mult)
            P.tt(x1, x1, x2, ALU.subtract)
            P.tt(fim, x1, den, ALU.mult)
            P.ts(nfim, fim, -1.0, None, ALU.mult)
            for k in range(LOGS):
                re_, im_ = pw[:, k, 0, :], pw[:, k, 1, :]
                P.ts(pw[:, k, 2, :], im_, -1.0, None, ALU.mult)
                if k + 1 < LOGS:
                    P.tt(x1, re_, re_, ALU.mult)
                    P.tt(x2, im_, im_, ALU.mult)
                    P.tt(pw[:, k + 1, 0, :], x1, x2, ALU.subtract)
                    P.tt(x1, re_, im_, ALU.mult)
                    P.ts(pw[:, k + 1, 1, :], x1, 2.0, None, ALU.mult)
            return fre, fim, nfim, pw

        def phase_s5():
            ar.reset()
            fre, fim, nfim, pw = s5_tables()
            dsk = ar.alloc(4, F32)
            glb = ar.alloc(4, F32)
            P.dma(dsk, wl["s5_d"])
            P.dma(glb, wl["s5_glu_b"])
            gw = ar.alloc((4, W), BF16)
            for kc in range(4):
                load_w(gw[:, kc, :], wl["s5_glu_w"].ap[kc * 128:(kc + 1) * 128, :], wl["s5_glu_w"].t)
            wB = ar.alloc((8, 2, 128), BF16)
            wC = ar.alloc((8, 2, 128), BF16)
            uT = ar.alloc(S, BF16)
            yacc = ar.alloc(S, F32)
            zg = ar.alloc((4, S), BF16)
            RA = [ar.alloc(S, F32) for _ in range(2)]
            RB = [ar.alloc(S, F32) for _ in range(2)]
            tm = [ar.alloc(512, F32) for _ in range(2)]
            stg = [ar.alloc(512, BF16) for _ in range(2)]
            sg = ar.alloc(512, F32)
            for b in range(2):
                for cc in range(4):
                    for pr in range(4):
                        for d in range(2):
                            ii = pr * 2 + d
                            gi = d * 16 + cc * 4 + pr
                            load_w(wB[:, ii, 0, :], wl["s5_btre"].ap[gi], wl["s5_btre"].t)
                            load_w(wB[:, ii, 1, :], wl["s5_btim"].ap[gi], wl["s5_btim"].t)
                            load_w(wC[:, ii, 0, :], wl["s5_ctre"].ap[gi], wl["s5_ctre"].t)
                            load_w(wC[:, ii, 1, :], wl["s5_ctim"].ap[gi], wl["s5_ctim"].t)
                    P.ts(wC[:, :, 1, :], wC[:, :, 1, :], -1.0, None, ALU.mult)
                    P.dma(uT, proj_v(MC_S5U + cc, b))
                    P.ts(yacc, uT, dsk[:, cc:cc + 1], None, ALU.mult)
                    for pr in range(4):
                        for d in range(2):
                            ii = pr * 2 + d
                            col = d * 16 + cc * 4 + pr
                            cur, oth = RA, RB
                            for tt in range(NT5):
                                ts_ = slice(tt * 512, (tt + 1) * 512)
                                p1, p2 = P.bank(), P.bank()
                                P.mm(p1, wB[:, ii, 0, :], uT[:, ts_])
                                P.mm(p2, wB[:, ii, 1, :], uT[:, ts_])
                                P.ts(tm[0], p1, fre[:, col:col + 1], None, ALU.mult)
                                P.stt(cur[0][:, ts_], p2, nfim[:, col:col + 1], tm[0], ALU.mult, ALU.add)
                                P.ts(tm[1], p2, fre[:, col:col + 1], None, ALU.mult)
                                P.stt(cur[1][:, ts_], p1, fim[:, col:col + 1], tm[1], ALU.mult, ALU.add)
                            for k in range(LOGS):
                                sh = 1 << k
                                cr, ci, nci = pw[:, k, 0, col:col + 1], pw[:, k, 1, col:col + 1], pw[:, k, 2, col:col + 1]
                                if d == 0:
                                    dst, src, keep = slice(sh, S), slice(0, S - sh), slice(0, sh)
                                else:
                                    dst, src, keep = slice(0, S - sh), slice(sh, S), slice(S - sh, S)
                                P.stt(oth[0][:, dst], cur[0][:, src], cr, cur[0][:, dst], ALU.mult, ALU.add)
                                P.stt(oth[0][:, dst], cur[1][:, src], nci, oth[0][:, dst], ALU.mult, ALU.add)
                                P.stt(oth[1][:, dst], cur[1][:, src], cr, cur[1][:, dst], ALU.mult, ALU.add)
                                P.stt(oth[1][:, dst], cur[0][:, src], ci, oth[1][:, dst], ALU.mult, ALU.add)
                                P.cp(oth[0][:, keep], cur[0][:, keep], eng="act")
                                P.cp(oth[1][:, keep], cur[1][:, keep], eng="act")
                                cur, oth = oth, cur
                            hb = [V(oth[i].ap.bitcast(BF16)[:, 0:S], oth[i].t) for i in range(2)]
                            P.cp(hb[0], cur[0], eng="act")
                            P.cp(hb[1], cur[1], eng="act")
                            for tt in range(NT5):
                                ts_ = slice(tt * 512, (tt + 1) * 512)
                                pc = P.bank()
                                P.mm(pc, wC[:, ii, 0, :], hb[0][:, ts_], start=True, stop=False)
                                P.mm(pc, wC[:, ii, 1, :], hb[1][:, ts_], start=False, stop=True)
                                P.tt(yacc[:, ts_], yacc[:, ts_], pc, ALU.add)
                    P.act(zg[:, cc, :], yacc, AF.Gelu)
                for co in range(4):
                    for tt in range(NT5):
                        ts_ = slice(tt * 512, (tt + 1) * 512)
                        ps = P.bank()
                        for kc in range(4):
                            P.mm(ps, gw[:, kc, co * 128:(co + 1) * 128], zg[:, kc, ts_], start=(kc == 0), stop=(kc == 3))
                        P.act(sg, ps, AF.Sigmoid, bias=glb[:, co:co + 1])
                        s_ = stg[(co * NT5 + tt) % 2]
                        P.tt(s_, zg[:, co, ts_], sg, ALU.mult)
                        P.dma(DV(ysT[512 + co * 128: 512 + (co + 1) * 128, b * S + tt * 512: b * S + (tt + 1) * 512], ("ys", 1, b, co, tt)), s_)

        def phase_lru_sc():
            ar.reset()
            cwl = ar.alloc((4, 4), F32)
            cbl = ar.alloc(4, F32)
            ba_, bx_, lam_ = ar.alloc(8, F32), ar.alloc(8, F32), ar.alloc(8, F32)
            scw = ar.alloc((4, 3), F32)
            P.dma(cwl, wl["lru_cw"])
            P.dma(cbl, wl["lru_cb"])
            P.dma(ba_, wl["lru_ba"])
            P.dma(bx_, wl["lru_bx"])
            P.dma(lam_, wl["lru_lam"])
            P.dma(scw, wl["sc_cw"])
            WA, WX = ar.alloc((8, 128), BF16), ar.alloc((8, 128), BF16)
            for i in range(8):
                load_w(WA[:, i, :], wl["lru_wa"].ap[i], wl["lru_wa"].t)
                load_w(WX[:, i, :], wl["lru_wx"].ap[i], wl["lru_wx"].t)
            P.act(lam_, lam_, AF.Exp, scale=-1.0)
            P.act(lam_, lam_, AF.Ln, bias=one_c)
            P.ts(lam_, lam_, -8.0, None, ALU.mult)
            xp = ar.alloc(S + 4, BF16)
            xc = ar.alloc(S, F32)
            xcb = ar.alloc(S, BF16)
            aa, bb = ar.alloc(S, F32), ar.alloc(S, F32)
            hh = [ar.alloc(S, F32) for _ in range(2)]
            gch = ar.alloc(S, BF16)
            gg = ar.alloc(S, F32)
            yo = ar.alloc(S, BF16)
            r_, i_, t_ = ar.alloc(512, F32), ar.alloc(512, F32), ar.alloc(512, F32)
            s1, s2, s3 = ar.alloc(S, BF16), ar.alloc(S, BF16), ar.alloc(S + 2, F32)
            P.memset(xp[:, 0:2], 0.0)
            P.memset(xp[:, S + 2:S + 4], 0.0)
            P.memset(s3[:, 0:1], 0.0)
            P.memset(s3[:, S + 1:S + 2], 0.0)
            for b in range(2):
                for c in range(4):
                    P.dma(xp[:, 2:S + 2], proj_v(MC_LRUX + c, b))
                    P.dma(gch, proj_v(MC_LRUG + c, b))
                    P.ts(xc, xp[:, 0:S], cwl[:, c, 0:1], cbl[:, c:c + 1], ALU.mult, ALU.add)
                    for k in range(1, 4):
                        P.stt(xc, xp[:, k:k + S], cwl[:, c, k:k + 1], xc, ALU.mult, ALU.add)
                    P.cp(xcb, xc, eng="act")
                    for d in range(2):
                        j = d * 4 + c
                        for tt in range(NT5):
                            ts_ = slice(tt * 512, (tt + 1) * 512)
                            pr_, pi_ = P.bank(), P.bank()
                            P.mm(pr_, WA[:, j, :], xcb[:, ts_])
                            P.mm(pi_, WX[:, j, :], xcb[:, ts_])
                            P.act(r_, pr_, AF.Sigmoid, bias=ba_[:, j:j + 1])
                            P.act(i_, pi_, AF.Sigmoid, bias=bx_[:, j:j + 1])
                            P.act(aa[:, ts_], r_, AF.Exp, scale=lam_[:, j:j + 1])
                            P.act(t_, aa[:, ts_], AF.Square)
                            P.act(t_, t_, AF.Sqrt, scale=-1.0, bias=one_c)
                            P.tt(i_, i_, t_, ALU.mult)
                            P.tt(bb[:, ts_], i_, xc[:, ts_], ALU.mult)
                        if d == 0:
                            P.scan(hh[0], aa, bb, 0.0)
                        else:
                            P.scan(hh[1][:, ::-1], aa[:, ::-1], bb[:, ::-1], 0.0)
                    P.tt(hh[0], hh[0], hh[1], ALU.add)
                    P.act(gg, gch, AF.Gelu)
                    P.tt(yo, hh[0], gg, ALU.mult)
                    P.dma(DV(ysT[1024 + c * 128: 1024 + (c + 1) * 128, b * S:(b + 1) * S], ("ys", 2, b, c)), yo)
                    P.dma(s1, proj_v(MC_SCC + c, b))
                    P.dma(s2, proj_v(MC_SCX + c, b))
                    P.tt(s3[:, 1:S + 1], s1, s2, ALU.mult)
                    P.dma(s1, proj_v(MC_SCB + c, b))
                    P.ts(gg, s3[:, 0:S], scw[:, c, 0:1], None, ALU.mult)
                    for k in range(1, 3):
                        P.stt(gg, s3[:, k:k + S], scw[:, c, k:k + 1], gg, ALU.mult, ALU.add)
                    P.tt(s2, gg, s1, ALU.mult)
                    P.dma(DV(ysT[1536 + c * 128: 1536 + (c + 1) * 128, b * S:(b + 1) * S], ("ys", 3, b, c)), s2)

        def ys_v(m, b, cols):
            if m == 0:
                keys = [("ys", 0, b, n) for n in range(NT1)]
            elif m == 1:
                keys = [("ys", 1, b, co, tt) for co in range(4) for tt in range(NT5)]
            else:
                keys = [("ys", m, b, c) for c in range(4)]
            return DV(ysT[m * 512:(m + 1) * 512, b * S + cols[0]: b * S + cols[1]].rearrange("(k p) t -> p k t", p=128), *keys)

        def phase_merge_xa():
            ar.reset()
            wbr = ar.alloc((16, D), BF16)
            wmo = ar.alloc((8, D), BF16)
            wq = ar.alloc((8, D), BF16)
            wo = ar.alloc((8, D), BF16)
            for kc in range(16):
                for hf in range(2):
                    load_w(wbr[:, kc, hf * 512:(hf + 1) * 512], wl["w_branch"].ap[kc * 128:(kc + 1) * 128, hf * 512:(hf + 1) * 512], wl["w_branch"].t)
            for kc in range(8):
                for hf in range(2):
                    cs = slice(hf * 512, (hf + 1) * 512)
                    load_w(wmo[:, kc, cs], wl["w_mix_out"].ap[kc * 128:(kc + 1) * 128, cs], wl["w_mix_out"].t)
                    load_w(wq[:, kc, cs], wl["xa_wq"].ap[kc * 128:(kc + 1) * 128, cs], wl["xa_wq"].t)
                    load_w(wo[:, kc, cs], wl["xa_wo"].ap[kc * 128:(kc + 1) * 128, cs], wl["xa_wo"].t)
            gx = ar.alloc(8, F32)
            gm = ar.alloc(8, F32)
            P.dma(gx, wl["xa_norm"])
            P.dma(gm, wl["xa_mnorm"])
            KT = ar.alloc((8, MEM), BF16)
            Vm = ar.alloc((2, D), BF16)
            mark = ar.off
            for b in range(2):
                ar.reset(mark)
                wkv = ar.alloc((8, 512), BF16)
                mt = ar.alloc((8, MEM), F32)
                mh = ar.alloc((8, MEM), BF16)
                sq = ar.alloc((8, 512), BF16)
                rt = ar.alloc(512, F32)
                P.dma(mt, V(memT.ap[:, b * MEM:(b + 1) * MEM].rearrange("(k p) t -> p k t", p=128), memT.t))
                rmsnorm_tile(mt, gm, mh, MEM, sq, rt)
                for cb in range(4):
                    for kc in range(8):
                        load_w(wkv[:, kc, :], wl["xa_wkv"].ap[kc * 128:(kc + 1) * 128, cb * 512:(cb + 1) * 512], wl["xa_wkv"].t)
                    if cb < 2:
                        for j in range(4):
                            ps = P.bank()
                            for kc in range(8):
                                P.mm(ps[:, 0:MEM], wkv[:, kc, j * 128:(j + 1) * 128], mh[:, kc, :], start=(kc == 0), stop=(kc == 7))
                            evac(KT[:, cb * 4 + j, :], ps[:, 0:MEM])
                    else:
                        for mz in range(2):
                            ps = P.bank()
                            for kc in range(8):
                                P.mm(ps, mh[:, kc, mz * 128:(mz + 1) * 128], wkv[:, kc, :], start=(kc == 0), stop=(kc == 7))
                            evac(Vm[:, mz, (cb - 2) * 512:(cb - 1) * 512], ps)
                xt = ar.alloc((8, 512), F32)
                ysb = ar.alloc((16, 512), BF16)
                gtj = ar.alloc((2, 4, 512), BF16)
                acc = ar.alloc(512, F32)
                tmp = ar.alloc(512, F32)
                mg = ar.alloc((8, 512), BF16)
                xn = ar.alloc((8, 512), BF16)
                qT = mg
                oT = xn
                E = ar.alloc((2, 512), BF16)
                rsum = ar.alloc(512, F32)
                for tt in range(NT5):
                    cols = (tt * 512, (tt + 1) * 512)
                    P.dma(xt, xres_v(b, cols))
                    for m in range(4):
                        P.dma(ysb[:, m * 4:(m + 1) * 4, :], ys_v(m, b, cols))
                    for j in range(8):
                        gts = gtj[:, j % 2, :, :]
                        for m in range(4):
                            gv = proj_v(MC_GATE + m * 8 + j, b)
                            P.dma(gts[:, m, :], V(gv.ap[:, cols[0]:cols[1]], gv.t))
                        for m in range(4):
                            ps = P.bank()
                            for kc in range(4):
                                P.mm(ps, wbr[:, m * 4 + kc, j * 128:(j + 1) * 128], ysb[:, m * 4 + kc, :], start=(kc == 0), stop=(kc == 3))
                            if m == 0:
                                P.tt(acc, ps, gts[:, 0, :], ALU.mult)
                            else:
                                P.tt(tmp, ps, gts[:, m, :], ALU.mult)
                                P.tt(mg[:, j, :] if m == 3 else acc, acc, tmp, ALU.add, eng="pool")
                    for jo in range(8):
                        ps = P.bank()
                        for kc in range(8):
                            P.mm(ps, wmo[:, kc, jo * 128:(jo + 1) * 128], mg[:, kc, :], start=(kc == 0), stop=(kc == 7))
                        P.tt(xt[:, jo, :], xt[:, jo, :], ps, ALU.add)
                    rmsnorm_tile(xt, gx, xn, 512, sq, rt)
                    for jo in range(8):
                        ps = P.bank()
                        for kc in range(8):
                            P.mm(ps, wq[:, kc, jo * 128:(jo + 1) * 128], xn[:, kc, :], start=(kc == 0), stop=(kc == 7))
                        evac(qT[:, jo, :], ps)
                    for h in range(4):
                        for mz in range(2):
                            ps = P.bank()
                            for hc in range(2):
                                P.mm(ps, KT[:, 2 * h + hc, mz * 128:(mz + 1) * 128], qT[:, 2 * h + hc, :], start=(hc == 0), stop=(hc == 1))
                            P.act(E[:, mz, :], ps, AF.Exp, scale=1.0 / 16.0)
                        ps = P.bank()
                        for mz in range(2):
                            P.mm(ps, ones_b, E[:, mz, :], start=(mz == 0), stop=(mz == 1))
                        P.recip(rsum, ps)
                        for hc in range(2):
                            ps = P.bank()
                            for mz in range(2):
                                P.mm(ps, Vm[:, mz, (2 * h + hc) * 128:(2 * h + hc + 1) * 128], E[:, mz, :], start=(mz == 0), stop=(mz == 1))
                            P.tt(oT[:, 2 * h + hc, :], ps, rsum, ALU.mult)
                    for jo in range(8):
                        ps = P.bank()
                        for kc in range(8):
                            P.mm(ps, wo[:, kc, jo * 128:(jo + 1) * 128], oT[:, kc, :], start=(kc == 0), stop=(kc == 7))
                        P.tt(xt[:, jo, :], xt[:, jo, :], ps, ALU.add)
                    P.dma(xres_v(b, cols), xt)

        def phase_ffn():
            ar.reset()
            gf = ar.alloc(8, F32)
            P.dma(gf, wl["ffn_norm"])
            fcw = ar.alloc((44, 3), F32)
            fcb = ar.alloc(44, F32)
            P.dma(fcw, wl["ffn_cw"])
            P.dma(fcb, wl["ffn_cb"])
            xt = ar.alloc((8, 512), F32)
            hb = ar.alloc((8, 512), BF16)
            sq = ar.alloc((8, 512), BF16)
            rt = ar.alloc(512, F32)
            for b in range(2):
                for tt in range(NT5):
                    cols = (tt * 512, (tt + 1) * 512)
                    P.dma(xt, xres_v(b, cols))
                    rmsnorm_tile(xt, gf, hb, 512, sq, rt)
                    P.dma(DV(hffT[:, b * S + cols[0]: b * S + cols[1]].rearrange("(k p) t -> p k t", p=128), ("hff", b, tt)), hb)
            wup = ar.alloc((8, 22 * 128), BF16)
            wdn = ar.alloc((11, D), BF16)
            hp = ar.alloc((8, 512), BF16)
            hid = ar.alloc((11, 512), BF16)
            cg, cu = ar.alloc(512, F32), ar.alloc(512, F32)
            tiles = []
            t0 = 0
            while t0 < S:
                n = min(510, S - t0)
                tiles.append((t0, n))
                t0 += n
            for hf in range(2):
                for kc in range(8):
                    for part in range(2):
                        c0 = part * DFF + hf * 1408
                        for q3 in range(3):
                            w_ = 512 if q3 < 2 else 384
                            load_w(wup[:, kc, part * 1408 + q3 * 512: part * 1408 + q3 * 512 + w_],
                                   wl["ffn_wup"].ap[kc * 128:(kc + 1) * 128, c0 + q3 * 512: c0 + q3 * 512 + w_], wl["ffn_wup"].t)
                for kc in range(11):
                    for hh_ in range(2):
                        cs = slice(hh_ * 512, (hh_ + 1) * 512)
                        load_w(wdn[:, kc, cs], wl["ffn_wdown"].ap[hf * 1408 + kc * 128: hf * 1408 + (kc + 1) * 128, cs], wl["ffn_wdown"].t)
                for b in range(2):
                    hkeys = [("hff", b, tt) for tt in range(NT5)]
                    for (t0, n) in tiles:
                        lo, hi = max(t0 - 1, 0), min(t0 + n + 1, S)
                        off = lo - (t0 - 1)
                        nin = hi - lo
                        if off:
                            P.memset(hp[:, :, 0:1], 0.0)
                        if hi < t0 + n + 1:
                            P.memset(hp[:, :, n + 1:n + 2], 0.0)
                        P.dma(hp[:, :, off:off + nin], DV(hffT[:, b * S + lo: b * S + hi].rearrange("(k p) t -> p k t", p=128), *hkeys))
                        for fc in range(11):
                            pg_, pu_ = P.bank(), P.bank()
                            for kc in range(8):
                                P.mm(pg_[:, 0:n + 2], wup[:, kc, fc * 128:(fc + 1) * 128], hp[:, kc, 0:n + 2], start=(kc == 0), stop=(kc == 7))
                            for kc in range(8):
                                P.mm(pu_[:, 0:n + 2], wup[:, kc, 1408 + fc * 128: 1408 + (fc + 1) * 128], hp[:, kc, 0:n + 2], start=(kc == 0), stop=(kc == 7))
                            ig, iu = hf * 11 + fc, 22 + hf * 11 + fc
                            P.ts(cg[:, 0:n], pg_[:, 0:n], fcw[:, ig, 0:1], fcb[:, ig:ig + 1], ALU.mult, ALU.add)
                            P.stt(cg[:, 0:n], pg_[:, 1:n + 1], fcw[:, ig, 1:2], cg[:, 0:n], ALU.mult, ALU.add)
                            P.stt(cg[:, 0:n], pg_[:, 2:n + 2], fcw[:, ig, 2:3], cg[:, 0:n], ALU.mult, ALU.add)
                            P.ts(cu[:, 0:n], pu_[:, 0:n], fcw[:, iu, 0:1], fcb[:, iu:iu + 1], ALU.mult, ALU.add, eng="pool") if False else \
                                P.ts(cu[:, 0:n], pu_[:, 0:n], fcw[:, iu, 0:1], fcb[:, iu:iu + 1], ALU.mult, ALU.add)
                            P.stt(cu[:, 0:n], pu_[:, 1:n + 1], fcw[:, iu, 1:2], cu[:, 0:n], ALU.mult, ALU.add)
                            P.stt(cu[:, 0:n], pu_[:, 2:n + 2], fcw[:, iu, 2:3], cu[:, 0:n], ALU.mult, ALU.add)
                            P.act(cg[:, 0:n], cg[:, 0:n], AF.Silu)
                            P.tt(hid[:, fc, 0:n], cg[:, 0:n], cu[:, 0:n], ALU.mult)
                        P.dma(xt[:, :, 0:n], xres_v(b, (t0, t0 + n)))
                        for jo in range(8):
                            ps = P.bank()
                            for kc in range(11):
                                P.mm(ps[:, 0:n], wdn[:, kc, jo * 128:(jo + 1) * 128], hid[:, kc, 0:n], start=(kc == 0), stop=(kc == 10))
                            P.tt(xt[:, jo, 0:n], xt[:, jo, 0:n], ps[:, 0:n], ALU.add)
                        P.dma(xres_v(b, (t0, t0 + n)), xt[:, :, 0:n])

        for b in range(2):
            phase_A(b)
        for b in range(2):
            phase_gdn(b)
        phase_s5()
        phase_lru_sc()
        phase_merge_xa()
        phase_ffn()

    ar.reset()
    gfin = ar.alloc(8, F32)
    P.dma(gfin, fin_norm)
    xt = ar.alloc((8, 512), F32)
    yt = ar.alloc((8, 512), F32)
    sq = ar.alloc((8, 512), BF16)
    rt = ar.alloc(512, F32)
    toks = []
    for b in range(2):
        for tt in range(NT5):
            cols = (tt * 512, (tt + 1) * 512)
            P.dma(xt, xres_v(b, cols))
            P.act(sq, xt, AF.Square)
            ps = P.bank()
            for kc in range(8):
                P.mm(ps, ones_b, sq[:, kc, :], start=(kc == 0), stop=(kc == 7))
            P.act(rt, ps, AF.Sqrt, scale=1.0 / D, bias=eps_c)
            P.recip(rt, rt)
            for kc in range(8):
                P.stt(yt[:, kc, :], xt[:, kc, :], gfin[:, kc:kc + 1], rt, ALU.mult, ALU.mult)
            osl = slice(b * S + cols[0], b * S + cols[1])
            toks.append(P.dma(V(xoT.ap[:, osl].rearrange("(k p) t -> p k t", p=128), Trk()), xt))
            toks.append(P.dma(V(ynT.ap[:, osl].rearrange("(k p) t -> p k t", p=128), Trk()), yt))
    P.wait_all("sp", toks)
    P.build()
    return nc


def _pc(a, n):
    return np.ascontiguousarray(a.reshape(n, 128).T)


def prep_layer(inp, l):
    f = lambda k: np.asarray(inp[k][l], dtype=np.float32)
    o = {}
    o["mix_norm"] = _pc(f("mix_norm"), 8)
    o["w_in"] = f("w_in")
    o["gdn_conv"] = np.ascontiguousarray(f("gdn_conv").reshape(4, 12, 128).transpose(2, 1, 0))
    o["gdn_alog"] = f("gdn_a_log").reshape(1, 8)
    o["gdn_dtb"] = f("gdn_dt_bias").reshape(1, 8)
    o["gdn_gain"] = f("gdn_out_norm").reshape(1, 128)

    def st(a):
        return np.ascontiguousarray(a.reshape(2, 16, 2, 64).transpose(2, 3, 0, 1).reshape(128, 32))
    o["s5_lre"] = st(f("s5_lambda_re"))
    o["s5_lim"] = st(f("s5_lambda_im"))
    o["s5_lstep"] = st(np.repeat(f("s5_log_step")[:, :, None], 64, axis=2))

    def bt(a):
        out = np.zeros((2, 16, 128, 128), np.float32)
        for g in range(32):
            pair, g2 = g // 2, g % 2
            r0 = 16 * (g % 8)
            out[:, pair, r0:r0 + 16, g2 * 64:(g2 + 1) * 64] = a[:, g].transpose(0, 2, 1)
        return out.reshape(32, 128, 128)

    def ct(a):
        out = np.zeros((2, 16, 128, 128), np.float32)
        for g in range(32):
            pair, g2 = g // 2, g % 2
            c0 = 16 * (g % 8)
            out[:, pair, g2 * 64:(g2 + 1) * 64, c0:c0 + 16] = a[:, g].transpose(0, 2, 1)
        return out.reshape(32, 128, 128)
    o["s5_btre"], o["s5_btim"] = bt(f("s5_b_re")), bt(f("s5_b_im"))
    o["s5_ctre"], o["s5_ctim"] = ct(f("s5_c_re")), ct(f("s5_c_im"))
    o["s5_d"] = _pc(f("s5_d"), 4)
    o["s5_glu_w"] = f("s5_glu_w")
    o["s5_glu_b"] = _pc(f("s5_glu_b"), 4)
    o["lru_cw"] = np.ascontiguousarray(f("lru_conv_w").reshape(4, 4, 128).transpose(2, 1, 0))
    o["lru_cb"] = _pc(f("lru_conv_b"), 4)

    def bd(a):
        out = np.zeros((2, 4, 128, 128), np.float32)
        for n in range(8):
            c, bq = n // 2, n % 2
            out[:, c, bq * 64:(bq + 1) * 64, bq * 64:(bq + 1) * 64] = a[:, n]
        return out.reshape(8, 128, 128)
    o["lru_wa"], o["lru_wx"] = bd(f("lru_gate_a_w")), bd(f("lru_gate_x_w"))
    p2 = lambda a: np.ascontiguousarray(a.reshape(2, 4, 128).transpose(2, 0, 1).reshape(128, 8))
    o["lru_ba"], o["lru_bx"], o["lru_lam"] = p2(f("lru_gate_a_b")), p2(f("lru_gate_x_b")), p2(f("lru_lambda"))
    o["sc_cw"] = np.ascontiguousarray(f("sc_conv").reshape(3, 4, 128).transpose(2, 1, 0))
    o["w_branch"] = f("w_branch").reshape(4 * W, D)
    o["w_mix_out"] = f("w_mix_out")
    o["xa_norm"], o["xa_mnorm"] = _pc(f("xa_norm"), 8), _pc(f("xa_mem_norm"), 8)
    o["xa_wq"], o["xa_wkv"], o["xa_wo"] = f("xa_w_q"), f("xa_w_kv"), f("xa_w_o")
    o["ffn_norm"] = _pc(f("ffn_norm"), 8)
    o["ffn_wup"] = f("ffn_w_up")
    o["ffn_cw"] = np.ascontiguousarray(f("ffn_conv_w").reshape(3, 44, 128).transpose(2, 1, 0))
    o["ffn_cb"] = _pc(f("ffn_conv_b"), 44)
    o["ffn_wdown"] = f("ffn_w_down")
    return o


def consts():
    i = np.arange(128)
    c = {"c_ident": np.eye(128, dtype=np.float32)}
    uc = np.zeros((2, 128, 128), np.float32)
    uc[0] = (i[:, None] <= i[None, :])
    uc[1] = (i[:, None] >= i[None, :])
    ng = np.zeros((2, 128, 128), np.float32)
    ng[0] = np.where(i[None, :] > i[:, None], 0.0, -30000.0)
    ng[1] = np.where(i[None, :] < i[:, None], 0.0, -30000.0)
    c["c_ucum"], c["c_negm"] = uc, ng
    mk = np.zeros((7, 128, 128), np.float32)
    for lv in range(7):
        sz = 1 << lv
        mk[lv] = ((i[:, None] // (2 * sz)) == (i[None, :] // (2 * sz))) & ((i[:, None] // sz) != (i[None, :] // sz))
    c["c_msk"] = mk
    return c


_PROG = {}


DBG = []


def run_layers(xT_list, memT_list, inp, layers, S):
    NL = len(layers)
    key = (S, NL)
    if key not in _PROG:
        _PROG[key] = build_program(S, NL, dbg=bool(DBG))
    nc = _PROG[key]
    per = [prep_layer(inp, l) for l in layers]
    shared = {k: np.ascontiguousarray(np.stack([p[k] for p in per])) for k in per[0]}
    shared["fin_norm"] = _pc(np.asarray(inp["final_norm"], np.float32), 8)
    shared.update(consts())
    in_maps = []
    for c in range(len(xT_list)):
        m = dict(shared)
        m["xT"] = xT_list[c]
        m["memT"] = memT_list[c]
        in_maps.append(m)
    res = run_bass_kernel_spmd(nc, in_maps, core_ids=list(range(len(xT_list))))
    if DBG:
        DBG.append(res.results)
    return [r["xoT"] for r in res.results], [r["ynT"] for r in res.results]


def kernel(**inp):
    x = np.asarray(inp["x"], np.float32)
    mem = np.asarray(inp["mem"], np.float32)
    B, S, _ = x.shape
    depth = inp["w_in"].shape[0]
    nco = B // 2
    xT = [np.ascontiguousarray(x[2 * c:2 * c + 2].reshape(2 * S, D).T) for c in range(nco)]
    mT = [np.ascontiguousarray(mem[2 * c:2 * c + 2].reshape(2 * MEM, D).T) for c in range(nco)]
    xT, yn = run_layers(xT, mT, inp, list(range(depth)), S)
    out = np.empty((B, S, D), np.float32)
    for c in range(nco):
        out[2 * c:2 * c + 2] = yn[c].T.reshape(2, S, D)
    return out
```

```python
import contextlib
import math
import numpy as np
import concourse.bass as bass
import concourse.mybir as mybir
from concourse.bass_utils import run_bass_kernel_spmd

F32 = mybir.dt.float32
BF16 = mybir.dt.bfloat16
AF = mybir.ActivationFunctionType
ALU = mybir.AluOpType

D = 1024
W = 512
MEM = 256
DFF = 2816
INW = 9232
EPS = 1e-6
NCORES = 8


class Trk:
    __slots__ = ("w", "r")

    def __init__(self):
        self.w = None
        self.r = []


def _trks(v):
    t = v.t
    return t if isinstance(t, (list, tuple)) else (t,)


class V:
    __slots__ = ("ap", "t")

    def __init__(self, ap, t):
        self.ap = ap
        self.t = t

    def __getitem__(self, k):
        return V(self.ap[k], self.t)

    def rr(self, pat, **kw):
        return V(self.ap.rearrange(pat, **kw), self.t)

    def bc(self, shape):
        return V(self.ap.broadcast_to(list(shape)), self.t)


class Prog:
    CE = ("pe", "act", "dve", "pool")
    NSLOT = 8

    def __init__(self, nc):
        self.nc = nc
        self.es = contextlib.ExitStack()
        self.q = {e: [] for e in ("pe", "act", "dve", "pool", "sp")}
        self.cnt = {e: 0 for e in self.CE}
        self.sem = {e: self.es.enter_context(nc.semaphore("s_" + e)) for e in self.CE}
        self.known = {e: {} for e in self.q}
        self.dsem = {}
        self.dcnt = {}
        for qn in ("sp", "pool"):
            self.dsem[qn] = [self.es.enter_context(nc.semaphore("d_%s%d" % (qn, i))) for i in range(self.NSLOT)]
            self.dcnt[qn] = 0
        self.nuid = 0
        self.nbank = 0
        self.gbank = {}
        self.banks = []

    def sb(self, shape, dt, name=None):
        self.nuid += 1
        h = self.es.enter_context(self.nc.sbuf_tensor(name or ("sb%d" % self.nuid), list(shape), dt))
        return V(h[:], Trk())

    def dram(self, name, shape, dt, kind="Internal"):
        h = self.nc.dram_tensor(name, list(shape), dt, kind=kind)
        return V(h.ap(), Trk())

    def init_banks(self):
        for i in range(8):
            h = self.es.enter_context(self.nc.psum_tensor("bank%d" % i, [128, 512], F32))
            self.banks.append(V(h[:], Trk()))

    def bank(self, group=None):
        if group is None:
            b = self.banks[self.nbank % 8]
            self.nbank += 1
            return b
        k = self.gbank.get(group, 0)
        self.gbank[group] = k + 1
        return self.banks[group * 4 + k % 4]

    def bankbf(self, group=None):
        b = self.bank(group)
        return V(b.ap.bitcast(BF16), b.t)

    def _deps(self, reads, writes):
        deps = []
        for v in reads:
            for t in _trks(v):
                if t.w is not None:
                    deps.append(t.w)
        for v in writes:
            for t in _trks(v):
                if t.w is not None:
                    deps.append(t.w)
                deps.extend(t.r)
        return deps

    def _waits(self, eng, deps, raw_same, always=False):
        kn = self.known[eng]
        need = {}
        for tok in deps:
            key, val, src = tok
            if key == eng and not always:
                if eng == "pe":
                    continue
                if tok not in raw_same or val < self.cnt[eng] - 1:
                    continue
            if kn.get(key, 0) >= val:
                continue
            if need.get(key, 0) < val:
                need[key] = val
        out = []
        for key, val in need.items():
            kn[key] = val
            out.append((self._semof(key), val))
        return out

    def _semof(self, key):
        if isinstance(key, str):
            return self.sem[key]
        return self.dsem[key[0]][key[1]]

    def op(self, eng, name, args, kw, reads, writes):
        deps = self._deps(reads, writes)
        raw = set()
        for v in reads:
            for t in _trks(v):
                if t.w is not None:
                    raw.add(t.w)
        waits = self._waits(eng, deps, raw)
        self.cnt[eng] += 1
        tok = (eng, self.cnt[eng], eng)
        self.q[eng].append((waits, name, args, kw, (self.sem[eng], 1)))
        for v in reads:
            for t in _trks(v):
                t.r.append(tok)
        for v in writes:
            for t in _trks(v):
                t.w = tok
                t.r = []
        return tok

    def dma(self, out, in_, qn="sp", **kw):
        deps = self._deps([in_], [out])
        n = self.dcnt[qn]
        slot = n % self.NSLOT
        gen = n // self.NSLOT
        self.dcnt[qn] += 1
        key = (qn, slot)
        waits = self._waits(qn, deps, set(), always=True)
        if gen > 0 and self.known[qn].get(key, 0) < 16 * gen:
            self.known[qn][key] = 16 * gen
            waits.append((self.dsem[qn][slot], 16 * gen))
        tok = (key, 16 * (gen + 1), qn)
        self.q[qn].append((waits, "dma_start", (), dict(out=out.ap, in_=in_.ap, **kw), (self.dsem[qn][slot], 16)))
        for t in _trks(in_):
            t.r.append(tok)
        for t in _trks(out):
            t.w = tok
            t.r = []
        return tok

    def barrier(self):
        toks = [(e, self.cnt[e], e) for e in self.CE if self.cnt[e] > 0]
        for qn in self.dsem:
            n = self.dcnt[qn]
            for s in range(self.NSLOT):
                k = (n - s + self.NSLOT - 1) // self.NSLOT
                if k > 0:
                    toks.append(((qn, s), 16 * k, qn))
        for eng in self.q:
            waits = self._waits(eng, toks, set(), always=True)
            if waits:
                self.q[eng].append((waits, None, (), {}, None))

    @staticmethod
    def _a(x):
        return x.ap if isinstance(x, V) else x

    def mm(self, out, lhsT, rhs, start=True, stop=True):
        rd = [lhsT, rhs] + ([] if start else [out])
        return self.op("pe", "matmul", (out.ap, lhsT.ap, rhs.ap), dict(start=start, stop=stop), rd, [out])

    def tr(self, out, in_, ident):
        return self.op("pe", "transpose", (out.ap, in_.ap, ident.ap), {}, [in_, ident], [out])

    def act(self, out, in_, func, scale=1.0, bias=0.0, accum_out=None):
        rd = [in_] + [x for x in (scale, bias) if isinstance(x, V)]
        wr = [out] + ([accum_out] if accum_out is not None else [])
        kw = dict(scale=self._a(scale), bias=self._a(bias))
        if accum_out is not None:
            kw["accum_out"] = accum_out.ap
        return self.op("act", "activation", (out.ap, in_.ap, func), kw, rd, wr)

    def ts(self, out, in0, s1, s2, op0, op1=None, eng="dve"):
        rd = [in0] + [x for x in (s1, s2) if isinstance(x, V)]
        kw = {}
        if op1 is not None:
            kw["op1"] = op1
        return self.op(eng, "tensor_scalar", (out.ap, in0.ap, self._a(s1), self._a(s2), op0), kw, rd, [out])

    def tt(self, out, in0, in1, op, eng="dve"):
        return self.op(eng, "tensor_tensor", (out.ap, in0.ap, in1.ap, op), {}, [in0, in1], [out])

    def stt(self, out, in0, s, in1, op0, op1):
        rd = [in0, in1] + ([s] if isinstance(s, V) else [])
        return self.op("dve", "scalar_tensor_tensor", (out.ap, in0.ap, self._a(s), in1.ap, op0, op1), {}, rd, [out])

    def cp(self, out, in_, eng="dve"):
        if eng == "act":
            return self.op("act", "activation", (out.ap, in_.ap, AF.Identity), {}, [in_], [out])
        return self.op(eng, "tensor_copy", (out.ap, in_.ap), {}, [in_], [out])

    def scan(self, out, d0, d1, init):
        rd = [d0, d1] + ([init] if isinstance(init, V) else [])
        return self.op("dve", "tensor_tensor_scan", (out.ap, d0.ap, d1.ap, self._a(init), ALU.mult, ALU.add), {}, rd, [out])

    def memset(self, out, val, eng="dve"):
        return self.op(eng, "memset", (out.ap, val), {}, [], [out])

    def recip(self, out, in_):
        return self.op("dve", "reciprocal", (out.ap, in_.ap), {}, [in_], [out])

    def wait_all(self, eng, toks):
        waits = self._waits(eng, toks, set(), always=True)
        self.q[eng].append((waits, None, (), {}, None))

    def build(self):
        nc = self.nc
        q = self.q

        def replay(e, ops):
            for waits, name, args, kw, inc in ops:
                for sem, val in waits:
                    e.wait_ge(sem, val)
                if name is None:
                    continue
                ins = getattr(e, name)(*args, **kw)
                if inc is not None:
                    ins.then_inc(inc[0], inc[1])

        with nc.Block() as block:
            @block.tensor
            def _(e):
                replay(e, q["pe"])

            @block.scalar
            def _(e):
                replay(e, q["act"])

            @block.vector
            def _(e):
                replay(e, q["dve"])

            @block.gpsimd
            def _(e):
                replay(e, q["pool"])

            @block.sync
            def _(e):
                replay(e, q["sp"])
        self.es.close()


class Arena:
    def __init__(self, P, nelem):
        self.P = P
        self.n = nelem
        self.h = P.es.enter_context(P.nc.sbuf_tensor("arena", [128, nelem], BF16))
        self.off = 0

    def reset(self, to=0):
        self.P.barrier()
        self.off = to

    def alloc(self, shape, dt):
        if isinstance(shape, int):
            shape = (shape,)
        n = int(np.prod(shape))
        four = dt in (F32, mybir.dt.int32)
        ne = n * 2 if four else n
        ne = (ne + 1) // 2 * 2
        assert self.off + ne <= self.n, "arena overflow %d + %d > %d" % (self.off, ne, self.n)
        ap = self.h[:, self.off:self.off + ne]
        self.off += ne
        if four:
            ap = ap.bitcast(dt)
        if n != (ne // 2 if four else ne):
            ap = ap[:, 0:n]
        if len(shape) == 2:
            ap = ap.rearrange("p (a b) -> p a b", a=shape[0])
        elif len(shape) == 3:
            ap = ap.rearrange("p (a b c) -> p a b c", a=shape[0], b=shape[1])
        return V(ap, Trk())


MCH = [(128 * i, 128) for i in range(16)] + [(2048, 16)] + [(2064 + 128 * i, 128) for i in range(56)]
MC_S5U, MC_LRUX, MC_LRUG, MC_SCB, MC_SCC, MC_SCX, MC_GATE = 17, 21, 25, 29, 33, 37, 41

WSPEC = [
    ("mix_norm", (128, 8)), ("w_in", (D, INW)), ("gdn_conv", (128, 12, 4)), ("gdn_alog", (1, 8)),
    ("gdn_dtb", (1, 8)), ("gdn_gain", (1, 128)),
    ("s5_lre", (128, 32)), ("s5_lim", (128, 32)), ("s5_lstep", (128, 32)),
    ("s5_btre", (32, 128, 128)), ("s5_btim", (32, 128, 128)), ("s5_ctre", (32, 128, 128)), ("s5_ctim", (32, 128, 128)),
    ("s5_d", (128, 4)), ("s5_glu_w", (W, W)), ("s5_glu_b", (128, 4)),
    ("lru_cw", (128, 4, 4)), ("lru_cb", (128, 4)), ("lru_wa", (8, 128, 128)), ("lru_wx", (8, 128, 128)),
    ("lru_ba", (128, 8)), ("lru_bx", (128, 8)), ("lru_lam", (128, 8)),
    ("sc_cw", (128, 4, 3)), ("w_branch", (4 * W, D)), ("w_mix_out", (D, D)),
    ("xa_norm", (128, 8)), ("xa_mnorm", (128, 8)), ("xa_wq", (D, D)), ("xa_wkv", (D, 2 * D)), ("xa_wo", (D, D)),
    ("ffn_norm", (128, 8)), ("ffn_wup", (D, 2 * DFF)), ("ffn_cw", (128, 44, 3)), ("ffn_cb", (128, 44)),
    ("ffn_wdown", (DFF, D)),
]


def build_program(S, NL, dbg=()):
    T = 2 * S
    NT5 = S // 512
    NT1 = S // 128
    LOGS = int(round(math.log2(S)))
    nc = bass.Bass("TRN2", target_bir_lowering=False)
    P = Prog(nc)
    P.init_banks()
    xT = P.dram("xT", [D, T], F32, kind="ExternalInput")
    memT = P.dram("memT", [D, 2 * MEM], F32, kind="ExternalInput")
    wts = {}
    for name, shp in WSPEC:
        wts[name] = P.dram(name, [NL] + list(shp), F32, kind="ExternalInput")
    fin_norm = P.dram("fin_norm", [128, 8], F32, kind="ExternalInput")
    c_ident = P.dram("c_ident", [128, 128], F32, kind="ExternalInput")
    c_ucum = P.dram("c_ucum", [2, 128, 128], F32, kind="ExternalInput")
    c_negm = P.dram("c_negm", [2, 128, 128], F32, kind="ExternalInput")
    c_msk = P.dram("c_msk", [7, 128, 128], F32, kind="ExternalInput")
    xoT = P.dram("xoT", [D, T], F32, kind="ExternalOutput")
    ynT = P.dram("ynT", [D, T], F32, kind="ExternalOutput")
    xres = nc.dram_tensor("xres", [D, T], F32, kind="Internal").ap()
    projT = nc.dram_tensor("projT", [73 * 128, T], BF16, kind="Internal").ap()
    baTok = nc.dram_tensor("baTok", [T, 16], F32, kind="Internal").ap()
    ofwd = nc.dram_tensor("ofwd", [T, 512], F32, kind="Internal").ap()
    obwd = nc.dram_tensor("obwd", [T, 512], F32, kind="Internal").ap()
    ysT = nc.dram_tensor("ysT", [4 * W, T], BF16, kind="ExternalOutput" if dbg else "Internal").ap()
    hffT = nc.dram_tensor("hffT", [D, T], BF16, kind="Internal").ap()
    dtrk = {}

    def DV(ap, *keys):
        ts_ = []
        for k in keys:
            if k not in dtrk:
                dtrk[k] = Trk()
            ts_.append(dtrk[k])
        return V(ap, ts_)

    def xres_v(b, cols=None):
        ap = xres[:, b * S:(b + 1) * S] if cols is None else xres[:, b * S + cols[0]: b * S + cols[1]]
        return DV(ap.rearrange("(k p) t -> p k t", p=128), ("xres", b))

    ident_f = P.sb([128, 128], F32)
    ident_b = P.sb([128, 128], BF16)
    ones_f = P.sb([128, 128], F32)
    ones_b = P.sb([128, 128], BF16)
    identf4 = P.sb([128, 4, 128], F32)
    ucum = P.sb([128, 2, 128], F32)
    negm = P.sb([128, 2, 128], F32)
    eps_c = P.sb([128, 1], F32)
    one_c = P.sb([128, 1], F32)
    P.dma(ident_f, c_ident)
    P.dma(ucum, c_ucum.rr("d p i -> p d i"))
    P.dma(negm, c_negm.rr("d p i -> p d i"))
    mskf = P.sb([128, 7, 128], F32)
    msk = P.sb([128, 7, 128], BF16)
    P.dma(mskf, c_msk.rr("d p i -> p d i"))
    P.cp(msk, mskf)
    P.cp(ident_b, ident_f)
    P.memset(ones_f, 1.0)
    P.memset(ones_b, 1.0)
    P.memset(eps_c, EPS)
    P.memset(one_c, 1.0)
    for h in range(4):
        P.cp(identf4[:, h, :], ident_f)
    rem = nc.sbuf_bytes_remaining
    rem = rem() if callable(rem) else rem
    ar = Arena(P, (int(rem) - 4096) // 2 // 2 * 2)

    for b in range(2):
        P.dma(DV(xres[:, b * S:(b + 1) * S], ("xres", b)), V(xT.ap[:, b * S:(b + 1) * S], xT.t))

    rr_ = [0]

    def evac(out, in_):
        rr_[0] += 1
        P.cp(out, in_, eng="act" if rr_[0] % 2 else "dve")

    def load_w(dst, src_ap, trk):
        P.dma(dst, V(src_ap, trk), qn="pool")

    def rmsnorm_tile(xt, gain, dst, n, tmp_sq, tmp_r):
        P.act(tmp_sq[:, :, 0:n], xt, AF.Square)
        ps = P.bank()
        for kc in range(8):
            P.mm(ps[:, 0:n], ones_b, tmp_sq[:, kc, 0:n], start=(kc == 0), stop=(kc == 7))
        P.act(tmp_r[:, 0:n], ps[:, 0:n], AF.Sqrt, scale=1.0 / D, bias=eps_c)
        P.recip(tmp_r[:, 0:n], tmp_r[:, 0:n])
        for kc in range(8):
            P.stt(dst[:, kc, :], xt[:, kc, :], gain[:, kc:kc + 1], tmp_r[:, 0:n], ALU.mult, ALU.mult)

    for l in range(NL):
        wl = {k: V(v.ap[l], v.t) for k, v in wts.items()}

        def phase_A(b):
            ar.reset()
            hT = ar.alloc((8, S), BF16)
            gain = ar.alloc(8, F32)
            P.dma(gain, wl["mix_norm"])
            xb = [ar.alloc((8, 512), F32) for _ in range(2)]
            sq = ar.alloc((8, 512), BF16)
            rt = ar.alloc(512, F32)
            for tt in range(NT5):
                xt = xb[tt % 2]
                P.dma(xt, xres_v(b, (tt * 512, tt * 512 + 512)))
                rmsnorm_tile(xt, gain, hT[:, :, tt * 512:(tt + 1) * 512], 512, sq, rt)
            wb = [ar.alloc((8, 512), BF16) for _ in range(2)]
            st = [ar.alloc(512, BF16) for _ in range(4)]
            wba = ar.alloc((8, 16), BF16)
            batok = ar.alloc((NT1, 16), F32)
            for kc in range(8):
                load_w(wba[:, kc, :], wl["w_in"].ap[kc * 128:(kc + 1) * 128, 2048:2064], wl["w_in"].t)
            for n in range(NT1):
                ps = P.bank()
                for kc in range(8):
                    P.mm(ps[:, 0:16], hT[:, kc, n * 128:(n + 1) * 128], wba[:, kc, :], start=(kc == 0), stop=(kc == 7))
                evac(batok[:, n, :], ps[:, 0:16])
            for n0 in range(0, NT1, 8):
                n1 = min(n0 + 8, NT1)
                P.dma(DV(baTok[b * S + n0 * 128: b * S + n1 * 128, :].rearrange("(n p) c -> p n c", p=128), ("ba", b, n0)), batok[:, n0:n1, :])
            groups = [list(range(g * 4, g * 4 + 4)) for g in range(4)] + [list(range(17 + g * 4, 17 + g * 4 + 4)) for g in range(14)]
            si = 0
            for gi, grp in enumerate(groups):
                wt = wb[gi % 2]
                c0 = MCH[grp[0]][0]
                for kc in range(8):
                    load_w(wt[:, kc, :], wl["w_in"].ap[kc * 128:(kc + 1) * 128, c0:c0 + 512], wl["w_in"].t)
                for tt in range(NT5):
                    for j, mc in enumerate(grp):
                        ps = P.bank()
                        for kc in range(8):
                            P.mm(ps, wt[:, kc, j * 128:(j + 1) * 128], hT[:, kc, tt * 512:(tt + 1) * 512],
                                 start=(kc == 0), stop=(kc == 7))
                        s_ = st[si % 4]
                        si += 1
                        if mc >= MC_GATE:
                            P.act(s_, ps, AF.Sigmoid)
                        else:
                            evac(s_, ps)
                        P.dma(DV(projT[mc * 128:(mc + 1) * 128, b * S + tt * 512: b * S + (tt + 1) * 512], ("proj", mc, b, tt)), s_)

        def proj_v(mc, b, n=1):
            keys = [("proj", mc + i, b, tt) for i in range(n) for tt in range(NT5)]
            return DV(projT[mc * 128:(mc + n) * 128, b * S:(b + 1) * S], *keys)

        def phase_gdn(b):
            ar.reset()
            qn = ar.alloc((4, S), BF16)
            kn = ar.alloc((4, S), BF16)
            vv = ar.alloc((4, S), BF16)
            cw = ar.alloc((12, 4), F32)
            P.dma(cw, wl["gdn_conv"])
            mark = ar.off
            xp = ar.alloc(S + 4, BF16)
            acc = ar.alloc(S, F32)
            sqb = ar.alloc(S, BF16)
            rtf = ar.alloc(S, F32)
            P.memset(xp[:, 0:2], 0.0)
            P.memset(xp[:, S + 2:S + 4], 0.0)
            for c in range(12):
                which, hd = c // 4, c % 4
                P.dma(xp[:, 2:S + 2], proj_v(c, b))
                P.ts(acc, xp[:, 0:S], cw[:, c, 0:1], None, ALU.mult)
                for k in range(1, 4):
                    P.stt(acc, xp[:, k:k + S], cw[:, c, k:k + 1], acc, ALU.mult, ALU.add)
                if which == 2:
                    P.act(vv[:, hd, :], acc, AF.Silu)
                    continue
                P.act(acc, acc, AF.Silu)
                P.act(sqb, acc, AF.Square)
                for tt in range(NT5):
                    ps = P.bank()
                    P.mm(ps, ones_b, sqb[:, tt * 512:(tt + 1) * 512])
                    P.act(rtf[:, tt * 512:(tt + 1) * 512], ps, AF.Sqrt, bias=eps_c)
                P.recip(rtf, rtf)
                P.tt((qn if which == 0 else kn)[:, hd, :], acc, rtf, ALU.mult)
            ar.reset(mark)
            bt = ar.alloc((NT1, 16), F32)
            for n0 in range(0, NT1, 8):
                n1 = min(n0 + 8, NT1)
                P.dma(bt[:, n0:n1, :], DV(baTok[b * S + n0 * 128: b * S + n1 * 128, :].rearrange("(n p) c -> p n c", p=128), ("ba", b, n0)))
            beta = ar.alloc((NT1, 8), F32)
            gtk = ar.alloc((NT1, 8), F32)
            dtb = ar.alloc(8, F32)
            nega = ar.alloc(8, F32)
            gainrow = ar.alloc(128, F32)
            P.dma(dtb, V(wl["gdn_dtb"].ap.partition_broadcast(128), wl["gdn_dtb"].t))
            P.dma(nega, V(wl["gdn_alog"].ap.partition_broadcast(128), wl["gdn_alog"].t))
            P.dma(gainrow, V(wl["gdn_gain"].ap.partition_broadcast(128), wl["gdn_gain"].t))
            P.act(nega, nega, AF.Exp)
            P.ts(nega, nega, -1.0, None, ALU.mult)
            P.act(beta, bt[:, :, 0:8], AF.Sigmoid)
            P.tt(gtk, bt[:, :, 8:16], dtb.rr("p (o c) -> p o c", o=1).bc([128, NT1, 8]), ALU.add)
            P.act(gtk, gtk, AF.Exp)
            P.act(gtk, gtk, AF.Ln, bias=one_c)
            P.tt(gtk, gtk, nega.rr("p (o c) -> p o c", o=1).bc([128, NT1, 8]), ALU.mult)
            A = lambda shp, dt: ar.alloc(shp, dt)
            sdk = 128.0 ** -0.5
            mark2 = ar.off

            def h4(v):
                return v.rr("p (h d) -> p h d", h=4)

            def sweep(d):
                ktok, vtok = A((4, 128), BF16), A((4, 128), BF16)
                gam, ngam, eg, negeg, egs, dl, egl, gl = [A(4, F32) for _ in range(8)]
                gbs = A((4, 128), F32)
                Ds = A((4, 128), F32)
                Di = A((4, 128), F32)
                Y = A((4, 128), BF16)
                qkT = A((4, 128), BF16)
                Zs = A((7, 4 * 128), BF16)
                Z = A((4, 128), BF16)
                Xb = [A((4, 128), BF16) for _ in range(2)]
                Wt, T1 = A((4, 128), BF16), A((4, 128), BF16)
                xx, vn, kdec = A((4, 128), BF16), A((4, 128), BF16), A((4, 128), BF16)
                Bs, o_ = A((4, 128), F32), A((4, 128), F32)
                Sm, Sb = A((4, 128), F32), A((4, 128), BF16)
                yield
                P.memset(Sm, 0.0)
                P.memset(Sb, 0.0)
                order = range(NT1) if d == 0 else range(NT1 - 1, -1, -1)
                for n in order:
                    sl = slice(n * 128, (n + 1) * 128)
                    pk = h4(P.bankbf(d)[:, 0:512])
                    for h in range(4):
                        P.tr(pk[:, h, :], kn[:, h, sl], ident_b)
                    pv = h4(P.bankbf(d)[:, 0:512])
                    for h in range(4):
                        P.tr(pv[:, h, :], vv[:, h, sl], ident_b)
                    gcol = gtk[:, n, d * 4:(d + 1) * 4]
                    bcol = beta[:, n, d * 4:(d + 1) * 4]
                    pg = P.bank(d)
                    P.mm(pg[:, 0:4], ucum[:, d, :], gcol)
                    P.mm(pg[:, 4:8], ones_f, gcol)
                    for h in range(4):
                        P.ts(gbs[:, h, :], ones_f, gcol[:, h:h + 1], None, ALU.mult)
                    yield
                    P.cp(ktok, pk, eng="act")
                    P.cp(vtok, pv)
                    P.cp(gam, pg[:, 0:4])
                    P.ts(ngam, pg[:, 0:4], -1.0, None, ALU.mult)
                    P.act(eg, pg[:, 0:4], AF.Exp)
                    P.act(gl, pg[:, 4:8], AF.Exp)
                    P.tt(dl, pg[:, 4:8], gam, ALU.subtract)
                    pd = h4(P.bank(d))
                    for h in range(4):
                        P.mm(pd[:, h, :], gbs[:, h, :], ucum[:, d, :], start=True, stop=False)
                        P.mm(pd[:, h, :], ident_f, negm[:, d, :], start=False, stop=True)
                    pkk = h4(P.bank(d))
                    pqk = h4(P.bank(d))
                    for h in range(4):
                        P.mm(pkk[:, h, :], kn[:, h, sl], kn[:, h, sl])
                    for h in range(4):
                        P.mm(pqk[:, h, :], kn[:, h, sl], qn[:, h, sl])
                    yield
                    P.ts(negeg, eg, -1.0, None, ALU.mult)
                    P.ts(egs, eg, sdk, None, ALU.mult)
                    P.act(egl, dl, AF.Exp)
                    for h in range(4):
                        P.act(Ds[:, h, :], pd[:, h, :], AF.Exp, bias=ngam[:, h:h + 1])
                    yield
                    for h in range(4):
                        P.stt(Y[:, h, :], pkk[:, h, :], bcol[:, h:h + 1], Ds[:, h, :], ALU.mult, ALU.mult)
                    P.tt(Di, Ds, identf4, ALU.add)
                    P.stt(qkT, pqk, sdk, Di, ALU.mult, ALU.mult)
                    yield
                    pz = h4(P.bankbf(d)[:, 0:512])
                    for h in range(4):
                        P.tr(pz[:, h, :], Y[:, h, :], ident_b)
                    m0 = V(msk.ap[:, 0:1, :].broadcast_to([128, 4, 128]), msk.t)
                    P.tt(T1, Y, m0, ALU.mult)
                    X = Xb[0]
                    P.tt(X, identf4, T1, ALU.subtract)
                    yield
                    P.cp(Z, pz, eng="act")
                    yield
                    for lv in range(1, 7):
                        mb = V(msk.ap[:, lv:lv + 1, :].broadcast_to([128, 4, 128]), msk.t)
                        P.tt(Zs[:, lv, :].rr("p (h d) -> p h d", h=4), Z, mb, ALU.mult, eng="pool")
                    yield
                    for lv in range(1, 7):
                        Xn = Xb[lv % 2]
                        pW = h4(P.bankbf(d)[:, 0:512])
                        for h in range(4):
                            P.tr(pW[:, h, :], X[:, h, :], ident_b)
                        p1 = h4(P.bank(d))
                        for h in range(4):
                            P.mm(p1[:, h, :], Zs[:, lv, h * 128:(h + 1) * 128], X[:, h, :])
                        yield
                        P.cp(Wt, pW, eng="act")
                        P.cp(T1, p1)
                        yield
                        p2 = h4(P.bank(d))
                        for h in range(4):
                            P.mm(p2[:, h, :], Wt[:, h, :], T1[:, h, :])
                        yield
                        P.tt(Xn, X, p2, ALU.subtract)
                        X = Xn
                        yield
                    pks = h4(P.bank(d))
                    for h in range(4):
                        P.mm(pks[:, h, :], kn[:, h, sl], Sb[:, h, :])
                    pA = h4(P.bank(d))
                    for h in range(4):
                        P.mm(pA[:, h, :], qn[:, h, sl], Sb[:, h, :])
                    for h in range(4):
                        P.act(kdec[:, h, :], ktok[:, h, :], AF.Identity, scale=egl[:, h:h + 1])
                    yield
                    for h in range(4):
                        P.stt(xx[:, h, :], pks[:, h, :], negeg[:, h:h + 1], vtok[:, h, :], ALU.mult, ALU.add)
                    yield
                    pvn = h4(P.bank(d))
                    for h in range(4):
                        P.mm(pvn[:, h, :], X[:, h, :], xx[:, h, :])
                    yield
                    for h in range(4):
                        P.act(vn[:, h, :], pvn[:, h, :], AF.Identity, scale=bcol[:, h:h + 1])
                    yield
                    pB = h4(P.bank(d))
                    for h in range(4):
                        P.mm(pB[:, h, :], qkT[:, h, :], vn[:, h, :])
                    pS = h4(P.bank(d))
                    for h in range(4):
                        P.mm(pS[:, h, :], kdec[:, h, :], vn[:, h, :])
                    yield
                    P.cp(Bs, pB, eng="act")
                    for h in range(4):
                        P.stt(Sm[:, h, :], Sm[:, h, :], gl[:, h:h + 1], pS[:, h, :], ALU.mult, ALU.add)
                    yield
                    P.cp(Sb, Sm, eng="act")
                    for h in range(4):
                        P.stt(o_[:, h, :], pA[:, h, :], egs[:, h:h + 1], Bs[:, h, :], ALU.mult, ALU.add)
                    osc = ofwd if d == 0 else obwd
                    P.dma(DV(osc[b * S + n * 128: b * S + (n + 1) * 128, :].rearrange("p (h d) -> p h d", h=4), ("o", d, b, n)), o_)
                    yield

            gens = [sweep(0), sweep(1)]
            for g in gens:
                next(g)
            alive = list(gens)
            while alive:
                for g in list(alive):
                    try:
                        next(g)
                    except StopIteration:
                        alive.remove(g)
            ar.reset(mark2)
            FB = []
            for _ in range(2):
                FB.append(dict(of_=A((4, 128), F32), ob_=A((4, 128), F32), junk=A(128, F32), ss=A(4, F32), rs=A(4, F32),
                               on=A((4, 128), BF16), zt=A((4, 128), BF16), sz=A((4, 128), F32), y0=A((4, 128), BF16)))
            zv = proj_v(12, b, 4)
            for n in range(NT1):
                f_ = FB[n % 2]
                of_, ob_, junk, ss, rs, on, zt, sz, y0 = [f_[k] for k in ("of_", "ob_", "junk", "ss", "rs", "on", "zt", "sz", "y0")]
                rows = slice(b * S + n * 128, b * S + (n + 1) * 128)
                P.dma(of_, DV(ofwd[rows, :].rearrange("p (h d) -> p h d", h=4), ("o", 0, b, n)))
                P.dma(ob_, DV(obwd[rows, :].rearrange("p (h d) -> p h d", h=4), ("o", 1, b, n)))
                P.dma(zt, V(zv.ap[:, n * 128:(n + 1) * 128].rearrange("(h p) t -> p h t", p=128), zv.t))
                P.tt(of_, of_, ob_, ALU.add, eng="pool")
                for h in range(4):
                    P.act(junk, of_[:, h, :], AF.Square, accum_out=ss[:, h:h + 1])
                P.act(rs, ss, AF.Sqrt, scale=1.0 / 128, bias=eps_c)
                P.recip(rs, rs)
                for h in range(4):
                    P.stt(on[:, h, :], of_[:, h, :], rs[:, h:h + 1], gainrow, ALU.mult, ALU.mult)
                pt = h4(P.bankbf()[:, 0:512])
                for h in range(4):
                    P.tr(pt[:, h, :], on[:, h, :], ident_b)
                P.act(sz, zt, AF.Silu)
                P.tt(y0, pt, sz, ALU.mult)
                P.dma(DV(ysT[0:512, b * S + n * 128: b * S + (n + 1) * 128].rearrange("(h p) t -> p h t", p=128), ("ys", 0, b, n)), y0)

        def s5_tables():
            lre, lim, lst = [ar.alloc(32, F32) for _ in range(3)]
            P.dma(lre, wl["s5_lre"])
            P.dma(lim, wl["s5_lim"])
            P.dma(lst, wl["s5_lstep"])
            t = [ar.alloc(32, F32) for _ in range(8)]
            ti = ar.alloc(32, mybir.dt.int32)
            fre, fim, nfre, nfim = [ar.alloc(32, F32) for _ in range(4)]
            pw = ar.alloc((LOGS, 3, 32), F32)
            pa = ar.alloc((8, 3, 32), F32)
            P.act(lst, lst, AF.Exp)
            P.ts(lre, lre, -1e-4, None, ALU.min)
            mag, ph = t[0], t[1]
            P.tt(mag, lre, lst, ALU.mult)
            P.act(mag, mag, AF.Exp)
            P.tt(ph, lim, lst, ALU.mult)

            def sin_of(dst, src, shift):
                u, r, m = t[5], t[6], t[7]
                P.ts(u, src, shift, 1.0 / (2 * math.pi), ALU.add, ALU.mult)
                P.cp(ti, u)
                P.cp(r, ti)
                P.tt(r, u, r, ALU.subtract)
                P.ts(m, r, 0.5, None, ALU.is_gt)
                P.tt(r, r, m, ALU.subtract)
                P.ts(m, r, -0.5, None, ALU.is_lt)
                P.tt(r, r, m, ALU.add)
                P.act(dst, r, AF.Sin, scale=2 * math.pi)

            cs, sn = t[2], t[3]
            sin_of(sn, ph, 0.0)
            sin_of(cs, ph, math.pi / 2)
            are, aim = pw[:, 0, 0, :], pw[:, 0, 1, :]
            P.tt(are, mag, cs, ALU.mult)
            P.tt(aim, mag, sn, ALU.mult)
            den, am1, x1, x2 = t[0], t[1], t[2], t[3]
            P.tt(den, lre, lre, ALU.mult)
            P.tt(x1, lim, lim, ALU.mult)
            P.tt(den, den, x1, ALU.add)
            P.recip(den, den)
            P.ts(am1, are, -1.0, None, ALU.add)
            P.tt(x1, am1, lre, ALU.mult)
            P.tt(x2, aim, lim, ALU.mult)
            P.tt(x1, x1, x2, ALU.add)
            P.tt(fre, x1, den, ALU.mult)
            P.tt(x1, aim, lre, ALU.mult)
            P.tt(x2, am1, lim, ALU.mult)
            P.tt(x1, x1, x2, ALU.subtract)
            P.tt(fim, x1, den, ALU.mult)
            P.ts(nfim, fim, -1.0, None, ALU.mult)
            P.ts(nfre, fre, -1.0, None, ALU.mult)
            for k in range(LOGS):
                re_, im_ = pw[:, k, 0, :], pw[:, k, 1, :]
                P.ts(pw[:, k, 2, :], im_, -1.0, None, ALU.mult)
                if k + 1 < LOGS:
                    P.tt(x1, re_, re_, ALU.mult)
                    P.tt(x2, im_, im_, ALU.mult)
                    P.tt(pw[:, k + 1, 0, :], x1, x2, ALU.subtract)
                    P.tt(x1, re_, im_, ALU.mult)
                    P.ts(pw[:, k + 1, 1, :], x1, 2.0, None, ALU.mult)
            P.cp(pa[:, 0, :, :], pw[:, 0, :, :])
            for n in range(1, 8):
                pr_, pi_ = pa[:, n - 1, 0, :], pa[:, n - 1, 1, :]
                P.tt(x1, pr_, are, ALU.mult)
                P.tt(x2, pi_, aim, ALU.mult)
                P.tt(pa[:, n, 0, :], x1, x2, ALU.subtract)
                P.tt(x1, pr_, aim, ALU.mult)
                P.tt(x2, pi_, are, ALU.mult)
                P.tt(pa[:, n, 1, :], x1, x2, ALU.add)
                P.ts(pa[:, n, 2, :], pa[:, n, 1, :], -1.0, None, ALU.mult)
            return fre, fim, nfre, nfim, pw, pa

        def phase_s5():
            ar.reset()
            fre, fim, nfre, nfim, pw, pa = s5_tables()
            NC = S // 8
            LOGC = LOGS - 3
            dsk = ar.alloc(4, F32)
            glb = ar.alloc(4, F32)
            P.dma(dsk, wl["s5_d"])
            P.dma(glb, wl["s5_glu_b"])
            gw = ar.alloc((4, W), BF16)
            for kc in range(4):
                load_w(gw[:, kc, :], wl["s5_glu_w"].ap[kc * 128:(kc + 1) * 128, :], wl["s5_glu_w"].t)
            wB = ar.alloc((8, 2, 128), BF16)
            wC = ar.alloc((8, 2, 128), BF16)
            cst = [ar.alloc((2, 128), F32) for _ in range(2)]
            ctm = ar.alloc(128, F32)
            uT = ar.alloc(S, BF16)
            yacc = ar.alloc(S, F32)
            zg = ar.alloc((4, S), BF16)
            R = [ar.alloc(S, F32) for _ in range(2)]
            hbf = [ar.alloc(S, BF16) for _ in range(2)]
            EA = [ar.alloc(NC, F32) for _ in range(2)]
            EB = [ar.alloc(NC, F32) for _ in range(2)]
            stg = [ar.alloc(512, BF16) for _ in range(2)]
            sg = ar.alloc(512, F32)
            r3 = R[0].rr("p (c s) -> p c s", s=8)
            i3 = R[1].rr("p (c s) -> p c s", s=8)
            MA = (ALU.mult, ALU.add)
            for b in range(2):
                for cc in range(4):
                    for pr in range(4):
                        for d in range(2):
                            ii = pr * 2 + d
                            gi = d * 16 + cc * 4 + pr
                            load_w(wB[:, ii, 0, :], wl["s5_btre"].ap[gi], wl["s5_btre"].t)
                            load_w(wB[:, ii, 1, :], wl["s5_btim"].ap[gi], wl["s5_btim"].t)
                            c_ = cst[ii % 2]
                            P.dma(c_[:, 0, :], V(wl["s5_ctre"].ap[gi], wl["s5_ctre"].t))
                            P.dma(c_[:, 1, :], V(wl["s5_ctim"].ap[gi], wl["s5_ctim"].t))
                            P.ts(ctm, c_[:, 0, :], fre[:, gi:gi + 1], None, ALU.mult)
                            P.stt(wC[:, ii, 0, :], c_[:, 1, :], nfim[:, gi:gi + 1], ctm, *MA)
                            P.ts(ctm, c_[:, 0, :], nfim[:, gi:gi + 1], None, ALU.mult)
                            P.stt(wC[:, ii, 1, :], c_[:, 1, :], nfre[:, gi:gi + 1], ctm, *MA)
                    P.dma(uT, proj_v(MC_S5U + cc, b))
                    P.ts(yacc, uT, dsk[:, cc:cc + 1], None, ALU.mult)
                    for pr in range(4):
                        for d in range(2):
                            ii = pr * 2 + d
                            col = d * 16 + cc * 4 + pr
                            for tt in range(NT5):
                                ts_ = slice(tt * 512, (tt + 1) * 512)
                                p1, p2 = P.bank(), P.bank()
                                P.mm(p1, wB[:, ii, 0, :], uT[:, ts_])
                                P.mm(p2, wB[:, ii, 1, :], uT[:, ts_])
                                P.cp(R[0][:, ts_], p1, eng="act")
                                P.cp(R[1][:, ts_], p2, eng="act")
                            cr, ci, nci = [pa[:, 0, q_, col:col + 1] for q_ in range(3)]
                            for s_ in (range(1, 8) if d == 0 else range(6, -1, -1)):
                                sp = s_ - 1 if d == 0 else s_ + 1
                                P.stt(r3[:, :, s_], r3[:, :, sp], cr, r3[:, :, s_], *MA)
                                P.stt(r3[:, :, s_], i3[:, :, sp], nci, r3[:, :, s_], *MA)
                                P.stt(i3[:, :, s_], i3[:, :, sp], cr, i3[:, :, s_], *MA)
                                P.stt(i3[:, :, s_], r3[:, :, sp], ci, i3[:, :, s_], *MA)
                            e_ = 7 if d == 0 else 0
                            P.cp(EA[0], r3[:, :, e_], eng="act")
                            P.cp(EA[1], i3[:, :, e_], eng="act")
                            cur, oth = EA, EB
                            for k in range(LOGC):
                                sh = 1 << k
                                cr, ci, nci = [pw[:, k + 3, q_, col:col + 1] for q_ in range(3)]
                                if d == 0:
                                    dst, src, keep = slice(sh, NC), slice(0, NC - sh), slice(0, sh)
                                else:
                                    dst, src, keep = slice(0, NC - sh), slice(sh, NC), slice(NC - sh, NC)
                                P.stt(oth[0][:, dst], cur[0][:, src], cr, cur[0][:, dst], *MA)
                                P.stt(oth[0][:, dst], cur[1][:, src], nci, oth[0][:, dst], *MA)
                                P.stt(oth[1][:, dst], cur[1][:, src], cr, cur[1][:, dst], *MA)
                                P.stt(oth[1][:, dst], cur[0][:, src], ci, oth[1][:, dst], *MA)
                                P.cp(oth[0][:, keep], cur[0][:, keep], eng="act")
                                P.cp(oth[1][:, keep], cur[1][:, keep], eng="act")
                                cur, oth = oth, cur
                            if d == 0:
                                dst, src = slice(1, NC), slice(0, NC - 1)
                            else:
                                dst, src = slice(0, NC - 1), slice(1, NC)
                            for s_ in range(8):
                                n_ = s_ if d == 0 else 7 - s_
                                cr, ci, nci = [pa[:, n_, q_, col:col + 1] for q_ in range(3)]
                                P.stt(r3[:, dst, s_], cur[0][:, src], cr, r3[:, dst, s_], *MA)
                                P.stt(r3[:, dst, s_], cur[1][:, src], nci, r3[:, dst, s_], *MA)
                                P.stt(i3[:, dst, s_], cur[1][:, src], cr, i3[:, dst, s_], *MA)
                                P.stt(i3[:, dst, s_], cur[0][:, src], ci, i3[:, dst, s_], *MA)
                            P.cp(hbf[0], R[0], eng="act")
                            P.cp(hbf[1], R[1], eng="act")
                            for tt in range(NT5):
                                ts_ = slice(tt * 512, (tt + 1) * 512)
                                pc = P.bank()
                                P.mm(pc, wC[:, ii, 0, :], hbf[0][:, ts_], start=True, stop=False)
                                P.mm(pc, wC[:, ii, 1, :], hbf[1][:, ts_], start=False, stop=True)
                                P.tt(yacc[:, ts_], yacc[:, ts_], pc, ALU.add)
                    P.act(zg[:, cc, :], yacc, AF.Gelu)
                for co in range(4):
                    for tt in range(NT5):
                        ts_ = slice(tt * 512, (tt + 1) * 512)
                        ps = P.bank()
                        for kc in range(4):
                            P.mm(ps, gw[:, kc, co * 128:(co + 1) * 128], zg[:, kc, ts_], start=(kc == 0), stop=(kc == 3))
                        P.act(sg, ps, AF.Sigmoid, bias=glb[:, co:co + 1])
                        s_ = stg[(co * NT5 + tt) % 2]
                        P.tt(s_, zg[:, co, ts_], sg, ALU.mult)
                        P.dma(DV(ysT[512 + co * 128: 512 + (co + 1) * 128, b * S + tt * 512: b * S + (tt + 1) * 512], ("ys", 1, b, co, tt)), s_)

        def phase_lru_sc():
            ar.reset()
            cwl = ar.alloc((4, 4), F32)
            cbl = ar.alloc(4, F32)
            ba_, bx_, lam_ = ar.alloc(8, F32), ar.alloc(8, F32), ar.alloc(8, F32)
            scw = ar.alloc((4, 3), F32)
            P.dma(cwl, wl["lru_cw"])
            P.dma(cbl, wl["lru_cb"])
            P.dma(ba_, wl["lru_ba"])
            P.dma(bx_, wl["lru_bx"])
            P.dma(lam_, wl["lru_lam"])
            P.dma(scw, wl["sc_cw"])
            WA, WX = ar.alloc((8, 128), BF16), ar.alloc((8, 128), BF16)
            for i in range(8):
                load_w(WA[:, i, :], wl["lru_wa"].ap[i], wl["lru_wa"].t)
                load_w(WX[:, i, :], wl["lru_wx"].ap[i], wl["lru_wx"].t)
            P.act(lam_, lam_, AF.Exp, scale=-1.0)
            P.act(lam_, lam_, AF.Ln, bias=one_c)
            P.ts(lam_, lam_, -8.0, None, ALU.mult)
            xp = ar.alloc(S + 4, BF16)
            xc = ar.alloc(S, F32)
            xcb = ar.alloc(S, BF16)
            aa, bb = ar.alloc(S, F32), ar.alloc(S, F32)
            hh = [ar.alloc(S, F32) for _ in range(2)]
            gch = ar.alloc(S, BF16)
            gg = ar.alloc(S, F32)
            yo = ar.alloc(S, BF16)
            r_, i_, t_ = ar.alloc(512, F32), ar.alloc(512, F32), ar.alloc(512, F32)
            s1, s2, s3 = ar.alloc(S, BF16), ar.alloc(S, BF16), ar.alloc(S + 2, F32)
            P.memset(xp[:, 0:2], 0.0)
            P.memset(xp[:, S + 2:S + 4], 0.0)
            P.memset(s3[:, 0:1], 0.0)
            P.memset(s3[:, S + 1:S + 2], 0.0)
            for b in range(2):
                for c in range(4):
                    P.dma(xp[:, 2:S + 2], proj_v(MC_LRUX + c, b))
                    P.dma(gch, proj_v(MC_LRUG + c, b))
                    P.ts(xc, xp[:, 0:S], cwl[:, c, 0:1], cbl[:, c:c + 1], ALU.mult, ALU.add)
                    for k in range(1, 4):
                        P.stt(xc, xp[:, k:k + S], cwl[:, c, k:k + 1], xc, ALU.mult, ALU.add)
                    P.cp(xcb, xc, eng="act")
                    for d in range(2):
                        j = d * 4 + c
                        for tt in range(NT5):
                            ts_ = slice(tt * 512, (tt + 1) * 512)
                            pr_, pi_ = P.bank(), P.bank()
                            P.mm(pr_, WA[:, j, :], xcb[:, ts_])
                            P.mm(pi_, WX[:, j, :], xcb[:, ts_])
                            P.act(r_, pr_, AF.Sigmoid, bias=ba_[:, j:j + 1])
                            P.act(i_, pi_, AF.Sigmoid, bias=bx_[:, j:j + 1])
                            P.act(aa[:, ts_], r_, AF.Exp, scale=lam_[:, j:j + 1])
                            P.act(t_, aa[:, ts_], AF.Square)
                            P.act(t_, t_, AF.Sqrt, scale=-1.0, bias=one_c)
                            P.tt(i_, i_, t_, ALU.mult)
                            P.tt(bb[:, ts_], i_, xc[:, ts_], ALU.mult)
                        if d == 0:
                            P.scan(hh[0], aa, bb, 0.0)
                        else:
                            P.scan(hh[1][:, ::-1], aa[:, ::-1], bb[:, ::-1], 0.0)
                    P.tt(hh[0], hh[0], hh[1], ALU.add)
                    P.act(gg, gch, AF.Gelu)
                    P.tt(yo, hh[0], gg, ALU.mult)
                    P.dma(DV(ysT[1024 + c * 128: 1024 + (c + 1) * 128, b * S:(b + 1) * S], ("ys", 2, b, c)), yo)
                    P.dma(s1, proj_v(MC_SCC + c, b))
                    P.dma(s2, proj_v(MC_SCX + c, b))
                    P.tt(s3[:, 1:S + 1], s1, s2, ALU.mult)
                    P.dma(s1, proj_v(MC_SCB + c, b))
                    P.ts(gg, s3[:, 0:S], scw[:, c, 0:1], None, ALU.mult)
                    for k in range(1, 3):
                        P.stt(gg, s3[:, k:k + S], scw[:, c, k:k + 1], gg, ALU.mult, ALU.add)
                    P.tt(s2, gg, s1, ALU.mult)
                    P.dma(DV(ysT[1536 + c * 128: 1536 + (c + 1) * 128, b * S:(b + 1) * S], ("ys", 3, b, c)), s2)

        def ys_v(m, b, cols):
            if m == 0:
                keys = [("ys", 0, b, n) for n in range(NT1)]
            elif m == 1:
                keys = [("ys", 1, b, co, tt) for co in range(4) for tt in range(NT5)]
            else:
                keys = [("ys", m, b, c) for c in range(4)]
            return DV(ysT[m * 512:(m + 1) * 512, b * S + cols[0]: b * S + cols[1]].rearrange("(k p) t -> p k t", p=128), *keys)

        def phase_merge_xa():
            ar.reset()
            wbr = ar.alloc((16, D), BF16)
            wmo = ar.alloc((8, D), BF16)
            wq = ar.alloc((8, D), BF16)
            wo = ar.alloc((8, D), BF16)
            for kc in range(16):
                for hf in range(2):
                    load_w(wbr[:, kc, hf * 512:(hf + 1) * 512], wl["w_branch"].ap[kc * 128:(kc + 1) * 128, hf * 512:(hf + 1) * 512], wl["w_branch"].t)
            for kc in range(8):
                for hf in range(2):
                    cs = slice(hf * 512, (hf + 1) * 512)
                    load_w(wmo[:, kc, cs], wl["w_mix_out"].ap[kc * 128:(kc + 1) * 128, cs], wl["w_mix_out"].t)
                    load_w(wq[:, kc, cs], wl["xa_wq"].ap[kc * 128:(kc + 1) * 128, cs], wl["xa_wq"].t)
                    load_w(wo[:, kc, cs], wl["xa_wo"].ap[kc * 128:(kc + 1) * 128, cs], wl["xa_wo"].t)
            gx = ar.alloc(8, F32)
            gm = ar.alloc(8, F32)
            P.dma(gx, wl["xa_norm"])
            P.dma(gm, wl["xa_mnorm"])
            KT = ar.alloc((8, MEM), BF16)
            Vm = ar.alloc((2, D), BF16)
            mark = ar.off
            for b in range(2):
                ar.reset(mark)
                wkv = ar.alloc((8, 512), BF16)
                mt = ar.alloc((8, MEM), F32)
                mh = ar.alloc((8, MEM), BF16)
                sq = ar.alloc((8, 512), BF16)
                rt = ar.alloc(512, F32)
                P.dma(mt, V(memT.ap[:, b * MEM:(b + 1) * MEM].rearrange("(k p) t -> p k t", p=128), memT.t))
                rmsnorm_tile(mt, gm, mh, MEM, sq, rt)
                for cb in range(4):
                    for kc in range(8):
                        load_w(wkv[:, kc, :], wl["xa_wkv"].ap[kc * 128:(kc + 1) * 128, cb * 512:(cb + 1) * 512], wl["xa_wkv"].t)
                    if cb < 2:
                        for j in range(4):
                            ps = P.bank()
                            for kc in range(8):
                                P.mm(ps[:, 0:MEM], wkv[:, kc, j * 128:(j + 1) * 128], mh[:, kc, :], start=(kc == 0), stop=(kc == 7))
                            evac(KT[:, cb * 4 + j, :], ps[:, 0:MEM])
                    else:
                        for mz in range(2):
                            ps = P.bank()
                            for kc in range(8):
                                P.mm(ps, mh[:, kc, mz * 128:(mz + 1) * 128], wkv[:, kc, :], start=(kc == 0), stop=(kc == 7))
                            evac(Vm[:, mz, (cb - 2) * 512:(cb - 1) * 512], ps)
                xt = ar.alloc((8, 512), F32)
                ysb = ar.alloc((16, 512), BF16)
                gtj = ar.alloc((2, 4, 512), BF16)
                acc = ar.alloc(512, F32)
                tmp = ar.alloc(512, F32)
                mg = ar.alloc((8, 512), BF16)
                xn = ar.alloc((8, 512), BF16)
                qT = mg
                oT = xn
                E = ar.alloc((2, 512), BF16)
                rsum = ar.alloc(512, F32)
                for tt in range(NT5):
                    cols = (tt * 512, (tt + 1) * 512)
                    P.dma(xt, xres_v(b, cols))
                    for m in range(4):
                        P.dma(ysb[:, m * 4:(m + 1) * 4, :], ys_v(m, b, cols))
                    for j in range(8):
                        gts = gtj[:, j % 2, :, :]
                        for m in range(4):
                            gv = proj_v(MC_GATE + m * 8 + j, b)
                            P.dma(gts[:, m, :], V(gv.ap[:, cols[0]:cols[1]], gv.t))
                        for m in range(4):
                            ps = P.bank()
                            for kc in range(4):
                                P.mm(ps, wbr[:, m * 4 + kc, j * 128:(j + 1) * 128], ysb[:, m * 4 + kc, :], start=(kc == 0), stop=(kc == 3))
                            if m == 0:
                                P.tt(acc, ps, gts[:, 0, :], ALU.mult)
                            else:
                                P.tt(tmp, ps, gts[:, m, :], ALU.mult)
                                P.tt(mg[:, j, :] if m == 3 else acc, acc, tmp, ALU.add, eng="pool")
                    for jo in range(8):
                        ps = P.bank()
                        for kc in range(8):
                            P.mm(ps, wmo[:, kc, jo * 128:(jo + 1) * 128], mg[:, kc, :], start=(kc == 0), stop=(kc == 7))
                        P.tt(xt[:, jo, :], xt[:, jo, :], ps, ALU.add)
                    rmsnorm_tile(xt, gx, xn, 512, sq, rt)
                    for jo in range(8):
                        ps = P.bank()
                        for kc in range(8):
                            P.mm(ps, wq[:, kc, jo * 128:(jo + 1) * 128], xn[:, kc, :], start=(kc == 0), stop=(kc == 7))
                        evac(qT[:, jo, :], ps)
                    for h in range(4):
                        for mz in range(2):
                            ps = P.bank()
                            for hc in range(2):
                                P.mm(ps, KT[:, 2 * h + hc, mz * 128:(mz + 1) * 128], qT[:, 2 * h + hc, :], start=(hc == 0), stop=(hc == 1))
                            P.act(E[:, mz, :], ps, AF.Exp, scale=1.0 / 16.0)
                        ps = P.bank()
                        for mz in range(2):
                            P.mm(ps, ones_b, E[:, mz, :], start=(mz == 0), stop=(mz == 1))
                        P.recip(rsum, ps)
                        for hc in range(2):
                            ps = P.bank()
                            for mz in range(2):
                                P.mm(ps, Vm[:, mz, (2 * h + hc) * 128:(2 * h + hc + 1) * 128], E[:, mz, :], start=(mz == 0), stop=(mz == 1))
                            P.tt(oT[:, 2 * h + hc, :], ps, rsum, ALU.mult)
                    for jo in range(8):
                        ps = P.bank()
                        for kc in range(8):
                            P.mm(ps, wo[:, kc, jo * 128:(jo + 1) * 128], oT[:, kc, :], start=(kc == 0), stop=(kc == 7))
                        P.tt(xt[:, jo, :], xt[:, jo, :], ps, ALU.add)
                    P.dma(xres_v(b, cols), xt)

        def phase_ffn():
            ar.reset()
            gf = ar.alloc(8, F32)
            P.dma(gf, wl["ffn_norm"])
            fcw = ar.alloc((44, 3), F32)
            fcb = ar.alloc(44, F32)
            P.dma(fcw, wl["ffn_cw"])
            P.dma(fcb, wl["ffn_cb"])
            xt = ar.alloc((8, 512), F32)
            hb = ar.alloc((8, 512), BF16)
            sq = ar.alloc((8, 512), BF16)
            rt = ar.alloc(512, F32)
            for b in range(2):
                for tt in range(NT5):
                    cols = (tt * 512, (tt + 1) * 512)
                    P.dma(xt, xres_v(b, cols))
                    rmsnorm_tile(xt, gf, hb, 512, sq, rt)
                    P.dma(DV(hffT[:, b * S + cols[0]: b * S + cols[1]].rearrange("(k p) t -> p k t", p=128), ("hff", b, tt)), hb)
            wup = ar.alloc((8, 22 * 128), BF16)
            wdn = ar.alloc((11, D), BF16)
            hp = ar.alloc((8, 512), BF16)
            hid = ar.alloc((11, 512), BF16)
            cgs = [ar.alloc(512, F32) for _ in range(2)]
            cus = [ar.alloc(512, F32) for _ in range(2)]
            tiles = []
            t0 = 0
            while t0 < S:
                n = min(510, S - t0)
                tiles.append((t0, n))
                t0 += n
            for hf in range(2):
                for kc in range(8):
                    for part in range(2):
                        c0 = part * DFF + hf * 1408
                        for q3 in range(3):
                            w_ = 512 if q3 < 2 else 384
                            load_w(wup[:, kc, part * 1408 + q3 * 512: part * 1408 + q3 * 512 + w_],
                                   wl["ffn_wup"].ap[kc * 128:(kc + 1) * 128, c0 + q3 * 512: c0 + q3 * 512 + w_], wl["ffn_wup"].t)
                for kc in range(11):
                    for hh_ in range(2):
                        cs = slice(hh_ * 512, (hh_ + 1) * 512)
                        load_w(wdn[:, kc, cs], wl["ffn_wdown"].ap[hf * 1408 + kc * 128: hf * 1408 + (kc + 1) * 128, cs], wl["ffn_wdown"].t)
                for b in range(2):
                    hkeys = [("hff", b, tt) for tt in range(NT5)]
                    for (t0, n) in tiles:
                        lo, hi = max(t0 - 1, 0), min(t0 + n + 1, S)
                        off = lo - (t0 - 1)
                        nin = hi - lo
                        if off:
                            P.memset(hp[:, :, 0:1], 0.0)
                        if hi < t0 + n + 1:
                            P.memset(hp[:, :, n + 1:n + 2], 0.0)
                        P.dma(hp[:, :, off:off + nin], DV(hffT[:, b * S + lo: b * S + hi].rearrange("(k p) t -> p k t", p=128), *hkeys))
                        for fc in range(11):
                            pg_, pu_ = P.bank(), P.bank()
                            for kc in range(8):
                                P.mm(pg_[:, 0:n + 2], wup[:, kc, fc * 128:(fc + 1) * 128], hp[:, kc, 0:n + 2], start=(kc == 0), stop=(kc == 7))
                            for kc in range(8):
                                P.mm(pu_[:, 0:n + 2], wup[:, kc, 1408 + fc * 128: 1408 + (fc + 1) * 128], hp[:, kc, 0:n + 2], start=(kc == 0), stop=(kc == 7))
                            ig, iu = hf * 11 + fc, 22 + hf * 11 + fc
                            cg, cu = cgs[fc % 2], cus[fc % 2]
                            P.act(cg[:, 0:n], pg_[:, 0:n], AF.Identity, scale=fcw[:, ig, 0:1], bias=fcb[:, ig:ig + 1])
                            P.stt(cg[:, 0:n], pg_[:, 1:n + 1], fcw[:, ig, 1:2], cg[:, 0:n], ALU.mult, ALU.add)
                            P.stt(cg[:, 0:n], pg_[:, 2:n + 2], fcw[:, ig, 2:3], cg[:, 0:n], ALU.mult, ALU.add)
                            P.act(cu[:, 0:n], pu_[:, 0:n], AF.Identity, scale=fcw[:, iu, 0:1], bias=fcb[:, iu:iu + 1])
                            P.stt(cu[:, 0:n], pu_[:, 1:n + 1], fcw[:, iu, 1:2], cu[:, 0:n], ALU.mult, ALU.add)
                            P.stt(cu[:, 0:n], pu_[:, 2:n + 2], fcw[:, iu, 2:3], cu[:, 0:n], ALU.mult, ALU.add)
                            P.act(cg[:, 0:n], cg[:, 0:n], AF.Silu)
                            P.tt(hid[:, fc, 0:n], cg[:, 0:n], cu[:, 0:n], ALU.mult, eng="pool")
                        P.dma(xt[:, :, 0:n], xres_v(b, (t0, t0 + n)))
                        for jo in range(8):
                            ps = P.bank()
                            for kc in range(11):
                                P.mm(ps[:, 0:n], wdn[:, kc, jo * 128:(jo + 1) * 128], hid[:, kc, 0:n], start=(kc == 0), stop=(kc == 10))
                            P.tt(xt[:, jo, 0:n], xt[:, jo, 0:n], ps[:, 0:n], ALU.add)
                        P.dma(xres_v(b, (t0, t0 + n)), xt[:, :, 0:n])

        for b in range(2):
            phase_A(b)
        for b in range(2):
            phase_gdn(b)
        phase_s5()
        phase_lru_sc()
        phase_merge_xa()
        phase_ffn()

    ar.reset()
    gfin = ar.alloc(8, F32)
    P.dma(gfin, fin_norm)
    xt = ar.alloc((8, 512), F32)
    yt = ar.alloc((8, 512), F32)
    sq = ar.alloc((8, 512), BF16)
    rt = ar.alloc(512, F32)
    toks = []
    for b in range(2):
        for tt in range(NT5):
            cols = (tt * 512, (tt + 1) * 512)
            P.dma(xt, xres_v(b, cols))
            P.act(sq, xt, AF.Square)
            ps = P.bank()
            for kc in range(8):
                P.mm(ps, ones_b, sq[:, kc, :], start=(kc == 0), stop=(kc == 7))
            P.act(rt, ps, AF.Sqrt, scale=1.0 / D, bias=eps_c)
            P.recip(rt, rt)
            for kc in range(8):
                P.stt(yt[:, kc, :], xt[:, kc, :], gfin[:, kc:kc + 1], rt, ALU.mult, ALU.mult)
            osl = slice(b * S + cols[0], b * S + cols[1])
            toks.append(P.dma(V(xoT.ap[:, osl].rearrange("(k p) t -> p k t", p=128), Trk()), xt))
            toks.append(P.dma(V(ynT.ap[:, osl].rearrange("(k p) t -> p k t", p=128), Trk()), yt))
    P.wait_all("sp", toks)
    P.build()
    return nc


def _pc(a, n):
    return np.ascontiguousarray(a.reshape(n, 128).T)


def prep_layer(inp, l):
    f = lambda k: np.asarray(inp[k][l], dtype=np.float32)
    o = {}
    o["mix_norm"] = _pc(f("mix_norm"), 8)
    o["w_in"] = f("w_in")
    o["gdn_conv"] = np.ascontiguousarray(f("gdn_conv").reshape(4, 12, 128).transpose(2, 1, 0))
    o["gdn_alog"] = f("gdn_a_log").reshape(1, 8)
    o["gdn_dtb"] = f("gdn_dt_bias").reshape(1, 8)
    o["gdn_gain"] = f("gdn_out_norm").reshape(1, 128)

    def st(a):
        return np.ascontiguousarray(a.reshape(2, 16, 2, 64).transpose(2, 3, 0, 1).reshape(128, 32))
    o["s5_lre"] = st(f("s5_lambda_re"))
    o["s5_lim"] = st(f("s5_lambda_im"))
    o["s5_lstep"] = st(np.repeat(f("s5_log_step")[:, :, None], 64, axis=2))

    def bt(a):
        out = np.zeros((2, 16, 128, 128), np.float32)
        for g in range(32):
            pair, g2 = g // 2, g % 2
            r0 = 16 * (g % 8)
            out[:, pair, r0:r0 + 16, g2 * 64:(g2 + 1) * 64] = a[:, g].transpose(0, 2, 1)
        return out.reshape(32, 128, 128)

    def ct(a):
        out = np.zeros((2, 16, 128, 128), np.float32)
        for g in range(32):
            pair, g2 = g // 2, g % 2
            c0 = 16 * (g % 8)
            out[:, pair, g2 * 64:(g2 + 1) * 64, c0:c0 + 16] = a[:, g].transpose(0, 2, 1)
        return out.reshape(32, 128, 128)
    o["s5_btre"], o["s5_btim"] = bt(f("s5_b_re")), bt(f("s5_b_im"))
    o["s5_ctre"], o["s5_ctim"] = ct(f("s5_c_re")), ct(f("s5_c_im"))
    o["s5_d"] = _pc(f("s5_d"), 4)
    o["s5_glu_w"] = f("s5_glu_w")
    o["s5_glu_b"] = _pc(f("s5_glu_b"), 4)
    o["lru_cw"] = np.ascontiguousarray(f("lru_conv_w").reshape(4, 4, 128).transpose(2, 1, 0))
    o["lru_cb"] = _pc(f("lru_conv_b"), 4)

    def bd(a):
        out = np.zeros((2, 4, 128, 128), np.float32)
        for n in range(8):
            c, bq = n // 2, n % 2
            out[:, c, bq * 64:(bq + 1) * 64, bq * 64:(bq + 1) * 64] = a[:, n]
        return out.reshape(8, 128, 128)
    o["lru_wa"], o["lru_wx"] = bd(f("lru_gate_a_w")), bd(f("lru_gate_x_w"))
    p2 = lambda a: np.ascontiguousarray(a.reshape(2, 4, 128).transpose(2, 0, 1).reshape(128, 8))
    o["lru_ba"], o["lru_bx"], o["lru_lam"] = p2(f("lru_gate_a_b")), p2(f("lru_gate_x_b")), p2(f("lru_lambda"))
    o["sc_cw"] = np.ascontiguousarray(f("sc_conv").reshape(3, 4, 128).transpose(2, 1, 0))
    o["w_branch"] = f("w_branch").reshape(4 * W, D)
    o["w_mix_out"] = f("w_mix_out")
    o["xa_norm"], o["xa_mnorm"] = _pc(f("xa_norm"), 8), _pc(f("xa_mem_norm"), 8)
    o["xa_wq"], o["xa_wkv"], o["xa_wo"] = f("xa_w_q"), f("xa_w_kv"), f("xa_w_o")
    o["ffn_norm"] = _pc(f("ffn_norm"), 8)
    o["ffn_wup"] = f("ffn_w_up")
    o["ffn_cw"] = np.ascontiguousarray(f("ffn_conv_w").reshape(3, 44, 128).transpose(2, 1, 0))
    o["ffn_cb"] = _pc(f("ffn_conv_b"), 44)
    o["ffn_wdown"] = f("ffn_w_down")
    return o


def consts():
    i = np.arange(128)
    c = {"c_ident": np.eye(128, dtype=np.float32)}
    uc = np.zeros((2, 128, 128), np.float32)
    uc[0] = (i[:, None] <= i[None, :])
    uc[1] = (i[:, None] >= i[None, :])
    ng = np.zeros((2, 128, 128), np.float32)
    ng[0] = np.where(i[None, :] > i[:, None], 0.0, -30000.0)
    ng[1] = np.where(i[None, :] < i[:, None], 0.0, -30000.0)
    c["c_ucum"], c["c_negm"] = uc, ng
    mk = np.zeros((7, 128, 128), np.float32)
    for lv in range(7):
        sz = 1 << lv
        mk[lv] = ((i[:, None] // (2 * sz)) == (i[None, :] // (2 * sz))) & ((i[:, None] // sz) != (i[None, :] // sz))
    c["c_msk"] = mk
    return c


_PROG = {}


DBG = []


def run_layers(xT_list, memT_list, inp, layers, S):
    NL = len(layers)
    key = (S, NL)
    if key not in _PROG:
        _PROG[key] = build_program(S, NL, dbg=bool(DBG))
    nc = _PROG[key]
    per = [prep_layer(inp, l) for l in layers]
    shared = {k: np.ascontiguousarray(np.stack([p[k] for p in per])) for k in per[0]}
    shared["fin_norm"] = _pc(np.asarray(inp["final_norm"], np.float32), 8)
    shared.update(consts())
    in_maps = []
    for c in range(len(xT_list)):
        m = dict(shared)
        m["xT"] = xT_list[c]
        m["memT"] = memT_list[c]
        in_maps.append(m)
    res = run_bass_kernel_spmd(nc, in_maps, core_ids=list(range(len(xT_list))))
    if DBG:
        DBG.append(res.results)
    return [r["xoT"] for r in res.results], [r["ynT"] for r in res.results]


def kernel(**inp):
    x = np.asarray(inp["x"], np.float32)
    mem = np.asarray(inp["mem"], np.float32)
    B, S, _ = x.shape
    depth = inp["w_in"].shape[0]
    nco = B // 2
    xT = [np.ascontiguousarray(x[2 * c:2 * c + 2].reshape(2 * S, D).T) for c in range(nco)]
    mT = [np.ascontiguousarray(mem[2 * c:2 * c + 2].reshape(2 * MEM, D).T) for c in range(nco)]
    xT, yn = run_layers(xT, mT, inp, list(range(depth)), S)
    out = np.empty((B, S, D), np.float32)
    for c in range(nco):
        out[2 * c:2 * c + 2] = yn[c].T.reshape(2, S, D)
    return out
```
